# Optimizing a Trainium2 kernel written in Bass

```python
import jax, jax.numpy as jnp
from jax import lax
import numpy as np

D_MODEL = 1024
BATCH = 2
SEQ = 8192
DEPTH = 1
DEC_BATCH = 128
DEC_SEQ = 1
PAST_LEN = 16384
PAGE_SIZE = 128

HEAD_DIM = 64
N_HEADS_A = 8
N_HEADS_B = 8
N_KV_B = 2
GROUP_B = N_HEADS_B // N_KV_B
DILATED = ((128, 1), (512, 4), (2048, 16))
WIN_A = 2048
WIN_B = 128
QBLOCK = 128
D_FF = 2816
D_PLE = 256
ROPE_THETA = 10000.0
EPS = 1e-6
SCALE = HEAD_DIM ** -0.5
QA_W = N_HEADS_A * HEAD_DIM
QB_W = N_HEADS_B * HEAD_DIM
KVB_W = N_KV_B * HEAD_DIM
MIX_W = QA_W + QB_W
IN_W = 3 * QA_W + QB_W + 2 * KVB_W
SPLITS = (QA_W, 2 * QA_W, 3 * QA_W, 3 * QA_W + QB_W, 3 * QA_W + QB_W + KVB_W)

kernel_name = 'hybrid_dilated_swa_sink_decoder_step'


def _rmsnorm(x, g):
    xf = x.astype(jnp.float32)
    y = xf * lax.rsqrt(jnp.mean(xf * xf, axis=-1, keepdims=True) + EPS)
    return (y * g.astype(jnp.float32)).astype(x.dtype)


def _rope(x, pos):
    half = HEAD_DIM // 2
    inv = jnp.power(ROPE_THETA, -jnp.arange(half, dtype=jnp.float32) * 2.0 / HEAD_DIM)
    ang = pos.astype(jnp.float32)[:, None] * inv[None, :]
    c = jnp.cos(ang)[:, None, :]
    s = jnp.sin(ang)[:, None, :]
    x1 = x[..., :half].astype(jnp.float32)
    x2 = x[..., half:].astype(jnp.float32)
    return jnp.concatenate([x1 * c - x2 * s, x2 * c + x1 * s], axis=-1).astype(x.dtype)


def _ffn_half(h, g_pre, g_post, w_gate, w_up, w_down):
    u = _rmsnorm(h, g_pre)
    y = (jax.nn.silu(u @ w_gate) * (u @ w_up)) @ w_down
    return h + 0.5 * _rmsnorm(y, g_post)


def _project(h, g_pre, w_in, pos):
    b, t = h.shape[:2]
    z = _rmsnorm(h, g_pre) @ w_in
    qa, ka, va, qb, kb, vb = jnp.split(z, SPLITS, axis=-1)
    qa = _rope(qa.reshape(b, t, N_HEADS_A, HEAD_DIM), pos)
    ka = _rope(ka.reshape(b, t, N_HEADS_A, HEAD_DIM), pos)
    va = va.reshape(b, t, N_HEADS_A, HEAD_DIM)
    qb = _rope(qb.reshape(b, t, N_HEADS_B, HEAD_DIM), pos)
    kb = _rope(kb.reshape(b, t, N_KV_B, HEAD_DIM), pos)
    vb = vb.reshape(b, t, N_KV_B, HEAD_DIM)
    return qa, ka, va, qb, kb, vb


def _dilated_mix(q, k, v, qidx):
    outs, lses = [], []
    for window, dil in DILATED:
        offs = dil * jnp.arange(window // dil + 1, dtype=jnp.int32)
        idx = qidx[:, None] - offs[None, :]
        valid = idx >= 0
        idx = jnp.maximum(idx, 0)
        kg = k[:, idx]
        vg = v[:, idx]
        s = jnp.einsum('bthd,btmhd->bhtm', q, kg, preferred_element_type=jnp.float32) * SCALE
        s = jnp.where(valid, s, -jnp.inf)
        mx = jnp.max(s, axis=-1, keepdims=True)
        e = jnp.exp(s - mx)
        den = jnp.sum(e, axis=-1, keepdims=True)
        outs.append(jnp.einsum('bhtm,btmhd->bthd', e / den, vg.astype(jnp.float32)))
        lses.append((mx + jnp.log(den))[..., 0])
    wts = jax.nn.softmax(jnp.stack(lses, axis=0), axis=0)
    wts = jnp.transpose(wts, (0, 1, 3, 2))[..., None]
    out = jnp.sum(wts * jnp.stack(outs, axis=0), axis=0)
    return out.astype(q.dtype)


def _dilated_prompt(q, k, v):
    b, s, h, d = q.shape
    nb = s // QBLOCK

    def block(i):
        q_blk = lax.dynamic_slice_in_dim(q, i * QBLOCK, QBLOCK, axis=1)
        qidx = i * QBLOCK + jnp.arange(QBLOCK, dtype=jnp.int32)
        return _dilated_mix(q_blk, k, v, qidx)

    out = lax.map(block, jnp.arange(nb, dtype=jnp.int32))
    return jnp.moveaxis(out, 0, 1).reshape(b, s, h, d)


def _sink_probs(s, sink, valid):
    s = jnp.where(valid, s, -jnp.inf)
    mx = jnp.maximum(jnp.max(s, axis=-1, keepdims=True), sink)
    e = jnp.exp(s - mx)
    den = jnp.sum(e, axis=-1, keepdims=True) + jnp.exp(sink - mx)
    return e / den


def _swa_prompt(q, k, v, sink):
    b, s = q.shape[:2]
    nb = s // QBLOCK
    qg = q.reshape(b, nb, QBLOCK, N_KV_B, GROUP_B, HEAD_DIM)
    kb = k.reshape(b, nb, QBLOCK, N_KV_B, HEAD_DIM)
    vb = v.reshape(b, nb, QBLOCK, N_KV_B, HEAD_DIM)
    padw = ((0, 0), (1, 0), (0, 0), (0, 0), (0, 0))
    kk = jnp.concatenate([jnp.pad(kb, padw)[:, :-1], kb], axis=2)
    vv = jnp.concatenate([jnp.pad(vb, padw)[:, :-1], vb], axis=2)
    qi = jnp.arange(QBLOCK, dtype=jnp.int32)[:, None]
    kj = jnp.arange(2 * QBLOCK, dtype=jnp.int32)[None, :]
    dist = QBLOCK + qi - kj
    kpos = (jnp.arange(nb, dtype=jnp.int32) * QBLOCK)[:, None, None] - QBLOCK + kj[None]
    valid = (dist >= 0) & (dist <= WIN_B) & (kpos >= 0)
    sc = jnp.einsum('bnqkgd,bnmkd->bkgnqm', qg, kk, preferred_element_type=jnp.float32) * SCALE
    p = _sink_probs(sc, sink[None, :, :, None, None, None], valid)
    o = jnp.einsum('bkgnqm,bnmkd->bnqkgd', p, vv.astype(jnp.float32))
    return o.reshape(b, s, QB_W).astype(q.dtype)


def _swa_sample(q, kk, vv, sink, qpos, kpos):
    b, t = q.shape[:2]
    qg = q.reshape(b, t, N_KV_B, GROUP_B, HEAD_DIM)
    sc = jnp.einsum('btkgd,blkd->bkgtl', qg, kk, preferred_element_type=jnp.float32) * SCALE
    dist = qpos[:, None] - kpos[None, :]
    valid = (dist >= 0) & (dist <= WIN_B)
    p = _sink_probs(sc, sink[None, :, :, None, None], valid)
    o = jnp.einsum('bkgtl,blkd->btkgd', p, vv.astype(jnp.float32))
    return o.reshape(b, t, QB_W).astype(q.dtype)


def _mix_out(h, oa, ob, w_out, g_post):
    b, t = h.shape[:2]
    y = jnp.concatenate([oa.reshape(b, t, QA_W), ob], axis=-1) @ w_out
    return h + _rmsnorm(y, g_post)


def _ple(h, p, g_pre, g_post, w_gate, w_proj):
    u = _rmsnorm(h, g_pre)
    y = jax.nn.sigmoid(u @ w_gate) * (p @ w_proj)
    return h + _rmsnorm(y, g_post)


def setup_inputs(seed: int = 0) -> dict:
    key = jax.random.key(seed)
    ks = jax.random.split(key, 32)
    f32 = jnp.float32
    la = min(WIN_A, PAST_LEN)
    lb = min(WIN_B, PAST_LEN)

    def nrm(k, shape, scale):
        return jax.random.normal(k, shape, f32) * scale

    def gain(k):
        return 1.0 + nrm(k, (DEPTH, D_MODEL), 0.02)

    return {
        'x_prompt': nrm(ks[0], (BATCH, SEQ, D_MODEL), 1.0),
        'x_sample': nrm(ks[1], (DEC_BATCH, DEC_SEQ, D_MODEL), 1.0),
        'cache_a_k': nrm(ks[2], (DEPTH, DEC_BATCH, la, N_HEADS_A, HEAD_DIM), 1.0),
        'cache_a_v': nrm(ks[3], (DEPTH, DEC_BATCH, la, N_HEADS_A, HEAD_DIM), 1.0),
        'cache_b_k': nrm(ks[4], (DEPTH, DEC_BATCH, lb, N_KV_B, HEAD_DIM), 1.0),
        'cache_b_v': nrm(ks[5], (DEPTH, DEC_BATCH, lb, N_KV_B, HEAD_DIM), 1.0),
        'p_prompt': nrm(ks[6], (DEPTH, BATCH, SEQ, D_PLE), 1.0),
        'p_sample': nrm(ks[7], (DEPTH, DEC_BATCH, DEC_SEQ, D_PLE), 1.0),
        'norm_f1_pre': gain(ks[8]),
        'norm_f1_post': gain(ks[9]),
        'w_f1_gate': nrm(ks[10], (DEPTH, D_MODEL, D_FF), D_MODEL ** -0.5),
        'w_f1_up': nrm(ks[11], (DEPTH, D_MODEL, D_FF), D_MODEL ** -0.5),
        'w_f1_down': nrm(ks[12], (DEPTH, D_FF, D_MODEL), D_FF ** -0.5),
        'norm_mix_pre': gain(ks[13]),
        'norm_mix_post': gain(ks[14]),
        'w_in': nrm(ks[15], (DEPTH, D_MODEL, IN_W), D_MODEL ** -0.5),
        'sinks_b': nrm(ks[16], (DEPTH, N_HEADS_B), 0.5),
        'w_out': nrm(ks[17], (DEPTH, MIX_W, D_MODEL), MIX_W ** -0.5),
        'norm_f2_pre': gain(ks[18]),
        'norm_f2_post': gain(ks[19]),
        'w_f2_gate': nrm(ks[20], (DEPTH, D_MODEL, D_FF), D_MODEL ** -0.5),
        'w_f2_up': nrm(ks[21], (DEPTH, D_MODEL, D_FF), D_MODEL ** -0.5),
        'w_f2_down': nrm(ks[22], (DEPTH, D_FF, D_MODEL), D_FF ** -0.5),
        'norm_ple_pre': gain(ks[23]),
        'norm_ple_post': gain(ks[24]),
        'w_ple_gate': nrm(ks[25], (DEPTH, D_MODEL, D_MODEL), D_MODEL ** -0.5),
        'w_ple_proj': nrm(ks[26], (DEPTH, D_PLE, D_MODEL), D_PLE ** -0.5),
    }


def reference(x_prompt, x_sample, cache_a_k, cache_a_v, cache_b_k, cache_b_v, p_prompt, p_sample,
              norm_f1_pre, norm_f1_post, w_f1_gate, w_f1_up, w_f1_down,
              norm_mix_pre, norm_mix_post, w_in, sinks_b, w_out,
              norm_f2_pre, norm_f2_post, w_f2_gate, w_f2_up, w_f2_down,
              norm_ple_pre, norm_ple_post, w_ple_gate, w_ple_proj):
    pos_p = jnp.arange(SEQ, dtype=jnp.int32)
    pos_s = PAST_LEN + jnp.arange(DEC_SEQ, dtype=jnp.int32)
    la = cache_a_k.shape[2]
    lb = cache_b_k.shape[2]
    keep_a = min(WIN_A, SEQ)
    keep_b = min(WIN_B, SEQ)
    kpos_b = PAST_LEN - lb + jnp.arange(lb + DEC_SEQ, dtype=jnp.int32)
    qidx_a = la + jnp.arange(DEC_SEQ, dtype=jnp.int32)
    hp, hs = x_prompt, x_sample
    nak_p, nav_p, nbk_p, nbv_p = [], [], [], []
    nak_s, nav_s, nbk_s, nbv_s = [], [], [], []
    for i in range(DEPTH):
        hp = _ffn_half(hp, norm_f1_pre[i], norm_f1_post[i], w_f1_gate[i], w_f1_up[i], w_f1_down[i])
        hs = _ffn_half(hs, norm_f1_pre[i], norm_f1_post[i], w_f1_gate[i], w_f1_up[i], w_f1_down[i])
        sink = sinks_b[i].astype(jnp.float32).reshape(N_KV_B, GROUP_B)
        qa, ka, va, qb, kb, vb = _project(hp, norm_mix_pre[i], w_in[i], pos_p)
        oa = _dilated_prompt(qa, ka, va)
        ob = _swa_prompt(qb, kb, vb, sink)
        hp = _mix_out(hp, oa, ob, w_out[i], norm_mix_post[i])
        nak_p.append(ka[:, SEQ - keep_a:])
        nav_p.append(va[:, SEQ - keep_a:])
        nbk_p.append(kb[:, SEQ - keep_b:])
        nbv_p.append(vb[:, SEQ - keep_b:])
        qa_s, ka_s, va_s, qb_s, kb_s, vb_s = _project(hs, norm_mix_pre[i], w_in[i], pos_s)
        kka = jnp.concatenate([cache_a_k[i].astype(ka_s.dtype), ka_s], axis=1)
        vva = jnp.concatenate([cache_a_v[i].astype(va_s.dtype), va_s], axis=1)
        kkb = jnp.concatenate([cache_b_k[i].astype(kb_s.dtype), kb_s], axis=1)
        vvb = jnp.concatenate([cache_b_v[i].astype(vb_s.dtype), vb_s], axis=1)
        oa_s = _dilated_mix(qa_s, kka, vva, qidx_a)
        ob_s = _swa_sample(qb_s, kkb, vvb, sink, pos_s, kpos_b)
        hs = _mix_out(hs, oa_s, ob_s, w_out[i], norm_mix_post[i])
        nak_s.append(kka[:, DEC_SEQ:])
        nav_s.append(vva[:, DEC_SEQ:])
        nbk_s.append(kkb[:, DEC_SEQ:])
        nbv_s.append(vvb[:, DEC_SEQ:])
        hp = _ffn_half(hp, norm_f2_pre[i], norm_f2_post[i], w_f2_gate[i], w_f2_up[i], w_f2_down[i])
        hs = _ffn_half(hs, norm_f2_pre[i], norm_f2_post[i], w_f2_gate[i], w_f2_up[i], w_f2_down[i])
        hp = _ple(hp, p_prompt[i], norm_ple_pre[i], norm_ple_post[i], w_ple_gate[i], w_ple_proj[i])
        hs = _ple(hs, p_sample[i], norm_ple_pre[i], norm_ple_post[i], w_ple_gate[i], w_ple_proj[i])
    return (hp, hs,
            jnp.stack(nak_p), jnp.stack(nav_p), jnp.stack(nbk_p), jnp.stack(nbv_p),
            jnp.stack(nak_s), jnp.stack(nav_s), jnp.stack(nbk_s), jnp.stack(nbv_s))
```

```python
import numpy as np
import concourse.bass as bass
import concourse.mybir as mybir
from concourse.bass_utils import run_bass_kernel_spmd
from contextlib import ExitStack

F32, BF16 = mybir.dt.float32, mybir.dt.bfloat16
AF = mybir.ActivationFunctionType
ALU = mybir.AluOpType
AX = mybir.AxisListType

D = 1024
DFF = 2816
FC = 22
NH = 2048
NO = 2048
NS = 16
NOS = NO + NS
NT = NH + NOS
INW = 2304
EPS = 1e-6
SCALE = 0.125
PAST = 16384
NCORES = 8
G_F1PRE, G_F1POST, G_MIXPRE, G_MIXPOST, G_F2PRE, G_F2POST, G_PLEPRE, G_PLEPOST = range(8)

DEBUG = False


class Res:
    __slots__ = ("w", "r")

    def __init__(self):
        self.w = {}
        self.r = {}


class Sched:
    def __init__(self, nc, es, n_dma_sems=40):
        self.nc = nc
        self.sems = []
        self.E = {}
        for name, eng in (("pe", nc.tensor), ("act", nc.scalar), ("dve", nc.vector),
                          ("pool", nc.gpsimd), ("sp", nc.sync)):
            sem = es.enter_context(nc.semaphore("s_" + name))
            self.sems.append(sem)
            self.E[name] = dict(eng=eng, key=len(self.sems) - 1, cnt=0, waited={}, name=name)
        self.dma = []
        for i in range(n_dma_sems):
            sem = es.enter_context(nc.semaphore("s_dma%d" % i))
            self.sems.append(sem)
            self.dma.append(dict(key=len(self.sems) - 1, cnt=0))
        self.dma_rr = 0
        self.big = dict(key=None, cnt=0)
        sem = es.enter_context(nc.semaphore("s_big"))
        self.sems.append(sem)
        self.big["key"] = len(self.sems) - 1

    def _wait(self, e, deps):
        for k, v in deps.items():
            if v <= 0 or e["waited"].get(k, 0) >= v:
                continue
            e["eng"].wait_ge(self.sems[k], v)
            e["waited"][k] = v

    def _deps(self, e, reads, writes):
        deps = {}
        own = e["key"]
        for r in reads:
            for k, v in r.w.items():
                if k == own and e["name"] == "pe":
                    continue
                if deps.get(k, 0) < v:
                    deps[k] = v
        for w in writes:
            for k, v in list(w.w.items()) + list(w.r.items()):
                if k == own:
                    continue
                if deps.get(k, 0) < v:
                    deps[k] = v
        return deps

    def op(self, ename, fn, reads=(), writes=(), signal=True):
        e = self.E[ename]
        self._wait(e, self._deps(e, reads, writes))
        ins = fn(e["eng"])
        if signal:
            e["cnt"] += 1
            ins.then_inc(self.sems[e["key"]], 1)
            val = e["cnt"]
        else:
            val = e["cnt"] + 1
        k = e["key"]
        for r in reads:
            if r.r.get(k, 0) < val:
                r.r[k] = val
        for w in writes:
            w.w[k] = val
            w.r = {}
        return ins

    def dma_start(self, qname, out, in_, reads=(), writes=(), big=False):
        q = self.E[qname]
        deps = self._deps(q, reads, writes)
        if big:
            s = self.big
        else:
            s = self.dma[self.dma_rr]
            self.dma_rr = (self.dma_rr + 1) % len(self.dma)
            if s["cnt"] > 0:
                deps[s["key"]] = max(deps.get(s["key"], 0), s["cnt"])
        self._wait(q, deps)
        ins = q["eng"].dma_start(out=out, in_=in_)
        s["cnt"] += 16
        ins.then_inc(self.sems[s["key"]], 16)
        k = s["key"]
        for r in reads:
            if r.r.get(k, 0) < s["cnt"]:
                r.r[k] = s["cnt"]
        for w in writes:
            w.w[k] = s["cnt"]
            w.r = {}
        return ins

    def barrier(self, final=False):
        tot = {}
        for e in self.E.values():
            tot[e["key"]] = e["cnt"]
        for s in self.dma + ([self.big] if final else []):
            tot[s["key"]] = s["cnt"]
        for e in self.E.values():
            d = dict(tot)
            d.pop(e["key"], None)
            self._wait(e, d)


def _rope_tables(pos):
    inv = np.power(np.float32(10000.0), -np.arange(32, dtype=np.float32) * np.float32(2.0) / np.float32(64.0)).astype(np.float32)
    ang = pos.astype(np.float32)[:, None] * inv[None, :]
    return np.cos(ang).astype(np.float32), np.sin(ang).astype(np.float32)


def build_program():
    nc = bass.Bass("TRN2", target_bir_lowering=False)

    def din(name, shape, dt=F32):
        return nc.dram_tensor(name, list(shape), dt, kind="ExternalInput").ap()

    def dout(name, shape, dt=F32):
        return nc.dram_tensor(name, list(shape), dt, kind="ExternalOutput").ap()

    def dscr(name, shape, dt):
        kind = "ExternalOutput" if DEBUG else "Internal"
        return nc.dram_tensor(name, list(shape), dt, kind=kind).ap()

    xT = din("xT", [D, NT])
    pT = din("pT", [256, NOS])
    gains_d = din("gains", [128, 64])
    cos_d = din("cos_t", [128, 33 * 32])
    sin_d = din("sin_t", [128, 33 * 32])
    mask_d = din("masks", [128, 768])
    ident_d = din("ident", [128, 128])
    sel_d = din("sel", [16, 16 * 128])
    selT_d = din("selT", [128, 16 * 16])
    sinks_d = din("sinks", [128, 8])
    w_f1g = din("w_f1_gate", [D, DFF]); w_f1u = din("w_f1_up", [D, DFF]); w_f1d = din("w_f1_down", [DFF, D])
    w_f2g = din("w_f2_gate", [D, DFF]); w_f2u = din("w_f2_up", [D, DFF]); w_f2d = din("w_f2_down", [DFF, D])
    w_in_d = din("w_in", [D, INW]); w_out_d = din("w_out", [D, D])
    w_pg_d = din("w_ple_gate", [D, D]); w_pp_d = din("w_ple_proj", [256, D])
    cak = din("cache_a_k", [NS, 2048, 512]); cav = din("cache_a_v", [NS, 2048, 512])
    cbk = din("cache_b_k", [NS, 128, 128]); cbv = din("cache_b_v", [NS, 128, 128])

    yT = dout("yT", [D, NOS])
    ka_o = dout("ka_o", [NO, 512]); va_o = dout("va_o", [NO, 512])
    kb_o = dout("kb_o", [NO, 128]); vb_o = dout("vb_o", [NO, 128])
    nak_s = dout("nak_s", [NS, 2048, 512]); nav_s = dout("nav_s", [NS, 2048, 512])
    nbk_s = dout("nbk_s", [NS, 128, 128]); nbv_s = dout("nbv_s", [NS, 128, 128])

    h1T = dscr("h1T", [D, NT], F32)
    h2T = dscr("h2T", [D, NOS], F32)
    h3T = dscr("h3T", [D, NOS], F32)
    QTs = dscr("QTs", [8, 128, NO], BF16)
    KaTs = dscr("KaTs", [4, 128, NH + NO], BF16)
    KbTs = dscr("KbTs", [2, 128, NH + NO], BF16)
    Va_s = dscr("Va_s", [4, NH + NO, 192], BF16)
    Vb_s = dscr("Vb_s", [2, NH + NO, 192], BF16)

    with ExitStack() as es:
        S = Sched(nc, es)

        def sb(stack, name, shape, dt):
            return stack.enter_context(nc.sbuf_tensor("sb_" + name, list(shape), dt))

        ones = sb(es, "ones", [128, 128], BF16)
        ident = sb(es, "ident", [128, 128], BF16)
        identf = sb(es, "identf", [128, 128], F32)
        gains = sb(es, "gains", [128, 64], F32)
        gains_h = sb(es, "gains_h", [128, 64], F32)
        masks = sb(es, "masks", [128, 768], BF16)
        esink = sb(es, "esink", [128, 8], F32)
        zs_qa = sb(es, "zs_qa", [16, 512], F32); zs_ka = sb(es, "zs_ka", [16, 512], F32)
        zs_va = sb(es, "zs_va", [16, 512], F32); zs_qb = sb(es, "zs_qb", [16, 512], F32)
        zs_kb = sb(es, "zs_kb", [16, 128], F32); zs_vb = sb(es, "zs_vb", [16, 128], F32)
        R_const = Res(); R_zs = Res(); R_ostok = Res()
        PS = []
        RPS = []
        for i in range(8):
            PS.append(es.enter_context(nc.psum_tensor("ps%d" % i, [128, 512], F32)))
            RPS.append(Res())

        S.op("dve", lambda e: e.memset(ones[:], 1.0), writes=[R_const])
        S.dma_start("sp", gains[:], gains_d, writes=[R_const])
        S.dma_start("pool", masks[:], mask_d, writes=[R_const])
        S.dma_start("pool", ident[:], ident_d, writes=[R_const])
        S.dma_start("sp", identf[:], ident_d, writes=[R_const])
        S.dma_start("sp", esink[:], sinks_d, writes=[R_const])
        S.op("dve", lambda e: e.tensor_scalar(out=gains_h[:], in0=gains[:], scalar1=0.5, scalar2=None, op0=ALU.mult),
             reads=[R_const], writes=[R_const])
        S.op("act", lambda e: e.activation(out=esink[:], in_=esink[:], func=AF.Exp), reads=[R_const], writes=[R_const])

        R_cache_out = Res()

        def flat16(ap):
            return ap.rearrange("r c -> (r c)").rearrange("(a b x) -> a b x", a=16, b=32)
        bg_dmas = []
        for b in range(NS):
            bg_dmas.append(lambda b=b: S.dma_start("sp", flat16(nak_s[b, 0:2047, :]), flat16(cak[b, 1:2048, :]), writes=[R_cache_out], big=True))
            bg_dmas.append(lambda b=b: S.dma_start("sp", flat16(nav_s[b, 0:2047, :]), flat16(cav[b, 1:2048, :]), writes=[R_cache_out], big=True))
        bg_dmas.append(lambda: S.dma_start("sp", nbk_s[:, 0:127, :], cbk[:, 1:128, :], writes=[R_cache_out], big=True))
        bg_dmas.append(lambda: S.dma_start("sp", nbv_s[:, 0:127, :], cbv[:, 1:128, :], writes=[R_cache_out], big=True))

        def mm(out, lhsT, rhs, start, stop, reads, writes, signal):
            return S.op("pe", lambda e: e.matmul(out, lhsT, rhs, start=start, stop=stop),
                        reads=reads, writes=writes, signal=signal)

        def gcol(n, c, half=False):
            t = gains_h if half else gains
            return t[:, n * 8 + c:n * 8 + c + 1]

        def rstd_from_sq(sq, R_sq, T, psn, R_psn, rstd, R_rstd):
            for c in range(8):
                mm(psn[:, :T], ones[:], sq[:, c, :T], c == 0, c == 7, [R_sq, R_const], [R_psn], c == 7)
            S.op("dve", lambda e: e.tensor_scalar(out=rstd[:, :T], in0=psn[:, :T], scalar1=1.0 / D, scalar2=EPS,
                                                  op0=ALU.mult, op1=ALU.add), reads=[R_psn], writes=[R_rstd])
            S.op("act", lambda e: e.activation(out=rstd[:, :T], in_=rstd[:, :T], func=AF.Sqrt), reads=[R_rstd], writes=[R_rstd])
            S.op("dve", lambda e: e.reciprocal(out=rstd[:, :T], in_=rstd[:, :T]), reads=[R_rstd], writes=[R_rstd])

        def ffn_phase(tag, wg_d, wu_d, wd_d, g_pre, g_post, src, src_off, dst, dst_off, ncols, bg_per_tile=0):
            TT = 256
            tiles = [(t0, min(TT, ncols - t0)) for t0 in range(0, ncols, TT)]
            with ExitStack() as ps:
                wg = sb(ps, tag + "wg", [128, 8, DFF], BF16)
                wu = sb(ps, tag + "wu", [128, 8, DFF], BF16)
                wd = sb(ps, tag + "wd", [128, FC, D], BF16)
                FG = [(0, 2), (2, 6), (6, 12), (12, 22)]
                R_wgu = [Res() for _ in FG]
                R_wd = [Res() for _ in FG]
                fgrp = {}
                for gi, (f0, f1) in enumerate(FG):
                    for f in range(f0, f1):
                        fgrp[f] = gi

                def load_weights():
                    for gi, (f0, f1) in enumerate(FG):
                        c0, c1 = f0 * 128, f1 * 128
                        S.dma_start("pool", wg[:, :, c0:c1], wg_d[:, c0:c1].rearrange("(k p) c -> p k c", p=128), writes=[R_wgu[gi]])
                        S.dma_start("pool", wu[:, :, c0:c1], wu_d[:, c0:c1].rearrange("(k p) c -> p k c", p=128), writes=[R_wgu[gi]])
                    for gi, (f0, f1) in enumerate(FG):
                        S.dma_start("pool", wd[:, f0:f1, :],
                                    wd_d[f0 * 128:f1 * 128, :].rearrange("(f p) c -> p f c", p=128), writes=[R_wd[gi]])
                X = [(sb(ps, tag + "x%d" % i, [128, 8, TT], F32), Res()) for i in range(3)]
                U = [(sb(ps, tag + "u%d" % i, [128, 8, TT], BF16), Res()) for i in range(2)]
                hid = sb(ps, tag + "hid", [128, FC, TT], BF16); R_hid = Res()
                sq = sb(ps, tag + "sq", [128, 8, TT], BF16); R_sq = Res()
                RS = [(sb(ps, tag + "rs%d" % i, [128, TT], F32), Res()) for i in range(2)]
                SG = [(sb(ps, tag + "sg%d" % i, [128, TT], F32), Res()) for i in range(3)]
                TMP = [(sb(ps, tag + "tmp%d" % i, [128, TT], F32), Res()) for i in range(2)]
                psn, R_psn = PS[6], RPS[6]

                def load(i):
                    t0, T = tiles[i]
                    xt, xr = X[i % 3]
                    S.dma_start("sp", xt[:, :, :T],
                                src[:, src_off + t0:src_off + t0 + T].rearrange("(c p) t -> p c t", p=128), writes=[xr])

                GUB = [4, 5, 7]

                def prenorm_steps(i):
                    t0, T = tiles[i]
                    xt, xr = X[i % 3]
                    u, ur = U[i % 2]
                    rs, rr = RS[0]

                    def a():
                        S.op("act", lambda e: e.activation(out=sq[:, :, :T], in_=xt[:, :, :T], func=AF.Square),
                             reads=[xr], writes=[R_sq])

                    def b():
                        for c in range(8):
                            mm(psn[:, :T], ones[:], sq[:, c, :T], c == 0, c == 7, [R_sq, R_const], [R_psn], c == 7)
                        S.op("dve", lambda e: e.tensor_scalar(out=rs[:, :T], in0=psn[:, :T], scalar1=1.0 / D, scalar2=EPS,
                                                              op0=ALU.mult, op1=ALU.add), reads=[R_psn], writes=[rr])
                        S.op("act", lambda e: e.activation(out=rs[:, :T], in_=rs[:, :T], func=AF.Sqrt), reads=[rr], writes=[rr])

                    def c_():
                        S.op("dve", lambda e: e.reciprocal(out=rs[:, :T], in_=rs[:, :T]), reads=[rr], writes=[rr])

                    def d(cs):
                        for c in cs:
                            S.op("dve", lambda e, c=c: e.scalar_tensor_tensor(
                                out=u[:, c, :T], in0=xt[:, c, :T], scalar=gcol(g_pre, c), in1=rs[:, :T],
                                op0=ALU.mult, op1=ALU.mult), reads=[xr, rr, R_const], writes=[ur])

                    return [(14, a), (16, b), (18, c_), (19, lambda: d(range(0, 3))), (20, lambda: d(range(3, 6))), (21, lambda: d(range(6, 8)))]

                def gateup(i, hooks):
                    t0, T = tiles[i]
                    u, ur = U[i % 2]
                    for f in range(FC):
                        for fn in hooks.get(f, []):
                            fn()
                        bkn = GUB[f % 3]
                        pb, rpb = PS[bkn], RPS[bkn]
                        sg, rsg = SG[f % 3]
                        for k in range(8):
                            mm(pb[:, 0:T], wg[:, k, f * 128:(f + 1) * 128], u[:, k, :T], k == 0, k == 7,
                               [R_wgu[fgrp[f]], ur], [rpb], False)
                        for k in range(8):
                            mm(pb[:, 256:256 + T], wu[:, k, f * 128:(f + 1) * 128], u[:, k, :T], k == 0, k == 7,
                               [R_wgu[fgrp[f]], ur], [rpb], k == 7)
                        S.op("act", lambda e: e.activation(out=sg[:, :T], in_=pb[:, 0:T], func=AF.Silu),
                             reads=[rpb], writes=[rsg])
                        S.op("dve", lambda e, f=f: e.tensor_tensor(out=hid[:, f, :T], in0=sg[:, :T], in1=pb[:, 256:256 + T],
                                                                    op=ALU.mult), reads=[rsg, rpb], writes=[R_hid])

                def down_mm(i):
                    t0, T = tiles[i]
                    for c in range(8):
                        bank, rb = PS[c // 2], RPS[c // 2]
                        off = (c % 2) * 256
                        for f in range(FC):
                            mm(bank[:, off:off + T], wd[:, f, c * 128:(c + 1) * 128], hid[:, f, :T], f == 0, f == FC - 1,
                               [R_wd[fgrp[f]], R_hid], [rb], f == FC - 1)
                    for b in range(4):
                        S.op("act", lambda e, b=b: e.activation(
                            out=sq[:, 2 * b:2 * b + 2, :T], in_=PS[b][:].rearrange("p (a t) -> p a t", a=2)[:, :, :T],
                            func=AF.Square), reads=[RPS[b]], writes=[R_sq])

                def post_steps(i):
                    t0, T = tiles[i]
                    xt, xr = X[i % 3]
                    rs, rr = RS[1]

                    def a():
                        for c in range(8):
                            mm(psn[:, :T], ones[:], sq[:, c, :T], c == 0, c == 7, [R_sq, R_const], [R_psn], c == 7)
                        S.op("dve", lambda e: e.tensor_scalar(out=rs[:, :T], in0=psn[:, :T], scalar1=1.0 / D, scalar2=EPS,
                                                              op0=ALU.mult, op1=ALU.add), reads=[R_psn], writes=[rr])
                        S.op("act", lambda e: e.activation(out=rs[:, :T], in_=rs[:, :T], func=AF.Sqrt), reads=[rr], writes=[rr])

                    def b():
                        S.op("dve", lambda e: e.reciprocal(out=rs[:, :T], in_=rs[:, :T]), reads=[rr], writes=[rr])

                    def cstep(c):
                        bank, rb = PS[c // 2], RPS[c // 2]
                        off = (c % 2) * 256
                        tm, rt = TMP[c % 2]
                        S.op("dve", lambda e: e.scalar_tensor_tensor(
                            out=tm[:, :T], in0=bank[:, off:off + T], scalar=gcol(g_post, c, True), in1=rs[:, :T],
                            op0=ALU.mult, op1=ALU.mult), reads=[rb, rr, R_const], writes=[rt])
                        S.op("pool", lambda e: e.tensor_tensor(out=xt[:, c, :T], in0=xt[:, c, :T], in1=tm[:, :T],
                                                               op=ALU.add), reads=[rt, xr], writes=[xr])

                    def st():
                        S.dma_start("pool", dst[:, dst_off + t0:dst_off + t0 + T].rearrange("(c p) t -> p c t", p=128),
                                    xt[:, :, :T], reads=[xr])

                    steps = [(2, a), (4, b)]
                    for c in range(8):
                        steps.append((5 + c, (lambda c=c: cstep(c))))
                    steps.append((13, st))
                    return steps

                n = len(tiles)
                load(0)
                load_weights()
                if n > 1:
                    load(1)
                for (_, fn) in prenorm_steps(0):
                    fn()
                for i in range(n):
                    hooks = {}
                    if i >= 1:
                        for (f, fn) in post_steps(i - 1):
                            hooks.setdefault(f, []).append(fn)
                    if i + 1 < n:
                        for (f, fn) in prenorm_steps(i + 1):
                            hooks.setdefault(f, []).append(fn)
                    gateup(i, hooks)
                    if i + 2 < n:
                        load(i + 2)
                    for _ in range(bg_per_tile):
                        if bg_dmas:
                            bg_dmas.pop(0)()
                    down_mm(i)
                for (_, fn) in post_steps(n - 1):
                    fn()
                S.barrier()

        ffn_phase("f1", w_f1g, w_f1u, w_f1d, G_F1PRE, G_F1POST, xT, 0, h1T, 0, NT, bg_per_tile=2)
        while bg_dmas:
            bg_dmas.pop(0)()

        with ExitStack() as ps:
            w_in = sb(ps, "w_in", [128, 8, INW], BF16); R_w = Res()
            for k in range(8):
                S.dma_start("pool", w_in[:, k, :], w_in_d[k * 128:(k + 1) * 128, :], writes=[R_w])
            cosT = sb(ps, "cosT", [128, 33, 32], F32); sinT = sb(ps, "sinT", [128, 33, 32], F32); R_tab = Res()
            S.dma_start("sp", cosT[:].rearrange("p a b -> p (a b)"), cos_d, writes=[R_tab])
            S.dma_start("sp", sinT[:].rearrange("p a b -> p (a b)"), sin_d, writes=[R_tab])
            ST = 512
            X = [(sb(ps, "p2x%d" % i, [128, 8, ST], F32), Res()) for i in range(2)]
            U = [(sb(ps, "p2u%d" % i, [128, 8, ST], BF16), Res()) for i in range(2)]
            sq = sb(ps, "p2sq", [128, 8, ST], BF16); R_sq = Res()
            rs = sb(ps, "p2rs", [128, ST], F32); R_rs = Res()
            NB = 2
            NT4 = 4
            T4 = [[(sb(ps, "p2t%d_%d" % (i, j), [128, 8, 32], F32), Res()) for j in range(4)] for i in range(NT4)]
            t4_ctr = [0]
            ka_f = [(sb(ps, "ka_f%d" % i, [128, 512], F32), Res()) for i in range(NB)]
            va_f = [(sb(ps, "va_f%d" % i, [128, 512], F32), Res()) for i in range(NB)]
            kvb_f = [(sb(ps, "kvb_f%d" % i, [128, 256], F32), Res()) for i in range(NB)]
            q_b = [(sb(ps, "q_b%d" % i, [128, 1024], BF16), Res()) for i in range(NB)]
            ka_b = [(sb(ps, "ka_b%d" % i, [128, 512], BF16), Res()) for i in range(NB)]
            kbdup = [(sb(ps, "kbdup%d" % i, [128, 2, 2, 64], BF16), Res()) for i in range(NB)]
            vaug = [(sb(ps, "vaug%d" % i, [128, 4, 192], BF16), Res()) for i in range(NB)]
            vbaug = [(sb(ps, "vbaug%d" % i, [128, 2, 192], BF16), Res()) for i in range(NB)]
            qT_st = [(sb(ps, "qT_st%d" % i, [128, 8, ST], BF16), Res()) for i in range(2)]
            kaT_st = [(sb(ps, "kaT_st%d" % i, [128, 4, ST], BF16), Res()) for i in range(2)]
            kbT_st = [(sb(ps, "kbT_st%d" % i, [128, 2, ST], BF16), Res()) for i in range(2)]
            TA = PS[6][:].bitcast(BF16); R_TA = RPS[6]
            TB = PS[7][:].bitcast(BF16); R_TB = RPS[7]
            psn, R_psn = PS[5], RPS[5]
            for i in range(NB):
                S.op("dve", lambda e, i=i: e.memset(vaug[i][0][:], 1.0), writes=[vaug[i][1]])
                S.op("dve", lambda e, i=i: e.memset(vbaug[i][0][:], 1.0), writes=[vbaug[i][1]])

            stiles = [(t0, min(ST, NT - t0)) for t0 in range(0, NT, ST)]

            def p2_load(i):
                t0, T = stiles[i]
                xt, xr = X[i % 2]
                S.dma_start("sp", xt[:, :, :T], h1T[:, t0:t0 + T].rearrange("(c p) t -> p c t", p=128), writes=[xr])

            def p2_norm_steps(i):
                t0, T = stiles[i]
                xt, xr = X[i % 2]
                u, ur = U[i % 2]

                def a():
                    S.op("act", lambda e: e.activation(out=sq[:, :, :T], in_=xt[:, :, :T], func=AF.Square),
                         reads=[xr], writes=[R_sq])

                def b():
                    rstd_from_sq(sq, R_sq, T, psn, R_psn, rs, R_rs)

                def d(cs):
                    for c in cs:
                        S.op("dve", lambda e, c=c: e.scalar_tensor_tensor(
                            out=u[:, c, :T], in0=xt[:, c, :T], scalar=gcol(G_MIXPRE, c), in1=rs[:, :T],
                            op0=ALU.mult, op1=ALU.mult), reads=[xr, R_rs, R_const], writes=[ur])

                return {0: [a], 1: [b], 2: [lambda: d(range(0, 4))], 3: [lambda: d(range(4, 8))]}

            def rope(src, H, np_, ti, dst1, dst2, reads, wres, bi):
                t4 = T4[t4_ctr[0] % NT4]
                t4_ctr[0] += 1
                cb = cosT[:np_, ti, :].unsqueeze(1).to_broadcast([np_, H, 32])
                sn = sinT[:np_, ti, :].unsqueeze(1).to_broadcast([np_, H, 32])
                x1 = src[:, :, 0:32]
                x2 = src[:, :, 32:64]
                for j, (a, b_) in enumerate(((x1, cb), (x2, sn), (x2, cb), (x1, sn))):
                    S.op("dve", lambda e, j=j, a=a, b_=b_: e.tensor_tensor(out=t4[j][0][:np_, :H, :], in0=a, in1=b_, op=ALU.mult),
                         reads=reads + [R_tab], writes=[t4[j][1]])
                S.op("pool", lambda e: e.tensor_tensor(out=dst1, in0=t4[0][0][:np_, :H, :], in1=t4[1][0][:np_, :H, :],
                                                       op=ALU.subtract), reads=[t4[0][1], t4[1][1]], writes=[wres])
                S.op("pool", lambda e: e.tensor_tensor(out=dst2, in0=t4[2][0][:np_, :H, :], in1=t4[3][0][:np_, :H, :],
                                                       op=ALU.add), reads=[t4[2][1], t4[3][1]], writes=[wres])

            def v3(ap, H):
                return ap.rearrange("p (h d) -> p h d", d=64)

            def p2_info(i, j):
                t0, T = stiles[i]
                c0 = j * 128
                np_ = min(128, T - c0)
                g0 = t0 + c0
                return dict(i=i, j=j, t0=t0, T=T, c0=c0, np_=np_, g0=g0, is_sample=g0 >= NH + NO, is_halo=g0 < NH, ti=g0 // 128)

            def p2_mm(sd):
                i, c0, np_ = sd["i"], sd["c0"], sd["np_"]
                u, ur = U[i % 2]
                slices = [(512, 512, 1), (2048, 256, 4), (1024, 512, 2), (0, 512, 0), (1536, 512, 3)]
                for (s0, w, bk) in slices:
                    if sd["is_halo"] and bk in (0, 3):
                        continue
                    for k in range(8):
                        mm(PS[bk][:np_, :w], u[:, k, c0:c0 + np_], w_in[:, k, s0:s0 + w], k == 0, k == 7,
                           [ur, R_w], [RPS[bk]], k == 7)

            def p2_post_a(sd, bi):
                i, c0, np_, g0, ti = sd["i"], sd["c0"], sd["np_"], sd["g0"], sd["ti"]
                is_halo = sd["is_halo"]
                if sd["is_sample"]:
                    rope(v3(PS[1][:np_, :], 8), 8, np_, ti, v3(zs_ka[:np_, :], 8)[:, :, 0:32], v3(zs_ka[:np_, :], 8)[:, :, 32:64], [RPS[1]], R_zs, bi)
                    rope(v3(PS[4][:np_, 0:128], 2), 2, np_, ti, v3(zs_kb[:np_, :], 2)[:, :, 0:32], v3(zs_kb[:np_, :], 2)[:, :, 32:64], [RPS[4]], R_zs, bi)
                    rope(v3(PS[0][:np_, :], 8), 8, np_, ti, v3(zs_qa[:np_, :], 8)[:, :, 0:32], v3(zs_qa[:np_, :], 8)[:, :, 32:64], [RPS[0]], R_zs, bi)
                    rope(v3(PS[3][:np_, :], 8), 8, np_, ti, v3(zs_qb[:np_, :], 8)[:, :, 0:32], v3(zs_qb[:np_, :], 8)[:, :, 32:64], [RPS[3]], R_zs, bi)
                    S.op("act", lambda e: e.activation(out=zs_va[:np_, :], in_=PS[2][:np_, :], func=AF.Copy), reads=[RPS[2]], writes=[R_zs])
                    S.op("act", lambda e: e.activation(out=zs_vb[:np_, :], in_=PS[4][:np_, 128:256], func=AF.Copy), reads=[RPS[4]], writes=[R_zs])
                    S.dma_start("pool", nak_s[:, 2047, :], zs_ka[:np_, :], reads=[R_zs], writes=[R_cache_out])
                    S.dma_start("pool", nav_s[:, 2047, :], zs_va[:np_, :], reads=[R_zs], writes=[R_cache_out])
                    S.dma_start("pool", nbk_s[:, 127, :], zs_kb[:np_, :], reads=[R_zs], writes=[R_cache_out])
                    S.dma_start("pool", nbv_s[:, 127, :], zs_vb[:np_, :], reads=[R_zs], writes=[R_cache_out])
                    return
                kaf, r_kaf = ka_f[bi]; vaf, r_vaf = va_f[bi]; kvf, r_kvf = kvb_f[bi]
                qb_, r_qb = q_b[bi]; kab, r_kab = ka_b[bi]; kbd, r_kbd = kbdup[bi]
                vg, r_vg = vaug[bi]; vbg, r_vbg = vbaug[bi]
                rope(v3(PS[1][:, :], 8), 8, 128, ti, v3(kaf[:], 8)[:, :, 0:32], v3(kaf[:], 8)[:, :, 32:64], [RPS[1]], r_kaf, bi)
                S.op("act", lambda e: e.activation(out=kab[:], in_=kaf[:], func=AF.Copy), reads=[r_kaf], writes=[r_kab])
                rope(v3(PS[4][:, 0:128], 2), 2, 128, ti, v3(kvf[:, 0:128], 2)[:, :, 0:32], v3(kvf[:, 0:128], 2)[:, :, 32:64], [RPS[4]], r_kvf, bi)
                S.op("act", lambda e: e.activation(out=kvf[:, 128:256], in_=PS[4][:, 128:256], func=AF.Copy), reads=[RPS[4]], writes=[r_kvf])
                for dd in range(2):
                    S.op("act", lambda e, dd=dd: e.activation(out=kbd[:, :, dd, :], in_=v3(kvf[:, 0:128], 2), func=AF.Copy),
                         reads=[r_kvf], writes=[r_kbd])
                vb4 = vbg[:].rearrange("p a (b c) -> p a b c", c=64)
                for dd in (0, 2):
                    S.op("act", lambda e, dd=dd: e.activation(out=vb4[:, :, dd, :], in_=v3(PS[4][:, 128:256], 2), func=AF.Copy),
                         reads=[RPS[4]], writes=[r_vbg])
                va4 = vg[:].rearrange("p a (b c) -> p a b c", c=64)
                S.op("act", lambda e: e.activation(out=va4[:, :, 0:3:2, :], in_=PS[2][:, :].rearrange("p (a b c) -> p a b c", b=2, c=64),
                                                   func=AF.Copy), reads=[RPS[2]], writes=[r_vg])
                if not is_halo:
                    S.op("act", lambda e: e.activation(out=vaf[:], in_=PS[2][:, :], func=AF.Copy), reads=[RPS[2]], writes=[r_vaf])
                    rope(v3(PS[0][:, :], 8), 8, 128, ti, v3(qb_[:, 0:512], 8)[:, :, 0:32], v3(qb_[:, 0:512], 8)[:, :, 32:64], [RPS[0]], r_qb, bi)
                    rope(v3(PS[3][:, :], 8), 8, 128, ti, v3(qb_[:, 512:1024], 8)[:, :, 0:32], v3(qb_[:, 512:1024], 8)[:, :, 32:64], [RPS[3]], r_qb, bi)

            def p2_post_b(sd, bi):
                i, c0, g0 = sd["i"], sd["c0"], sd["g0"]
                is_halo = sd["is_halo"]
                if sd["is_sample"]:
                    return
                kaf, r_kaf = ka_f[bi]; vaf, r_vaf = va_f[bi]; kvf, r_kvf = kvb_f[bi]
                qb_, r_qb = q_b[bi]; kab, r_kab = ka_b[bi]; kbd, r_kbd = kbdup[bi]
                vg, r_vg = vaug[bi]; vbg, r_vbg = vbaug[bi]
                S.dma_start("sp", Va_s[:, g0:g0 + 128, :].rearrange("a t c -> t a c"), vg[:], reads=[r_vg])
                S.dma_start("sp", Vb_s[:, g0:g0 + 128, :].rearrange("a t c -> t a c"), vbg[:], reads=[r_vbg])
                for c in range(4):
                    S.op("pe", lambda e, c=c: e.transpose(TB[:, c * 128:(c + 1) * 128], kab[:, c * 128:(c + 1) * 128], ident[:]),
                         reads=[r_kab, R_const], writes=[R_TB], signal=False)
                for g in range(2):
                    S.op("pe", lambda e, g=g: e.transpose(TB[:, (4 + g) * 128:(5 + g) * 128],
                                                          kbd[:, g, :, :].rearrange("p a b -> p (a b)"), ident[:]),
                         reads=[r_kbd, R_const], writes=[R_TB], signal=(g == 1))
                kst, r_kst = kaT_st[i % 2]
                bst, r_bst = kbT_st[i % 2]
                S.op("act", lambda e: e.activation(out=kst[:, :, c0:c0 + 128], in_=TB[:, 0:512].rearrange("p (c t) -> p c t", t=128),
                                                   func=AF.Copy), reads=[R_TB], writes=[r_kst])
                S.op("act", lambda e: e.activation(out=bst[:, :, c0:c0 + 128], in_=TB[:, 512:768].rearrange("p (c t) -> p c t", t=128),
                                                   func=AF.Copy), reads=[R_TB], writes=[r_bst])
                if not is_halo:
                    o0 = g0 - NH
                    S.dma_start("sp", ka_o[o0:o0 + 128, :], kaf[:], reads=[r_kaf])
                    S.dma_start("sp", va_o[o0:o0 + 128, :], vaf[:], reads=[r_vaf])
                    S.dma_start("sp", kb_o[o0:o0 + 128, :], kvf[:, 0:128], reads=[r_kvf])
                    S.dma_start("sp", vb_o[o0:o0 + 128, :], kvf[:, 128:256], reads=[r_kvf])
                    for c in range(8):
                        S.op("pe", lambda e, c=c: e.transpose(TA[:, c * 128:(c + 1) * 128], qb_[:, c * 128:(c + 1) * 128], ident[:]),
                             reads=[r_qb, R_const], writes=[R_TA], signal=(c == 7))
                    qst, r_qst = qT_st[i % 2]
                    S.op("act", lambda e: e.activation(out=qst[:, :, c0:c0 + 128], in_=TA.rearrange("p (c t) -> p c t", t=128),
                                                       func=AF.Copy), reads=[R_TA], writes=[r_qst])

            def p2_store(i):
                t0, T = stiles[i]
                if t0 >= NH + NO:
                    return
                kst, r_kst = kaT_st[i % 2]
                bst, r_bst = kbT_st[i % 2]
                S.dma_start("sp", KaTs[:, :, t0:t0 + T].rearrange("c p t -> p c t"), kst[:, :, :T], reads=[r_kst])
                S.dma_start("sp", KbTs[:, :, t0:t0 + T].rearrange("c p t -> p c t"), bst[:, :, :T], reads=[r_bst])
                if t0 >= NH:
                    qst, r_qst = qT_st[i % 2]
                    S.dma_start("sp", QTs[:, :, t0 - NH:t0 - NH + T].rearrange("c p t -> p c t"), qst[:, :, :T], reads=[r_qst])

            n = len(stiles)
            subs = [p2_info(i, j) for i in range(n) for j in range((stiles[i][1] + 127) // 128)]
            p2_load(0)
            for j in range(4):
                for fn in p2_norm_steps(0)[j]:
                    fn()
            if n > 1:
                p2_load(1)
            p2_mm(subs[0])
            nsteps = {}
            for si, sd in enumerate(subs):
                i = sd["i"]
                if sd["j"] == 0:
                    nsteps = p2_norm_steps(i + 1) if i + 1 < n else {}
                nsub_i = (stiles[i][1] + 127) // 128
                js = [sd["j"]] if sd["j"] + 1 < nsub_i else list(range(sd["j"], 4))
                for j in js:
                    for fn in nsteps.get(j, []):
                        fn()
                if sd["j"] == 0 and i + 2 < n:
                    p2_load(i + 2)
                p2_post_a(sd, si % NB)
                if si + 1 < len(subs):
                    p2_mm(subs[si + 1])
                p2_post_b(sd, si % NB)
                if si + 1 == len(subs) or subs[si + 1]["i"] != i:
                    p2_store(i)
            S.barrier()

        with ExitStack() as ps:
            OT = sb(ps, "OT", [128, 8, NOS], BF16); R_OT = Res()
            w_out = sb(ps, "w_out", [128, 8, D], BF16); R_wo = Res()
            for k in range(8):
                S.dma_start("pool", w_out[:, k, :], w_out_d[k * 128:(k + 1) * 128, :], writes=[R_wo])
            pa = ExitStack()
            QT = [(sb(pa, "a_qt%d" % i, [128, NO], BF16), Res()) for i in range(2)]
            KT = [(sb(pa, "a_kt%d" % i, [128, NH + NO], BF16), Res()) for i in range(2)]
            NVT = 17 + 20 + 32
            VT = [(sb(pa, "a_vt%d" % i, [128, NVT, 192], BF16), Res()) for i in range(2)]
            PB = [(sb(pa, "a_pb%d" % i, [128, 512], BF16), Res()) for i in range(4)]
            rec = sb(pa, "a_rec", [128, NO], F32); R_rec = Res()

            def tiles_for(dils):
                out = []
                idx = 0
                for d in dils:
                    for r in range(d):
                        for kb in range(16 // d - 1, 32 // d):
                            ci0 = max(128 * kb, NH // d)
                            ci1 = min(128 * (kb + 2), (NH + NO) // d)
                            nq = ci1 - ci0
                            out.append(dict(d=d, r=r, kb=kb, idx=idx, halo=(kb < 16 // d), nq=nq,
                                            q0=r + d * ci0 - NH, moff=ci0 - 128 * kb, k0=r + d * 128 * kb))
                            idx += 1
                return out

            tilesA = tiles_for((1, 4, 16))
            tilesB = tiles_for((1,))

            def job_load(job):
                qt, rq = QT[job % 2]; kt, rk = KT[job % 2]; vt, rv = VT[job % 2]
                S.dma_start("sp", qt[:], QTs[job, :, :], writes=[rq])
                if job < 4:
                    S.dma_start("sp", kt[:], KaTs[job, :, :], writes=[rk])
                    src, tl = Va_s[job], tilesA
                else:
                    g = (job - 4) // 2
                    S.dma_start("sp", kt[:], KbTs[g, :, :], writes=[rk])
                    src, tl = Vb_s[g], tilesB
                seen = {}
                for t in tl:
                    seen.setdefault((t["d"], t["r"]), []).append(t)
                for (d, r), ts in seen.items():
                    n = len(ts)
                    k0 = ts[0]["k0"]
                    S.dma_start("sp", vt[:, ts[0]["idx"]:ts[0]["idx"] + n, :],
                                src[k0:k0 + d * 128 * (n - 1) + d * 127 + 1:d, :].rearrange("(j i) c -> i j c", i=128),
                                writes=[rv])

            def build_packs(tl):
                packs, cur, cols = [], [], 0
                for t in tl:
                    if cur and (cols + t["nq"] > 512 or cur[0][0]["d"] != t["d"]):
                        packs.append(cur)
                        cur, cols = [], 0
                    cur.append((t, cols))
                    cols += t["nq"]
                if cur:
                    packs.append(cur)
                return packs

            packsA = build_packs(tilesA)
            packsB = build_packs(tilesB)
            work = []
            for job in range(8):
                pk = packsA if job < 4 else packsB
                for hh in range(2):
                    for pi, p in enumerate(pk):
                        work.append((job, hh, pi, p, pi == len(pk) - 1))
            started = {}

            def stage_abc(w, slot):
                job, hh, pi, p, last = w
                qt, rq = QT[job % 2]; kt, rk = KT[job % 2]
                hb = 64 * hh
                st_ps, r_st = PS[4 + slot], RPS[4 + slot]
                pbt, rpb = PB[slot]
                ncols = p[-1][1] + p[-1][0]["nq"]
                for ti, (t, a) in enumerate(p):
                    d, nq, q0, k0 = t["d"], t["nq"], t["q0"], t["k0"]
                    mm(st_ps[:, a:a + nq], kt[hb:hb + 64, k0:k0 + d * 127 + 1:d], qt[hb:hb + 64, q0:q0 + d * (nq - 1) + 1:d],
                       True, True, [rq, rk], [r_st], ti == len(p) - 1)
                S.op("act", lambda e: e.activation(out=pbt[:, :ncols], in_=st_ps[:, :ncols], func=AF.Exp, scale=SCALE),
                     reads=[r_st], writes=[rpb])
                sig = [(t["moff"] + (256 if t["halo"] else 0), t["nq"]) for (t, a) in p]
                if len(p) == 2 and sig == [(0, 256), (0, 256)]:
                    S.op("dve", lambda e: e.tensor_tensor(out=pbt[:, :512].rearrange("p (a c) -> p a c", a=2), in0=pbt[:, :512].rearrange("p (a c) -> p a c", a=2),
                                                          in1=masks[:, 0:256].unsqueeze(1).to_broadcast([128, 2, 256]), op=ALU.mult),
                         reads=[rpb, R_const], writes=[rpb])
                elif len(p) == 4 and sig == [(384, 128), (0, 128)] * 2:
                    S.op("dve", lambda e: e.tensor_tensor(out=pbt[:, :512].rearrange("p (a c) -> p a c", a=2), in0=pbt[:, :512].rearrange("p (a c) -> p a c", a=2),
                                                          in1=masks[:, 512:768].unsqueeze(1).to_broadcast([128, 2, 256]), op=ALU.mult),
                         reads=[rpb, R_const], writes=[rpb])
                else:
                    for (t, a), (mo, nq) in zip(p, sig):
                        S.op("dve", lambda e, a=a, mo=mo, nq=nq: e.tensor_tensor(out=pbt[:, a:a + nq], in0=pbt[:, a:a + nq], in1=masks[:, mo:mo + nq], op=ALU.mult),
                             reads=[rpb, R_const], writes=[rpb])

            def stage_d(w, slot):
                job, hh, pi, p, last = w
                vt, rv = VT[job % 2]
                pbt, rpb = PB[slot]
                if pi == 0:
                    for bank in range(4):
                        started[bank] = False
                allsegs = []
                for (t, a) in p:
                    d, nq, q0 = t["d"], t["nq"], t["q0"]
                    lw = vt[:, t["idx"], 0:128] if hh == 0 else vt[:, t["idx"], 64:192]
                    if d == 1:
                        assert q0 % 4 == 0 and nq % 4 == 0
                        for j in range(4):
                            allsegs.append((lw, pbt[:, a + j:a + nq:4], j, PS[j][:, q0 // 4:q0 // 4 + nq // 4]))
                    elif d == 4:
                        r4 = q0 % 4
                        c0 = q0 // 4
                        assert c0 + nq <= 512
                        allsegs.append((lw, pbt[:, a:a + nq], r4, PS[r4][:, c0:c0 + nq]))
                    else:
                        assert d == 16
                        bank = q0 % 4
                        c0 = q0 // 4
                        assert c0 + 4 * (nq - 1) < 512
                        allsegs.append((lw, pbt[:, a:a + nq], bank, PS[bank][:, c0:c0 + 4 * (nq - 1) + 1:4]))
                for si, (lw, rhs_ap, bank, out_ap) in enumerate(allsegs):
                    first = not started[bank]
                    started[bank] = True
                    mm(out_ap, lw, rhs_ap, first, True, [rv, rpb], [RPS[bank]], si == len(allsegs) - 1)
                if last:
                    ob, db = (0, 64) if hh == 0 else (64, 0)
                    c = job
                    for bank in range(4):
                        cs = slice(bank * 512, (bank + 1) * 512)
                        if job >= 4:
                            hq = 2 * (job - 4) + hh
                            S.op("act", lambda e, bank=bank, cs=cs, hq=hq: e.activation(
                                out=rec[db:db + 64, cs], in_=PS[bank][db:db + 64, :], func=AF.Ln, bias=esink[db:db + 64, hq:hq + 1]),
                                reads=[RPS[bank], R_const], writes=[R_rec])
                        else:
                            S.op("act", lambda e, bank=bank, cs=cs: e.activation(
                                out=rec[db:db + 64, cs], in_=PS[bank][db:db + 64, :], func=AF.Ln),
                                reads=[RPS[bank]], writes=[R_rec])
                    S.op("act", lambda e: e.activation(out=rec[db:db + 64, :], in_=rec[db:db + 64, :], func=AF.Exp, scale=-1.0),
                         reads=[R_rec], writes=[R_rec])
                    for bank in range(4):
                        cs = slice(bank * 512, (bank + 1) * 512)
                        S.op("dve", lambda e, bank=bank, cs=cs: e.tensor_tensor(
                            out=OT[ob:ob + 64, c, 0:NO].rearrange("p (x f) -> p f x", f=4)[:, bank, :], in0=PS[bank][ob:ob + 64, :],
                            in1=rec[db:db + 64, cs], op=ALU.mult), reads=[RPS[bank], R_rec], writes=[R_OT])

            LOOK = 2
            job_load(0)
            nw = len(work)
            for x in range(0, nw + LOOK, 2):
                for y in (x, x + 1):
                    if y < nw:
                        stage_abc(work[y], y % 4)
                for y in (x - LOOK, x - LOOK + 1):
                    if 0 <= y < nw:
                        w = work[y]
                        stage_d(w, y % 4)
                        if w[1] == 0 and w[2] == 0 and w[0] + 1 < 8:
                            job_load(w[0] + 1)
            S.barrier()
            pa.close()
            pb_ = ExitStack()
            sel = sb(pb_, "sel", [16, 16 * 128], F32)
            selT = sb(pb_, "selT", [128, 16 * 16], F32)
            R_sel = Res()
            S.dma_start("sp", sel[:], sel_d, writes=[R_sel])
            S.dma_start("sp", selT[:], selT_d, writes=[R_sel])
            NSB = 3
            KS = [(sb(pb_, "s_k%d" % i, [128, 3, 512], F32), Res()) for i in range(NSB)]
            VS = [(sb(pb_, "s_v%d" % i, [128, 3, 512], F32), Res()) for i in range(NSB)]
            KBS = [(sb(pb_, "s_kb%d" % i, [128, 128], F32), Res()) for i in range(NSB)]
            VBS = [(sb(pb_, "s_vb%d" % i, [128, 128], F32), Res()) for i in range(NSB)]
            prod = [(sb(pb_, "s_pr%d" % i, [128, 512], F32), Res()) for i in range(2)]
            sc = [(sb(pb_, "s_sc%d" % i, [128, 32], F32), Res()) for i in range(2)]
            ee = [(sb(pb_, "s_ee%d" % i, [128, 32], F32), Res()) for i in range(2)]
            pv = [(sb(pb_, "s_pv%d" % i, [128, 512], F32), Res()) for i in range(3)]
            tk = sb(pb_, "s_tk", [16, 1024], F32); R_tk = Res()
            snew = sb(pb_, "s_new", [16, 16], F32); R_snew = Res()
            den = sb(pb_, "s_den", [16, 16], F32); R_den = Res()
            osb = sb(pb_, "s_osb", [16, 1024], F32); R_osb = Res()
            pats = [(1920, 1), (1536, 4), (0, 16)]

            def s_load(b):
                k_, rk = KS[b % NSB]; v_, rv = VS[b % NSB]
                for pi, (r0, st) in enumerate(pats):
                    lo = 2048 - 128 * st
                    S.dma_start("sp", k_[:, pi, :], cak[b, lo:lo + st * 127 + 1:st, :], writes=[rk])
                S.dma_start("sp", KBS[b % NSB][0][:], cbk[b, :, :], writes=[KBS[b % NSB][1]])
                for pi, (r0, st) in enumerate(pats):
                    lo = 2048 - 128 * st
                    S.dma_start("pool", v_[:, pi, :], cav[b, lo:lo + st * 127 + 1:st, :], writes=[rv])
                S.dma_start("pool", VBS[b % NSB][0][:], cbv[b, :, :], writes=[VBS[b % NSB][1]])

            pvc = [0]

            def s_compute(b):
                k_, rk = KS[b % NSB]; v_, rv = VS[b % NSB]
                kb_, rkb = KBS[b % NSB]; vb_, rvb = VBS[b % NSB]
                s_, rs_ = sc[b % 2]; e_, re_ = ee[b % 2]
                bqa, r_bqa = BQ[b][0][:, 0, :], BQ[b][1]
                bqb, r_bqb = BQ[b][0][:, 1, :], BQ[b][1]
                for pi in range(3):
                    pr, rp = prod[pi % 2]
                    S.op("dve", lambda e, pi=pi, pr=pr: e.tensor_tensor(out=pr[:], in0=k_[:, pi, :], in1=bqa, op=ALU.mult),
                         reads=[rk, r_bqa], writes=[rp])
                    S.op("dve", lambda e, pi=pi, pr=pr: e.tensor_reduce(out=s_[:, pi * 8:(pi + 1) * 8], in_=pr[:].rearrange("p (h d) -> p h d", d=64),
                                                                         axis=AX.X, op=ALU.add), reads=[rp], writes=[rs_])
                pr, rp = prod[1]
                kb4 = kb_[:].rearrange("p (g d) -> p g d", d=64).unsqueeze(2).to_broadcast([128, 2, 4, 64])
                S.op("dve", lambda e: e.tensor_tensor(out=pr[:].rearrange("p (g j d) -> p g j d", g=2, j=4), in0=bqb.rearrange("p (g j d) -> p g j d", g=2, j=4),
                                                      in1=kb4, op=ALU.mult), reads=[rkb, r_bqb], writes=[rp])
                S.op("dve", lambda e: e.tensor_reduce(out=s_[:, 24:32], in_=pr[:].rearrange("p (h d) -> p h d", d=64), axis=AX.X, op=ALU.add),
                     reads=[rp], writes=[rs_])
                S.op("act", lambda e: e.activation(out=e_[:], in_=s_[:], func=AF.Exp, scale=SCALE), reads=[rs_], writes=[re_])
                first = (b == 0)
                last = (b == NS - 1)
                for pi in range(3):
                    p_, rpv = pv[pvc[0] % 3]; pvc[0] += 1
                    eb = e_[:, pi * 8:(pi + 1) * 8].unsqueeze(2).to_broadcast([128, 8, 64])
                    S.op("pool", lambda e, pi=pi, p_=p_, eb=eb: e.tensor_tensor(out=p_[:].rearrange("p (h d) -> p h d", d=64),
                                                                               in0=v_[:, pi, :].rearrange("p (h d) -> p h d", d=64), in1=eb, op=ALU.mult),
                         reads=[rv, re_], writes=[rpv])
                    mm(PS[0][:16, :], selT[:, b * 16:(b + 1) * 16], p_[:], first and pi == 0, last and pi == 2, [R_sel, rpv], [RPS[0]], True)
                p_, rpv = pv[pvc[0] % 3]; pvc[0] += 1
                eb = e_[:, 24:32].unsqueeze(2).to_broadcast([128, 8, 64])
                vb4 = vb_[:].rearrange("p (g d) -> p g d", d=64).unsqueeze(2).to_broadcast([128, 2, 4, 64])
                S.op("pool", lambda e: e.tensor_tensor(out=p_[:].rearrange("p (g j d) -> p g j d", g=2, j=4), in0=e_[:, 24:32].rearrange("p (g j) -> p g j", g=2).unsqueeze(3).to_broadcast([128, 2, 4, 64]),
                                                       in1=vb4, op=ALU.mult), reads=[rvb, re_], writes=[rpv])
                mm(PS[1][:16, :], selT[:, b * 16:(b + 1) * 16], p_[:], first, last, [R_sel, rpv], [RPS[1]], True)
                mm(PS[2][:16, 0:32], selT[:, b * 16:(b + 1) * 16], e_[:], first, last, [R_sel, re_], [RPS[2]], True)

            BQ = [(sb(pb_, "s_bq%d" % i, [128, 2, 512], F32), Res()) for i in range(NS)]

            def s_bcast(b):
                bq, rbq = BQ[b]
                mm(PS[4 + 2 * (b % 2)][:, :], sel[:, b * 128:(b + 1) * 128], zs_qa[:, :], True, True, [R_sel, R_zs], [RPS[4 + 2 * (b % 2)]], True)
                mm(PS[5 + 2 * (b % 2)][:, :], sel[:, b * 128:(b + 1) * 128], zs_qb[:, :], True, True, [R_sel, R_zs], [RPS[5 + 2 * (b % 2)]], True)
                S.op("act", lambda e: e.activation(out=bq[:, 0, :], in_=PS[4 + 2 * (b % 2)][:, :], func=AF.Copy), reads=[RPS[4 + 2 * (b % 2)]], writes=[rbq])
                S.op("act", lambda e: e.activation(out=bq[:, 1, :], in_=PS[5 + 2 * (b % 2)][:, :], func=AF.Copy), reads=[RPS[5 + 2 * (b % 2)]], writes=[rbq])

            s_load(0)
            s_load(1)
            for b in range(NS):
                s_bcast(b)
            for b in range(NS):
                if b + 2 < NS:
                    s_load(b + 2)
                s_compute(b)
            S.op("dve", lambda e: e.tensor_tensor(out=tk[:, 0:512], in0=zs_qa[:], in1=zs_ka[:], op=ALU.mult), reads=[R_zs], writes=[R_tk])
            S.op("dve", lambda e: e.tensor_tensor(out=tk[:, 512:1024].rearrange("p (g j d) -> p g j d", g=2, j=4),
                                                  in0=zs_qb[:].rearrange("p (g j d) -> p g j d", g=2, j=4),
                                                  in1=zs_kb[:].rearrange("p (g d) -> p g d", d=64).unsqueeze(2).to_broadcast([16, 2, 4, 64]), op=ALU.mult),
                 reads=[R_zs], writes=[R_tk])
            S.op("dve", lambda e: e.tensor_reduce(out=snew[:], in_=tk[:].rearrange("p (h d) -> p h d", d=64), axis=AX.X, op=ALU.add),
                 reads=[R_tk], writes=[R_snew])
            S.op("act", lambda e: e.activation(out=snew[:], in_=snew[:], func=AF.Exp, scale=SCALE), reads=[R_snew], writes=[R_snew])
            S.op("dve", lambda e: e.tensor_scalar(out=snew[:, 0:8], in0=snew[:, 0:8], scalar1=3.0, scalar2=None, op0=ALU.mult),
                 reads=[R_snew], writes=[R_snew])
            S.op("dve", lambda e: e.tensor_tensor(out=den[:, 0:8], in0=PS[2][:16, 0:8], in1=snew[:, 0:8], op=ALU.add), reads=[RPS[2], R_snew], writes=[R_den])
            S.op("dve", lambda e: e.tensor_tensor(out=den[:, 0:8], in0=PS[2][:16, 8:16], in1=den[:, 0:8], op=ALU.add), reads=[RPS[2], R_den], writes=[R_den])
            S.op("dve", lambda e: e.tensor_tensor(out=den[:, 0:8], in0=PS[2][:16, 16:24], in1=den[:, 0:8], op=ALU.add), reads=[RPS[2], R_den], writes=[R_den])
            S.op("dve", lambda e: e.tensor_tensor(out=den[:, 8:16], in0=PS[2][:16, 24:32], in1=snew[:, 8:16], op=ALU.add), reads=[RPS[2], R_snew], writes=[R_den])
            S.op("dve", lambda e: e.tensor_tensor(out=den[:, 8:16], in0=den[:, 8:16], in1=esink[:16, :], op=ALU.add), reads=[R_den, R_const], writes=[R_den])
            S.op("dve", lambda e: e.reciprocal(out=den[:], in_=den[:]), reads=[R_den], writes=[R_den])
            S.op("dve", lambda e: e.tensor_tensor(out=tk[:, 0:512].rearrange("p (h d) -> p h d", d=64), in0=zs_va[:].rearrange("p (h d) -> p h d", d=64),
                                                  in1=snew[:, 0:8].unsqueeze(2).to_broadcast([16, 8, 64]), op=ALU.mult), reads=[R_zs, R_snew], writes=[R_tk])
            S.op("dve", lambda e: e.tensor_tensor(out=tk[:, 512:1024].rearrange("p (g j d) -> p g j d", g=2, j=4),
                                                  in0=zs_vb[:].rearrange("p (g d) -> p g d", d=64).unsqueeze(2).to_broadcast([16, 2, 4, 64]),
                                                  in1=snew[:, 8:16].rearrange("p (g j) -> p g j", g=2).unsqueeze(3).to_broadcast([16, 2, 4, 64]), op=ALU.mult),
                 reads=[R_zs, R_snew, R_tk], writes=[R_tk])
            S.op("dve", lambda e: e.tensor_tensor(out=osb[:, 0:512], in0=PS[0][:16, :], in1=tk[:, 0:512], op=ALU.add), reads=[RPS[0], R_tk], writes=[R_osb])
            S.op("dve", lambda e: e.tensor_tensor(out=osb[:, 512:1024], in0=PS[1][:16, :], in1=tk[:, 512:1024], op=ALU.add), reads=[RPS[1], R_tk], writes=[R_osb])
            S.op("dve", lambda e: e.tensor_tensor(out=osb[:].rearrange("p (h d) -> p h d", d=64), in0=osb[:].rearrange("p (h d) -> p h d", d=64),
                                                  in1=den[:].unsqueeze(2).to_broadcast([16, 16, 64]), op=ALU.mult), reads=[R_osb, R_den], writes=[R_osb])
            for c in range(8):
                S.op("pe", lambda e, c=c: e.transpose(PS[4][:, c * 16:(c + 1) * 16], osb[:, c * 128:(c + 1) * 128], identf[:16, :16]),
                     reads=[R_osb, R_const], writes=[RPS[4]], signal=(c == 7))
            S.op("act", lambda e: e.activation(out=OT[:, :, NO:NOS], in_=PS[4][:, 0:128].rearrange("p (c t) -> p c t", t=16), func=AF.Copy),
                 reads=[RPS[4]], writes=[R_OT])

            S.barrier()
            pb_.close()
            TT = 256
            XH = [(sb(ps, "o_x%d" % i, [128, 8, TT], F32), Res()) for i in range(2)]
            sq = sb(ps, "o_sq", [128, 8, TT], BF16); R_sq = Res()
            rs = sb(ps, "o_rs", [128, TT], F32); R_rs = Res()
            TMP = [(sb(ps, "o_tmp%d" % i, [128, TT], F32), Res()) for i in range(2)]
            otiles = [(t0, min(TT, NOS - t0)) for t0 in range(0, NOS, TT)]
            for i, (t0, T) in enumerate(otiles):
                xt, xr = XH[i % 2]
                S.dma_start("sp", xt[:, :, :T], h1T[:, NH + t0:NH + t0 + T].rearrange("(c p) t -> p c t", p=128), writes=[xr])
                for c in range(8):
                    bank, rb = PS[c // 2], RPS[c // 2]
                    off = (c % 2) * 256
                    for k in range(8):
                        mm(bank[:, off:off + T], w_out[:, k, c * 128:(c + 1) * 128], OT[:, k, t0:t0 + T], k == 0, k == 7,
                           [R_wo, R_OT], [rb], k == 7)
                for b in range(4):
                    S.op("act", lambda e, b=b: e.activation(out=sq[:, 2 * b:2 * b + 2, :T], in_=PS[b][:].rearrange("p (a t) -> p a t", a=2)[:, :, :T],
                                                            func=AF.Square), reads=[RPS[b]], writes=[R_sq])
                rstd_from_sq(sq, R_sq, T, PS[6], RPS[6], rs, R_rs)
                for c in range(8):
                    bank, rb = PS[c // 2], RPS[c // 2]
                    off = (c % 2) * 256
                    tm, rt = TMP[c % 2]
                    S.op("dve", lambda e, c=c, bank=bank, off=off, tm=tm: e.scalar_tensor_tensor(
                        out=tm[:, :T], in0=bank[:, off:off + T], scalar=gcol(G_MIXPOST, c), in1=rs[:, :T],
                        op0=ALU.mult, op1=ALU.mult), reads=[rb, R_rs, R_const], writes=[rt])
                    S.op("pool", lambda e, c=c, tm=tm, xt=xt: e.tensor_tensor(out=xt[:, c, :T], in0=xt[:, c, :T], in1=tm[:, :T], op=ALU.add),
                         reads=[rt, xr], writes=[xr])
                S.dma_start("pool", h2T[:, t0:t0 + T].rearrange("(c p) t -> p c t", p=128), xt[:, :, :T], reads=[xr])
            S.barrier()

        ffn_phase("f2", w_f2g, w_f2u, w_f2d, G_F2PRE, G_F2POST, h2T, 0, h3T, 0, NOS)

        with ExitStack() as ps:
            wpg = sb(ps, "wpg", [128, 8, D], BF16); wpp = sb(ps, "wpp", [128, 2, D], BF16); R_w = Res()
            for k in range(8):
                S.dma_start("pool", wpg[:, k, :], w_pg_d[k * 128:(k + 1) * 128, :], writes=[R_w])
            for k in range(2):
                S.dma_start("pool", wpp[:, k, :], w_pp_d[k * 128:(k + 1) * 128, :], writes=[R_w])
            TT = 256
            X = [(sb(ps, "e_x%d" % i, [128, 8, TT], F32), Res()) for i in range(2)]
            Pt = [(sb(ps, "e_p%d" % i, [128, 2, TT], BF16), Res()) for i in range(2)]
            u = sb(ps, "e_u", [128, 8, TT], BF16); R_u = Res()
            yb = sb(ps, "e_y", [128, 8, TT], F32); R_y = Res()
            sg = [(sb(ps, "e_sg%d" % i, [128, TT], F32), Res()) for i in range(2)]
            sq = sb(ps, "e_sq", [128, 8, TT], BF16); R_sq = Res()
            rs = sb(ps, "e_rs", [128, TT], F32); R_rs = Res()
            TMP = [(sb(ps, "e_tmp%d" % i, [128, TT], F32), Res()) for i in range(2)]
            etiles = [(t0, min(TT, NOS - t0)) for t0 in range(0, NOS, TT)]
            X3 = X + [(sb(ps, "e_x2", [128, 8, TT], F32), Res())]
            U2 = [(u, R_u), (sb(ps, "e_u1", [128, 8, TT], BF16), Res())]
            rs2 = sb(ps, "e_rs2", [128, TT], F32); R_rs2 = Res()

            def e_load(i):
                t0, T = etiles[i]
                xt, xr = X3[i % 3]
                pt, rp = Pt[i % 2]
                S.dma_start("sp", xt[:, :, :T], h3T[:, t0:t0 + T].rearrange("(c p) t -> p c t", p=128), writes=[xr])
                S.dma_start("pool", pt[:, :, :T], pT[:, t0:t0 + T].rearrange("(c p) t -> p c t", p=128), writes=[rp])

            def e_pre(i):
                t0, T = etiles[i]
                xt, xr = X3[i % 3]
                uu, ur = U2[i % 2]
                S.op("act", lambda e: e.activation(out=sq[:, :, :T], in_=xt[:, :, :T], func=AF.Square), reads=[xr], writes=[R_sq])
                rstd_from_sq(sq, R_sq, T, PS[6], RPS[6], rs, R_rs)
                for c in range(8):
                    S.op("dve", lambda e, c=c: e.scalar_tensor_tensor(out=uu[:, c, :T], in0=xt[:, c, :T], scalar=gcol(G_PLEPRE, c), in1=rs[:, :T],
                                                                       op0=ALU.mult, op1=ALU.mult), reads=[xr, R_rs, R_const], writes=[ur])

            def e_gate(i):
                t0, T = etiles[i]
                uu, ur = U2[i % 2]
                pt, rp = Pt[i % 2]
                for c in range(8):
                    pb, rpb = PS[c % 4], RPS[c % 4]
                    s_, rs_ = sg[c % 2]
                    for k in range(8):
                        mm(pb[:, 0:T], wpg[:, k, c * 128:(c + 1) * 128], uu[:, k, :T], k == 0, k == 7, [R_w, ur], [rpb], False)
                    for k in range(2):
                        mm(pb[:, 256:256 + T], wpp[:, k, c * 128:(c + 1) * 128], pt[:, k, :T], k == 0, k == 1, [R_w, rp], [rpb], k == 1)
                    S.op("act", lambda e, pb=pb, s_=s_: e.activation(out=s_[:, :T], in_=pb[:, 0:T], func=AF.Sigmoid), reads=[rpb], writes=[rs_])
                    S.op("dve", lambda e, c=c, pb=pb, s_=s_: e.tensor_tensor(out=yb[:, c, :T], in0=s_[:, :T], in1=pb[:, 256:256 + T], op=ALU.mult),
                         reads=[rs_, rpb], writes=[R_y])

            def e_post(i):
                t0, T = etiles[i]
                xt, xr = X3[i % 3]
                S.op("act", lambda e: e.activation(out=sq[:, :, :T], in_=yb[:, :, :T], func=AF.Square), reads=[R_y], writes=[R_sq])
                rstd_from_sq(sq, R_sq, T, PS[6], RPS[6], rs2, R_rs2)
                for c in range(8):
                    tm, rt = TMP[c % 2]
                    S.op("dve", lambda e, c=c, tm=tm: e.scalar_tensor_tensor(out=tm[:, :T], in0=yb[:, c, :T], scalar=gcol(G_PLEPOST, c), in1=rs2[:, :T],
                                                                            op0=ALU.mult, op1=ALU.mult), reads=[R_y, R_rs2, R_const], writes=[rt])
                    S.op("pool", lambda e, c=c, tm=tm, xt=xt: e.tensor_tensor(out=xt[:, c, :T], in0=xt[:, c, :T], in1=tm[:, :T], op=ALU.add),
                         reads=[rt, xr], writes=[xr])
                S.dma_start("pool", yT[:, t0:t0 + T].rearrange("(c p) t -> p c t", p=128), xt[:, :, :T], reads=[xr])

            ne = len(etiles)
            e_load(0)
            if ne > 1:
                e_load(1)
            e_pre(0)
            for i in range(ne):
                e_gate(i)
                if i + 2 < ne:
                    e_load(i + 2)
                if i + 1 < ne:
                    e_pre(i + 1)
                e_post(i)
            S.barrier(final=True)
    return nc


_NC_CACHE = {}


def _host_inputs(inp):
    f32 = np.float32
    xp = np.asarray(inp["x_prompt"], f32)
    xs = np.asarray(inp["x_sample"], f32)[:, 0, :]
    pp = np.asarray(inp["p_prompt"], f32)[0]
    psm = np.asarray(inp["p_sample"], f32)[0][:, 0, :]
    names = ["norm_f1_pre", "norm_f1_post", "norm_mix_pre", "norm_mix_post", "norm_f2_pre", "norm_f2_post",
             "norm_ple_pre", "norm_ple_post"]
    gains = np.zeros((128, 64), f32)
    for n, nm in enumerate(names):
        g = np.asarray(inp[nm], f32)[0]
        gains[:, n * 8:(n + 1) * 8] = g.reshape(8, 128).T
    ident = np.eye(128, dtype=f32)
    kk = np.arange(128)[:, None]
    qq = np.arange(128)[None, :]
    m_cur = (kk <= qq).astype(f32)
    m_next = (kk >= qq).astype(f32)
    m_own = np.concatenate([m_cur, m_next], axis=1)
    sel = np.zeros((16, 16, 128), f32)
    selT = np.zeros((128, 16, 16), f32)
    for b in range(16):
        sel[b, b, :] = 1.0
        selT[:, b, b] = 1.0
    sinks = np.broadcast_to(np.asarray(inp["sinks_b"], f32)[0][None, :], (128, 8)).copy()
    shared = {
        "gains": gains, "ident": ident, "sel": sel.reshape(16, -1), "selT": selT.reshape(128, -1), "sinks": sinks,
    }
    for nm in ["w_f1_gate", "w_f1_up", "w_f1_down", "w_f2_gate", "w_f2_up", "w_f2_down", "w_in", "w_out",
               "w_ple_gate", "w_ple_proj"]:
        shared[nm] = np.ascontiguousarray(np.asarray(inp[nm], f32)[0])
    cak = np.asarray(inp["cache_a_k"], f32)[0].reshape(128, 2048, 512)
    cav = np.asarray(inp["cache_a_v"], f32)[0].reshape(128, 2048, 512)
    cbk = np.asarray(inp["cache_b_k"], f32)[0].reshape(128, 128, 128)
    cbv = np.asarray(inp["cache_b_v"], f32)[0].reshape(128, 128, 128)
    maps = []
    for c in range(NCORES):
        bb, j = c // 4, c % 4
        s = j * NO
        xT = np.zeros((D, NT), f32)
        if j > 0:
            xT[:, 0:NH] = xp[bb, s - NH:s, :].T
        xT[:, NH:NH + NO] = xp[bb, s:s + NO, :].T
        xT[:, NH + NO:] = xs[c * NS:(c + 1) * NS, :].T
        pT = np.zeros((256, NOS), f32)
        pT[:, :NO] = pp[bb, s:s + NO, :].T
        pT[:, NO:] = psm[c * NS:(c + 1) * NS, :].T
        pos = np.concatenate([np.arange(s - NH, s + NO), np.full(128, PAST)]).astype(np.int64)
        cs, sn = _rope_tables(pos)
        cos_t = cs.reshape(33, 128, 32).transpose(1, 0, 2).reshape(128, -1)
        sin_t = sn.reshape(33, 128, 32).transpose(1, 0, 2).reshape(128, -1)
        hp = 1.0 if j > 0 else 0.0
        masks = np.concatenate([m_own, m_own * hp, m_next * hp, m_cur], axis=1).astype(f32)
        m = dict(shared)
        m.update({
            "xT": xT, "pT": pT, "cos_t": np.ascontiguousarray(cos_t), "sin_t": np.ascontiguousarray(sin_t), "masks": masks,
            "cache_a_k": np.ascontiguousarray(cak[c * NS:(c + 1) * NS]), "cache_a_v": np.ascontiguousarray(cav[c * NS:(c + 1) * NS]),
            "cache_b_k": np.ascontiguousarray(cbk[c * NS:(c + 1) * NS]), "cache_b_v": np.ascontiguousarray(cbv[c * NS:(c + 1) * NS]),
        })
        maps.append(m)
    return maps


def kernel(**inputs):
    if "nc" not in _NC_CACHE:
        _NC_CACHE["nc"] = build_program()
    nc = _NC_CACHE["nc"]
    maps = _host_inputs(inputs)
    res = run_bass_kernel_spmd(nc, maps, core_ids=list(range(NCORES)))
    R = res.results
    f32 = np.float32
    y_prompt = np.zeros((2, 8192, D), f32)
    y_sample = np.zeros((128, 1, D), f32)
    nak_p = np.zeros((1, 2, 2048, 8, 64), f32); nav_p = np.zeros((1, 2, 2048, 8, 64), f32)
    nbk_p = np.zeros((1, 2, 128, 2, 64), f32); nbv_p = np.zeros((1, 2, 128, 2, 64), f32)
    nak_s = np.zeros((1, 128, 2048, 8, 64), f32); nav_s = np.zeros((1, 128, 2048, 8, 64), f32)
    nbk_s = np.zeros((1, 128, 128, 2, 64), f32); nbv_s = np.zeros((1, 128, 128, 2, 64), f32)
    for c in range(NCORES):
        bb, j = c // 4, c % 4
        r = R[c]
        yT = np.asarray(r["yT"], f32)
        y_prompt[bb, j * NO:(j + 1) * NO, :] = yT[:, :NO].T
        y_sample[c * NS:(c + 1) * NS, 0, :] = yT[:, NO:].T
        if j == 3:
            nak_p[0, bb] = np.asarray(r["ka_o"], f32).reshape(2048, 8, 64)
            nav_p[0, bb] = np.asarray(r["va_o"], f32).reshape(2048, 8, 64)
            nbk_p[0, bb] = np.asarray(r["kb_o"], f32)[NO - 128:].reshape(128, 2, 64)
            nbv_p[0, bb] = np.asarray(r["vb_o"], f32)[NO - 128:].reshape(128, 2, 64)
        nak_s[0, c * NS:(c + 1) * NS] = np.asarray(r["nak_s"], f32).reshape(NS, 2048, 8, 64)
        nav_s[0, c * NS:(c + 1) * NS] = np.asarray(r["nav_s"], f32).reshape(NS, 2048, 8, 64)
        nbk_s[0, c * NS:(c + 1) * NS] = np.asarray(r["nbk_s"], f32).reshape(NS, 128, 2, 64)
        nbv_s[0, c * NS:(c + 1) * NS] = np.asarray(r["nbv_s"], f32).reshape(NS, 128, 2, 64)
    return (y_prompt, y_sample, nak_p, nav_p, nbk_p, nbv_p, nak_s, nav_s, nbk_s, nbv_s)
```

```python
import numpy as np
import concourse.bass as bass
import concourse.mybir as mybir
from concourse.bass_utils import run_bass_kernel_spmd
from contextlib import ExitStack

F32, BF16 = mybir.dt.float32, mybir.dt.bfloat16
AF = mybir.ActivationFunctionType
ALU = mybir.AluOpType
AX = mybir.AxisListType

D = 1024
DFF = 2816
FC = 22
NH = 2048
NO = 2048
NS = 16
NOS = NO + NS
NT = NH + NOS
INW = 2304
EPS = 1e-6
SCALE = 0.125
PAST = 16384
NCORES = 8
G_F1PRE, G_F1POST, G_MIXPRE, G_MIXPOST, G_F2PRE, G_F2POST, G_PLEPRE, G_PLEPOST = range(8)

DEBUG = False


class Res:
    __slots__ = ("w", "r")

    def __init__(self):
        self.w = {}
        self.r = {}


class Sched:
    def __init__(self, nc, es, n_dma_sems=40):
        self.nc = nc
        self.sems = []
        self.E = {}
        for name, eng in (("pe", nc.tensor), ("act", nc.scalar), ("dve", nc.vector),
                          ("pool", nc.gpsimd), ("sp", nc.sync)):
            sem = es.enter_context(nc.semaphore("s_" + name))
            self.sems.append(sem)
            self.E[name] = dict(eng=eng, key=len(self.sems) - 1, cnt=0, waited={}, name=name)
        self.dma = []
        for i in range(n_dma_sems):
            sem = es.enter_context(nc.semaphore("s_dma%d" % i))
            self.sems.append(sem)
            self.dma.append(dict(key=len(self.sems) - 1, cnt=0))
        self.dma_rr = 0
        self.big = dict(key=None, cnt=0)
        sem = es.enter_context(nc.semaphore("s_big"))
        self.sems.append(sem)
        self.big["key"] = len(self.sems) - 1

    def _wait(self, e, deps):
        for k, v in deps.items():
            if v <= 0 or e["waited"].get(k, 0) >= v:
                continue
            e["eng"].wait_ge(self.sems[k], v)
            e["waited"][k] = v

    def _deps(self, e, reads, writes):
        deps = {}
        own = e["key"]
        for r in reads:
            for k, v in r.w.items():
                if k == own and e["name"] == "pe":
                    continue
                if deps.get(k, 0) < v:
                    deps[k] = v
        for w in writes:
            for k, v in list(w.w.items()) + list(w.r.items()):
                if k == own:
                    continue
                if deps.get(k, 0) < v:
                    deps[k] = v
        return deps

    def op(self, ename, fn, reads=(), writes=(), signal=True):
        e = self.E[ename]
        self._wait(e, self._deps(e, reads, writes))
        ins = fn(e["eng"])
        if signal:
            e["cnt"] += 1
            ins.then_inc(self.sems[e["key"]], 1)
            val = e["cnt"]
        else:
            val = e["cnt"] + 1
        k = e["key"]
        for r in reads:
            if r.r.get(k, 0) < val:
                r.r[k] = val
        for w in writes:
            w.w[k] = val
            w.r = {}
        return ins

    def dma_start(self, qname, out, in_, reads=(), writes=(), big=False):
        q = self.E[qname]
        deps = self._deps(q, reads, writes)
        if big:
            s = self.big
        else:
            s = self.dma[self.dma_rr]
            self.dma_rr = (self.dma_rr + 1) % len(self.dma)
            if s["cnt"] > 0:
                deps[s["key"]] = max(deps.get(s["key"], 0), s["cnt"])
        self._wait(q, deps)
        ins = q["eng"].dma_start(out=out, in_=in_)
        s["cnt"] += 16
        ins.then_inc(self.sems[s["key"]], 16)
        k = s["key"]
        for r in reads:
            if r.r.get(k, 0) < s["cnt"]:
                r.r[k] = s["cnt"]
        for w in writes:
            w.w[k] = s["cnt"]
            w.r = {}
        return ins

    def barrier(self, final=False):
        tot = {}
        for e in self.E.values():
            tot[e["key"]] = e["cnt"]
        for s in self.dma + ([self.big] if final else []):
            tot[s["key"]] = s["cnt"]
        for e in self.E.values():
            d = dict(tot)
            d.pop(e["key"], None)
            self._wait(e, d)


def _rope_tables(pos):
    inv = np.power(np.float32(10000.0), -np.arange(32, dtype=np.float32) * np.float32(2.0) / np.float32(64.0)).astype(np.float32)
    ang = pos.astype(np.float32)[:, None] * inv[None, :]
    return np.cos(ang).astype(np.float32), np.sin(ang).astype(np.float32)


def build_program():
    nc = bass.Bass("TRN2", target_bir_lowering=False)

    def din(name, shape, dt=F32):
        return nc.dram_tensor(name, list(shape), dt, kind="ExternalInput").ap()

    def dout(name, shape, dt=F32):
        return nc.dram_tensor(name, list(shape), dt, kind="ExternalOutput").ap()

    def dscr(name, shape, dt):
        kind = "ExternalOutput" if DEBUG else "Internal"
        return nc.dram_tensor(name, list(shape), dt, kind=kind).ap()

    xT = din("xT", [D, NT])
    pT = din("pT", [256, NOS])
    gains_d = din("gains", [128, 64])
    cos_d = din("cos_t", [128, 33 * 32])
    sin_d = din("sin_t", [128, 33 * 32])
    mask_d = din("masks", [128, 768])
    ident_d = din("ident", [128, 128])
    sel_d = din("sel", [16, 16 * 128])
    selT_d = din("selT", [128, 16 * 16])
    sinks_d = din("sinks", [128, 8])
    w_f1g = din("w_f1_gate", [D, DFF]); w_f1u = din("w_f1_up", [D, DFF]); w_f1d = din("w_f1_down", [DFF, D])
    w_f2g = din("w_f2_gate", [D, DFF]); w_f2u = din("w_f2_up", [D, DFF]); w_f2d = din("w_f2_down", [DFF, D])
    w_in_d = din("w_in", [D, INW]); w_out_d = din("w_out", [D, D])
    w_pg_d = din("w_ple_gate", [D, D]); w_pp_d = din("w_ple_proj", [256, D])
    cak = din("cache_a_k", [NS, 2048, 512]); cav = din("cache_a_v", [NS, 2048, 512])
    cbk = din("cache_b_k", [NS, 128, 128]); cbv = din("cache_b_v", [NS, 128, 128])

    yT = dout("yT", [D, NOS])
    ka_o = dout("ka_o", [NO, 512]); va_o = dout("va_o", [NO, 512])
    kb_o = dout("kb_o", [NO, 128]); vb_o = dout("vb_o", [NO, 128])
    nak_s = dout("nak_s", [NS, 2048, 512]); nav_s = dout("nav_s", [NS, 2048, 512])
    nbk_s = dout("nbk_s", [NS, 128, 128]); nbv_s = dout("nbv_s", [NS, 128, 128])

    h1T = dscr("h1T", [D, NT], F32)
    h2T = dscr("h2T", [D, NOS], F32)
    h3T = dscr("h3T", [D, NOS], F32)
    QTs = dscr("QTs", [8, 128, NO], BF16)
    KaTs = dscr("KaTs", [4, 128, NH + NO], BF16)
    KbTs = dscr("KbTs", [2, 128, NH + NO], BF16)
    Va_s = dscr("Va_s", [4, NH + NO, 192], BF16)
    Vb_s = dscr("Vb_s", [2, NH + NO, 192], BF16)

    with ExitStack() as es:
        S = Sched(nc, es)

        def sb(stack, name, shape, dt):
            return stack.enter_context(nc.sbuf_tensor("sb_" + name, list(shape), dt))

        ones = sb(es, "ones", [128, 128], BF16)
        ident = sb(es, "ident", [128, 128], BF16)
        identf = sb(es, "identf", [128, 128], F32)
        gains = sb(es, "gains", [128, 64], F32)
        gains_h = sb(es, "gains_h", [128, 64], F32)
        masks = sb(es, "masks", [128, 768], BF16)
        esink = sb(es, "esink", [128, 8], F32)
        zs_qa = sb(es, "zs_qa", [16, 512], F32); zs_ka = sb(es, "zs_ka", [16, 512], F32)
        zs_va = sb(es, "zs_va", [16, 512], F32); zs_qb = sb(es, "zs_qb", [16, 512], F32)
        zs_kb = sb(es, "zs_kb", [16, 128], F32); zs_vb = sb(es, "zs_vb", [16, 128], F32)
        R_const = Res(); R_zs = Res(); R_ostok = Res()
        PS = []
        RPS = []
        for i in range(8):
            PS.append(es.enter_context(nc.psum_tensor("ps%d" % i, [128, 512], F32)))
            RPS.append(Res())

        S.op("dve", lambda e: e.memset(ones[:], 1.0), writes=[R_const])
        S.dma_start("sp", gains[:], gains_d, writes=[R_const])
        S.dma_start("pool", masks[:], mask_d, writes=[R_const])
        S.dma_start("pool", ident[:], ident_d, writes=[R_const])
        S.dma_start("sp", identf[:], ident_d, writes=[R_const])
        S.dma_start("sp", esink[:], sinks_d, writes=[R_const])
        S.op("dve", lambda e: e.tensor_scalar(out=gains_h[:], in0=gains[:], scalar1=0.5, scalar2=None, op0=ALU.mult),
             reads=[R_const], writes=[R_const])
        S.op("act", lambda e: e.activation(out=esink[:], in_=esink[:], func=AF.Exp), reads=[R_const], writes=[R_const])

        R_cache_out = Res()

        def flat16(ap):
            return ap.rearrange("r c -> (r c)").rearrange("(a b x) -> a b x", a=16, b=32)
        bg_dmas = []
        for b in range(NS):
            bg_dmas.append(lambda b=b: S.dma_start("sp", flat16(nak_s[b, 0:2047, :]), flat16(cak[b, 1:2048, :]), writes=[R_cache_out], big=True))
            bg_dmas.append(lambda b=b: S.dma_start("sp", flat16(nav_s[b, 0:2047, :]), flat16(cav[b, 1:2048, :]), writes=[R_cache_out], big=True))
        bg_dmas.append(lambda: S.dma_start("sp", nbk_s[:, 0:127, :], cbk[:, 1:128, :], writes=[R_cache_out], big=True))
        bg_dmas.append(lambda: S.dma_start("sp", nbv_s[:, 0:127, :], cbv[:, 1:128, :], writes=[R_cache_out], big=True))

        def mm(out, lhsT, rhs, start, stop, reads, writes, signal):
            return S.op("pe", lambda e: e.matmul(out, lhsT, rhs, start=start, stop=stop),
                        reads=reads, writes=writes, signal=signal)

        def gcol(n, c, half=False):
            t = gains_h if half else gains
            return t[:, n * 8 + c:n * 8 + c + 1]

        def rstd_from_sq(sq, R_sq, T, psn, R_psn, rstd, R_rstd):
            for c in range(8):
                mm(psn[:, :T], ones[:], sq[:, c, :T], c == 0, c == 7, [R_sq, R_const], [R_psn], c == 7)
            S.op("dve", lambda e: e.tensor_scalar(out=rstd[:, :T], in0=psn[:, :T], scalar1=1.0 / D, scalar2=EPS,
                                                  op0=ALU.mult, op1=ALU.add), reads=[R_psn], writes=[R_rstd])
            S.op("act", lambda e: e.activation(out=rstd[:, :T], in_=rstd[:, :T], func=AF.Sqrt), reads=[R_rstd], writes=[R_rstd])
            S.op("dve", lambda e: e.reciprocal(out=rstd[:, :T], in_=rstd[:, :T]), reads=[R_rstd], writes=[R_rstd])

        def ffn_phase(tag, wg_d, wu_d, wd_d, g_pre, g_post, src, src_off, dst, dst_off, ncols, bg_per_tile=0):
            TT = 256
            tiles = [(t0, min(TT, ncols - t0)) for t0 in range(0, ncols, TT)]
            with ExitStack() as ps:
                wg = sb(ps, tag + "wg", [128, 8, DFF], BF16)
                wu = sb(ps, tag + "wu", [128, 8, DFF], BF16)
                wd = sb(ps, tag + "wd", [128, FC, D], BF16)
                FG = [(0, 2), (2, 6), (6, 12), (12, 22)]
                R_wgu = [Res() for _ in FG]
                R_wd = [Res() for _ in FG]
                fgrp = {}
                for gi, (f0, f1) in enumerate(FG):
                    for f in range(f0, f1):
                        fgrp[f] = gi

                def load_weights():
                    for gi, (f0, f1) in enumerate(FG):
                        c0, c1 = f0 * 128, f1 * 128
                        S.dma_start("pool", wg[:, :, c0:c1], wg_d[:, c0:c1].rearrange("(k p) c -> p k c", p=128), writes=[R_wgu[gi]])
                        S.dma_start("pool", wu[:, :, c0:c1], wu_d[:, c0:c1].rearrange("(k p) c -> p k c", p=128), writes=[R_wgu[gi]])
                    for gi, (f0, f1) in enumerate(FG):
                        S.dma_start("pool", wd[:, f0:f1, :],
                                    wd_d[f0 * 128:f1 * 128, :].rearrange("(f p) c -> p f c", p=128), writes=[R_wd[gi]])
                X = [(sb(ps, tag + "x%d" % i, [128, 8, TT], F32), Res()) for i in range(3)]
                U = [(sb(ps, tag + "u%d" % i, [128, 8, TT], BF16), Res()) for i in range(2)]
                hid = sb(ps, tag + "hid", [128, FC, TT], BF16); R_hid = Res()
                sq = sb(ps, tag + "sq", [128, 8, TT], BF16); R_sq = Res()
                RS = [(sb(ps, tag + "rs%d" % i, [128, TT], F32), Res()) for i in range(2)]
                SG = [(sb(ps, tag + "sg%d" % i, [128, TT], F32), Res()) for i in range(3)]
                TMP = [(sb(ps, tag + "tmp%d" % i, [128, TT], F32), Res()) for i in range(2)]
                psn, R_psn = PS[6], RPS[6]

                def load(i):
                    t0, T = tiles[i]
                    xt, xr = X[i % 3]
                    S.dma_start("sp", xt[:, :, :T],
                                src[:, src_off + t0:src_off + t0 + T].rearrange("(c p) t -> p c t", p=128), writes=[xr])

                GUB = [4, 5, 7]

                def prenorm_steps(i):
                    t0, T = tiles[i]
                    xt, xr = X[i % 3]
                    u, ur = U[i % 2]
                    rs, rr = RS[0]

                    def a():
                        S.op("act", lambda e: e.activation(out=sq[:, :, :T], in_=xt[:, :, :T], func=AF.Square),
                             reads=[xr], writes=[R_sq])

                    def b():
                        for c in range(8):
                            mm(psn[:, :T], ones[:], sq[:, c, :T], c == 0, c == 7, [R_sq, R_const], [R_psn], c == 7)
                        S.op("dve", lambda e: e.tensor_scalar(out=rs[:, :T], in0=psn[:, :T], scalar1=1.0 / D, scalar2=EPS,
                                                              op0=ALU.mult, op1=ALU.add), reads=[R_psn], writes=[rr])
                        S.op("act", lambda e: e.activation(out=rs[:, :T], in_=rs[:, :T], func=AF.Sqrt), reads=[rr], writes=[rr])

                    def c_():
                        S.op("dve", lambda e: e.reciprocal(out=rs[:, :T], in_=rs[:, :T]), reads=[rr], writes=[rr])

                    def d(cs):
                        for c in cs:
                            S.op("dve", lambda e, c=c: e.scalar_tensor_tensor(
                                out=u[:, c, :T], in0=xt[:, c, :T], scalar=gcol(g_pre, c), in1=rs[:, :T],
                                op0=ALU.mult, op1=ALU.mult), reads=[xr, rr, R_const], writes=[ur])

                    return [(14, a), (16, b), (18, c_), (19, lambda: d(range(0, 3))), (20, lambda: d(range(3, 6))), (21, lambda: d(range(6, 8)))]

                def gateup(i, hooks):
                    t0, T = tiles[i]
                    u, ur = U[i % 2]
                    for f in range(FC):
                        for fn in hooks.get(f, []):
                            fn()
                        bkn = GUB[f % 3]
                        pb, rpb = PS[bkn], RPS[bkn]
                        sg, rsg = SG[f % 3]
                        for k in range(8):
                            mm(pb[:, 0:T], wg[:, k, f * 128:(f + 1) * 128], u[:, k, :T], k == 0, k == 7,
                               [R_wgu[fgrp[f]], ur], [rpb], False)
                        for k in range(8):
                            mm(pb[:, 256:256 + T], wu[:, k, f * 128:(f + 1) * 128], u[:, k, :T], k == 0, k == 7,
                               [R_wgu[fgrp[f]], ur], [rpb], k == 7)
                        S.op("act", lambda e: e.activation(out=sg[:, :T], in_=pb[:, 0:T], func=AF.Silu),
                             reads=[rpb], writes=[rsg])
                        S.op("dve", lambda e, f=f: e.tensor_tensor(out=hid[:, f, :T], in0=sg[:, :T], in1=pb[:, 256:256 + T],
                                                                    op=ALU.mult), reads=[rsg, rpb], writes=[R_hid])

                def down_mm(i):
                    t0, T = tiles[i]
                    for c in range(8):
                        bank, rb = PS[c // 2], RPS[c // 2]
                        off = (c % 2) * 256
                        for f in range(FC):
                            mm(bank[:, off:off + T], wd[:, f, c * 128:(c + 1) * 128], hid[:, f, :T], f == 0, f == FC - 1,
                               [R_wd[fgrp[f]], R_hid], [rb], f == FC - 1)
                    for b in range(4):
                        S.op("act", lambda e, b=b: e.activation(
                            out=sq[:, 2 * b:2 * b + 2, :T], in_=PS[b][:].rearrange("p (a t) -> p a t", a=2)[:, :, :T],
                            func=AF.Square), reads=[RPS[b]], writes=[R_sq])

                def post_steps(i):
                    t0, T = tiles[i]
                    xt, xr = X[i % 3]
                    rs, rr = RS[1]

                    def a():
                        for c in range(8):
                            mm(psn[:, :T], ones[:], sq[:, c, :T], c == 0, c == 7, [R_sq, R_const], [R_psn], c == 7)
                        S.op("dve", lambda e: e.tensor_scalar(out=rs[:, :T], in0=psn[:, :T], scalar1=1.0 / D, scalar2=EPS,
                                                              op0=ALU.mult, op1=ALU.add), reads=[R_psn], writes=[rr])
                        S.op("act", lambda e: e.activation(out=rs[:, :T], in_=rs[:, :T], func=AF.Sqrt), reads=[rr], writes=[rr])

                    def b():
                        S.op("dve", lambda e: e.reciprocal(out=rs[:, :T], in_=rs[:, :T]), reads=[rr], writes=[rr])

                    def cstep(c):
                        bank, rb = PS[c // 2], RPS[c // 2]
                        off = (c % 2) * 256
                        tm, rt = TMP[c % 2]
                        S.op("dve", lambda e: e.scalar_tensor_tensor(
                            out=tm[:, :T], in0=bank[:, off:off + T], scalar=gcol(g_post, c, True), in1=rs[:, :T],
                            op0=ALU.mult, op1=ALU.mult), reads=[rb, rr, R_const], writes=[rt])
                        S.op("pool", lambda e: e.tensor_tensor(out=xt[:, c, :T], in0=xt[:, c, :T], in1=tm[:, :T],
                                                               op=ALU.add), reads=[rt, xr], writes=[xr])

                    def st():
                        S.dma_start("pool", dst[:, dst_off + t0:dst_off + t0 + T].rearrange("(c p) t -> p c t", p=128),
                                    xt[:, :, :T], reads=[xr])

                    steps = [(2, a), (4, b)]
                    for c in range(8):
                        steps.append((5 + c, (lambda c=c: cstep(c))))
                    steps.append((13, st))
                    return steps

                n = len(tiles)
                load(0)
                load_weights()
                if n > 1:
                    load(1)
                for (_, fn) in prenorm_steps(0):
                    fn()
                for i in range(n):
                    hooks = {}
                    if i >= 1:
                        for (f, fn) in post_steps(i - 1):
                            hooks.setdefault(f, []).append(fn)
                    if i + 1 < n:
                        for (f, fn) in prenorm_steps(i + 1):
                            hooks.setdefault(f, []).append(fn)
                    gateup(i, hooks)
                    if i + 2 < n:
                        load(i + 2)
                    for _ in range(bg_per_tile):
                        if bg_dmas:
                            bg_dmas.pop(0)()
                    down_mm(i)
                for (_, fn) in post_steps(n - 1):
                    fn()
                S.barrier()

        ffn_phase("f1", w_f1g, w_f1u, w_f1d, G_F1PRE, G_F1POST, xT, 0, h1T, 0, NT, bg_per_tile=2)
        while bg_dmas:
            bg_dmas.pop(0)()

        with ExitStack() as ps:
            w_in = sb(ps, "w_in", [128, 8, INW], BF16); R_w = Res()
            for k in range(8):
                S.dma_start("pool", w_in[:, k, :], w_in_d[k * 128:(k + 1) * 128, :], writes=[R_w])
            cosT = sb(ps, "cosT", [128, 33, 32], F32); sinT = sb(ps, "sinT", [128, 33, 32], F32); R_tab = Res()
            S.dma_start("sp", cosT[:].rearrange("p a b -> p (a b)"), cos_d, writes=[R_tab])
            S.dma_start("sp", sinT[:].rearrange("p a b -> p (a b)"), sin_d, writes=[R_tab])
            ST = 512
            X = [(sb(ps, "p2x%d" % i, [128, 8, ST], F32), Res()) for i in range(2)]
            U = [(sb(ps, "p2u%d" % i, [128, 8, ST], BF16), Res()) for i in range(2)]
            sq = sb(ps, "p2sq", [128, 8, ST], BF16); R_sq = Res()
            rs = sb(ps, "p2rs", [128, ST], F32); R_rs = Res()
            NB = 2
            NT4 = 4
            T4 = [[(sb(ps, "p2t%d_%d" % (i, j), [128, 8, 32], F32), Res()) for j in range(4)] for i in range(NT4)]
            t4_ctr = [0]
            ka_f = [(sb(ps, "ka_f%d" % i, [128, 512], F32), Res()) for i in range(NB)]
            va_f = [(sb(ps, "va_f%d" % i, [128, 512], F32), Res()) for i in range(NB)]
            kvb_f = [(sb(ps, "kvb_f%d" % i, [128, 256], F32), Res()) for i in range(NB)]
            q_b = [(sb(ps, "q_b%d" % i, [128, 1024], BF16), Res()) for i in range(NB)]
            ka_b = [(sb(ps, "ka_b%d" % i, [128, 512], BF16), Res()) for i in range(NB)]
            kbdup = [(sb(ps, "kbdup%d" % i, [128, 2, 2, 64], BF16), Res()) for i in range(NB)]
            vaug = [(sb(ps, "vaug%d" % i, [128, 4, 192], BF16), Res()) for i in range(NB)]
            vbaug = [(sb(ps, "vbaug%d" % i, [128, 2, 192], BF16), Res()) for i in range(NB)]
            qT_st = [(sb(ps, "qT_st%d" % i, [128, 8, ST], BF16), Res()) for i in range(2)]
            kaT_st = [(sb(ps, "kaT_st%d" % i, [128, 4, ST], BF16), Res()) for i in range(2)]
            kbT_st = [(sb(ps, "kbT_st%d" % i, [128, 2, ST], BF16), Res()) for i in range(2)]
            TA = PS[6][:].bitcast(BF16); R_TA = RPS[6]
            TB = PS[7][:].bitcast(BF16); R_TB = RPS[7]
            psn, R_psn = PS[5], RPS[5]
            for i in range(NB):
                S.op("dve", lambda e, i=i: e.memset(vaug[i][0][:], 1.0), writes=[vaug[i][1]])
                S.op("dve", lambda e, i=i: e.memset(vbaug[i][0][:], 1.0), writes=[vbaug[i][1]])

            stiles = [(t0, min(ST, NT - t0)) for t0 in range(0, NT, ST)]

            def p2_load(i):
                t0, T = stiles[i]
                xt, xr = X[i % 2]
                S.dma_start("sp", xt[:, :, :T], h1T[:, t0:t0 + T].rearrange("(c p) t -> p c t", p=128), writes=[xr])

            def p2_norm_steps(i):
                t0, T = stiles[i]
                xt, xr = X[i % 2]
                u, ur = U[i % 2]

                def a():
                    S.op("act", lambda e: e.activation(out=sq[:, :, :T], in_=xt[:, :, :T], func=AF.Square),
                         reads=[xr], writes=[R_sq])

                def b():
                    rstd_from_sq(sq, R_sq, T, psn, R_psn, rs, R_rs)

                def d(cs):
                    for c in cs:
                        S.op("dve", lambda e, c=c: e.scalar_tensor_tensor(
                            out=u[:, c, :T], in0=xt[:, c, :T], scalar=gcol(G_MIXPRE, c), in1=rs[:, :T],
                            op0=ALU.mult, op1=ALU.mult), reads=[xr, R_rs, R_const], writes=[ur])

                return {0: [a], 1: [b], 2: [lambda: d(range(0, 4))], 3: [lambda: d(range(4, 8))]}

            def rope(src, H, np_, ti, dst1, dst2, reads, wres, bi):
                t4 = T4[t4_ctr[0] % NT4]
                t4_ctr[0] += 1
                cb = cosT[:np_, ti, :].unsqueeze(1).to_broadcast([np_, H, 32])
                sn = sinT[:np_, ti, :].unsqueeze(1).to_broadcast([np_, H, 32])
                x1 = src[:, :, 0:32]
                x2 = src[:, :, 32:64]
                for j, (a, b_) in enumerate(((x1, cb), (x2, sn), (x2, cb), (x1, sn))):
                    S.op("dve", lambda e, j=j, a=a, b_=b_: e.tensor_tensor(out=t4[j][0][:np_, :H, :], in0=a, in1=b_, op=ALU.mult),
                         reads=reads + [R_tab], writes=[t4[j][1]])
                S.op("pool", lambda e: e.tensor_tensor(out=dst1, in0=t4[0][0][:np_, :H, :], in1=t4[1][0][:np_, :H, :],
                                                       op=ALU.subtract), reads=[t4[0][1], t4[1][1]], writes=[wres])
                S.op("pool", lambda e: e.tensor_tensor(out=dst2, in0=t4[2][0][:np_, :H, :], in1=t4[3][0][:np_, :H, :],
                                                       op=ALU.add), reads=[t4[2][1], t4[3][1]], writes=[wres])

            def v3(ap, H):
                return ap.rearrange("p (h d) -> p h d", d=64)

            def p2_info(i, j):
                t0, T = stiles[i]
                c0 = j * 128
                np_ = min(128, T - c0)
                g0 = t0 + c0
                return dict(i=i, j=j, t0=t0, T=T, c0=c0, np_=np_, g0=g0, is_sample=g0 >= NH + NO, is_halo=g0 < NH, ti=g0 // 128)

            def p2_mm(sd):
                i, c0, np_ = sd["i"], sd["c0"], sd["np_"]
                u, ur = U[i % 2]
                slices = [(512, 512, 1), (2048, 256, 4), (1024, 512, 2), (0, 512, 0), (1536, 512, 3)]
                for (s0, w, bk) in slices:
                    if sd["is_halo"] and bk in (0, 3):
                        continue
                    for k in range(8):
                        mm(PS[bk][:np_, :w], u[:, k, c0:c0 + np_], w_in[:, k, s0:s0 + w], k == 0, k == 7,
                           [ur, R_w], [RPS[bk]], k == 7)

            def p2_post_a(sd, bi):
                i, c0, np_, g0, ti = sd["i"], sd["c0"], sd["np_"], sd["g0"], sd["ti"]
                is_halo = sd["is_halo"]
                if sd["is_sample"]:
                    rope(v3(PS[1][:np_, :], 8), 8, np_, ti, v3(zs_ka[:np_, :], 8)[:, :, 0:32], v3(zs_ka[:np_, :], 8)[:, :, 32:64], [RPS[1]], R_zs, bi)
                    rope(v3(PS[4][:np_, 0:128], 2), 2, np_, ti, v3(zs_kb[:np_, :], 2)[:, :, 0:32], v3(zs_kb[:np_, :], 2)[:, :, 32:64], [RPS[4]], R_zs, bi)
                    rope(v3(PS[0][:np_, :], 8), 8, np_, ti, v3(zs_qa[:np_, :], 8)[:, :, 0:32], v3(zs_qa[:np_, :], 8)[:, :, 32:64], [RPS[0]], R_zs, bi)
                    rope(v3(PS[3][:np_, :], 8), 8, np_, ti, v3(zs_qb[:np_, :], 8)[:, :, 0:32], v3(zs_qb[:np_, :], 8)[:, :, 32:64], [RPS[3]], R_zs, bi)
                    S.op("act", lambda e: e.activation(out=zs_va[:np_, :], in_=PS[2][:np_, :], func=AF.Copy), reads=[RPS[2]], writes=[R_zs])
                    S.op("act", lambda e: e.activation(out=zs_vb[:np_, :], in_=PS[4][:np_, 128:256], func=AF.Copy), reads=[RPS[4]], writes=[R_zs])
                    S.dma_start("pool", nak_s[:, 2047, :], zs_ka[:np_, :], reads=[R_zs], writes=[R_cache_out])
                    S.dma_start("pool", nav_s[:, 2047, :], zs_va[:np_, :], reads=[R_zs], writes=[R_cache_out])
                    S.dma_start("pool", nbk_s[:, 127, :], zs_kb[:np_, :], reads=[R_zs], writes=[R_cache_out])
                    S.dma_start("pool", nbv_s[:, 127, :], zs_vb[:np_, :], reads=[R_zs], writes=[R_cache_out])
                    return
                kaf, r_kaf = ka_f[bi]; vaf, r_vaf = va_f[bi]; kvf, r_kvf = kvb_f[bi]
                qb_, r_qb = q_b[bi]; kab, r_kab = ka_b[bi]; kbd, r_kbd = kbdup[bi]
                vg, r_vg = vaug[bi]; vbg, r_vbg = vbaug[bi]
                rope(v3(PS[1][:, :], 8), 8, 128, ti, v3(kaf[:], 8)[:, :, 0:32], v3(kaf[:], 8)[:, :, 32:64], [RPS[1]], r_kaf, bi)
                S.op("act", lambda e: e.activation(out=kab[:], in_=kaf[:], func=AF.Copy), reads=[r_kaf], writes=[r_kab])
                rope(v3(PS[4][:, 0:128], 2), 2, 128, ti, v3(kvf[:, 0:128], 2)[:, :, 0:32], v3(kvf[:, 0:128], 2)[:, :, 32:64], [RPS[4]], r_kvf, bi)
                S.op("act", lambda e: e.activation(out=kvf[:, 128:256], in_=PS[4][:, 128:256], func=AF.Copy), reads=[RPS[4]], writes=[r_kvf])
                for dd in range(2):
                    S.op("act", lambda e, dd=dd: e.activation(out=kbd[:, :, dd, :], in_=v3(kvf[:, 0:128], 2), func=AF.Copy),
                         reads=[r_kvf], writes=[r_kbd])
                vb4 = vbg[:].rearrange("p a (b c) -> p a b c", c=64)
                for dd in (0, 2):
                    S.op("act", lambda e, dd=dd: e.activation(out=vb4[:, :, dd, :], in_=v3(PS[4][:, 128:256], 2), func=AF.Copy),
                         reads=[RPS[4]], writes=[r_vbg])
                va4 = vg[:].rearrange("p a (b c) -> p a b c", c=64)
                S.op("act", lambda e: e.activation(out=va4[:, :, 0:3:2, :], in_=PS[2][:, :].rearrange("p (a b c) -> p a b c", b=2, c=64),
                                                   func=AF.Copy), reads=[RPS[2]], writes=[r_vg])
                if not is_halo:
                    S.op("act", lambda e: e.activation(out=vaf[:], in_=PS[2][:, :], func=AF.Copy), reads=[RPS[2]], writes=[r_vaf])
                    rope(v3(PS[0][:, :], 8), 8, 128, ti, v3(qb_[:, 0:512], 8)[:, :, 0:32], v3(qb_[:, 0:512], 8)[:, :, 32:64], [RPS[0]], r_qb, bi)
                    rope(v3(PS[3][:, :], 8), 8, 128, ti, v3(qb_[:, 512:1024], 8)[:, :, 0:32], v3(qb_[:, 512:1024], 8)[:, :, 32:64], [RPS[3]], r_qb, bi)

            def p2_post_b(sd, bi):
                i, c0, g0 = sd["i"], sd["c0"], sd["g0"]
                is_halo = sd["is_halo"]
                if sd["is_sample"]:
                    return
                kaf, r_kaf = ka_f[bi]; vaf, r_vaf = va_f[bi]; kvf, r_kvf = kvb_f[bi]
                qb_, r_qb = q_b[bi]; kab, r_kab = ka_b[bi]; kbd, r_kbd = kbdup[bi]
                vg, r_vg = vaug[bi]; vbg, r_vbg = vbaug[bi]
                S.dma_start("sp", Va_s[:, g0:g0 + 128, :].rearrange("a t c -> t a c"), vg[:], reads=[r_vg])
                S.dma_start("sp", Vb_s[:, g0:g0 + 128, :].rearrange("a t c -> t a c"), vbg[:], reads=[r_vbg])
                for c in range(4):
                    S.op("pe", lambda e, c=c: e.transpose(TB[:, c * 128:(c + 1) * 128], kab[:, c * 128:(c + 1) * 128], ident[:]),
                         reads=[r_kab, R_const], writes=[R_TB], signal=False)
                for g in range(2):
                    S.op("pe", lambda e, g=g: e.transpose(TB[:, (4 + g) * 128:(5 + g) * 128],
                                                          kbd[:, g, :, :].rearrange("p a b -> p (a b)"), ident[:]),
                         reads=[r_kbd, R_const], writes=[R_TB], signal=(g == 1))
                kst, r_kst = kaT_st[i % 2]
                bst, r_bst = kbT_st[i % 2]
                S.op("act", lambda e: e.activation(out=kst[:, :, c0:c0 + 128], in_=TB[:, 0:512].rearrange("p (c t) -> p c t", t=128),
                                                   func=AF.Copy), reads=[R_TB], writes=[r_kst])
                S.op("act", lambda e: e.activation(out=bst[:, :, c0:c0 + 128], in_=TB[:, 512:768].rearrange("p (c t) -> p c t", t=128),
                                                   func=AF.Copy), reads=[R_TB], writes=[r_bst])
                if not is_halo:
                    o0 = g0 - NH
                    S.dma_start("sp", ka_o[o0:o0 + 128, :], kaf[:], reads=[r_kaf])
                    S.dma_start("sp", va_o[o0:o0 + 128, :], vaf[:], reads=[r_vaf])
                    S.dma_start("sp", kb_o[o0:o0 + 128, :], kvf[:, 0:128], reads=[r_kvf])
                    S.dma_start("sp", vb_o[o0:o0 + 128, :], kvf[:, 128:256], reads=[r_kvf])
                    for c in range(8):
                        S.op("pe", lambda e, c=c: e.transpose(TA[:, c * 128:(c + 1) * 128], qb_[:, c * 128:(c + 1) * 128], ident[:]),
                             reads=[r_qb, R_const], writes=[R_TA], signal=(c == 7))
                    qst, r_qst = qT_st[i % 2]
                    S.op("act", lambda e: e.activation(out=qst[:, :, c0:c0 + 128], in_=TA.rearrange("p (c t) -> p c t", t=128),
                                                       func=AF.Copy), reads=[R_TA], writes=[r_qst])

            def p2_store(i):
                t0, T = stiles[i]
                if t0 >= NH + NO:
                    return
                kst, r_kst = kaT_st[i % 2]
                bst, r_bst = kbT_st[i % 2]
                S.dma_start("sp", KaTs[:, :, t0:t0 + T].rearrange("c p t -> p c t"), kst[:, :, :T], reads=[r_kst])
                S.dma_start("sp", KbTs[:, :, t0:t0 + T].rearrange("c p t -> p c t"), bst[:, :, :T], reads=[r_bst])
                if t0 >= NH:
                    qst, r_qst = qT_st[i % 2]
                    S.dma_start("sp", QTs[:, :, t0 - NH:t0 - NH + T].rearrange("c p t -> p c t"), qst[:, :, :T], reads=[r_qst])

            n = len(stiles)
            subs = [p2_info(i, j) for i in range(n) for j in range((stiles[i][1] + 127) // 128)]
            p2_load(0)
            for j in range(4):
                for fn in p2_norm_steps(0)[j]:
                    fn()
            if n > 1:
                p2_load(1)
            p2_mm(subs[0])
            nsteps = {}
            for si, sd in enumerate(subs):
                i = sd["i"]
                if sd["j"] == 0:
                    nsteps = p2_norm_steps(i + 1) if i + 1 < n else {}
                nsub_i = (stiles[i][1] + 127) // 128
                js = [sd["j"]] if sd["j"] + 1 < nsub_i else list(range(sd["j"], 4))
                for j in js:
                    for fn in nsteps.get(j, []):
                        fn()
                if sd["j"] == 0 and i + 2 < n:
                    p2_load(i + 2)
                p2_post_a(sd, si % NB)
                if si + 1 < len(subs):
                    p2_mm(subs[si + 1])
                p2_post_b(sd, si % NB)
                if si + 1 == len(subs) or subs[si + 1]["i"] != i:
                    p2_store(i)
            S.barrier()

        with ExitStack() as ps:
            OT = sb(ps, "OT", [128, 8, NOS], BF16); R_OT = Res()
            w_out = sb(ps, "w_out", [128, 8, D], BF16); R_wo = Res()
            for k in range(8):
                S.dma_start("pool", w_out[:, k, :], w_out_d[k * 128:(k + 1) * 128, :], writes=[R_wo])
            pa = ExitStack()
            QT = [(sb(pa, "a_qt%d" % i, [128, NO], BF16), Res()) for i in range(2)]
            KT = [(sb(pa, "a_kt%d" % i, [128, NH + NO], BF16), Res()) for i in range(2)]
            NVT = 17 + 20 + 32
            VT = [(sb(pa, "a_vt%d" % i, [128, NVT, 192], BF16), Res()) for i in range(2)]
            PB = [(sb(pa, "a_pb%d" % i, [128, 512], BF16), Res()) for i in range(4)]
            rec = sb(pa, "a_rec", [128, NO], F32); R_rec = Res()

            def tiles_for(dils):
                out = []
                idx = 0
                for d in dils:
                    for r in range(d):
                        for kb in range(16 // d - 1, 32 // d):
                            ci0 = max(128 * kb, NH // d)
                            ci1 = min(128 * (kb + 2), (NH + NO) // d)
                            nq = ci1 - ci0
                            out.append(dict(d=d, r=r, kb=kb, idx=idx, halo=(kb < 16 // d), nq=nq,
                                            q0=r + d * ci0 - NH, moff=ci0 - 128 * kb, k0=r + d * 128 * kb))
                            idx += 1
                return out

            tilesA = tiles_for((1, 4, 16))
            tilesB = tiles_for((1,))

            def job_load(job):
                qt, rq = QT[job % 2]; kt, rk = KT[job % 2]; vt, rv = VT[job % 2]
                S.dma_start("sp", qt[:], QTs[job, :, :], writes=[rq])
                if job < 4:
                    S.dma_start("sp", kt[:], KaTs[job, :, :], writes=[rk])
                    src, tl = Va_s[job], tilesA
                else:
                    g = (job - 4) // 2
                    S.dma_start("sp", kt[:], KbTs[g, :, :], writes=[rk])
                    src, tl = Vb_s[g], tilesB
                seen = {}
                for t in tl:
                    seen.setdefault((t["d"], t["r"]), []).append(t)
                for (d, r), ts in seen.items():
                    n = len(ts)
                    k0 = ts[0]["k0"]
                    S.dma_start("sp", vt[:, ts[0]["idx"]:ts[0]["idx"] + n, :],
                                src[k0:k0 + d * 128 * (n - 1) + d * 127 + 1:d, :].rearrange("(j i) c -> i j c", i=128),
                                writes=[rv])

            def build_packs(tl):
                packs, cur, cols = [], [], 0
                for t in tl:
                    if cur and (cols + t["nq"] > 512 or cur[0][0]["d"] != t["d"]):
                        packs.append(cur)
                        cur, cols = [], 0
                    cur.append((t, cols))
                    cols += t["nq"]
                if cur:
                    packs.append(cur)
                return packs

            packsA = build_packs(tilesA)
            packsB = build_packs(tilesB)
            work = []
            for job in range(8):
                pk = packsA if job < 4 else packsB
                for hh in range(2):
                    for pi, p in enumerate(pk):
                        work.append((job, hh, pi, p, pi == len(pk) - 1))
            started = {}

            def stage_abc(w, slot):
                job, hh, pi, p, last = w
                qt, rq = QT[job % 2]; kt, rk = KT[job % 2]
                hb = 64 * hh
                st_ps, r_st = PS[4 + slot], RPS[4 + slot]
                pbt, rpb = PB[slot]
                ncols = p[-1][1] + p[-1][0]["nq"]
                for ti, (t, a) in enumerate(p):
                    d, nq, q0, k0 = t["d"], t["nq"], t["q0"], t["k0"]
                    mm(st_ps[:, a:a + nq], kt[hb:hb + 64, k0:k0 + d * 127 + 1:d], qt[hb:hb + 64, q0:q0 + d * (nq - 1) + 1:d],
                       True, True, [rq, rk], [r_st], ti == len(p) - 1)
                S.op("act", lambda e: e.activation(out=pbt[:, :ncols], in_=st_ps[:, :ncols], func=AF.Exp, scale=SCALE),
                     reads=[r_st], writes=[rpb])
                sig = [(t["moff"] + (256 if t["halo"] else 0), t["nq"]) for (t, a) in p]
                if len(p) == 2 and sig == [(0, 256), (0, 256)]:
                    S.op("dve", lambda e: e.tensor_tensor(out=pbt[:, :512].rearrange("p (a c) -> p a c", a=2), in0=pbt[:, :512].rearrange("p (a c) -> p a c", a=2),
                                                          in1=masks[:, 0:256].unsqueeze(1).to_broadcast([128, 2, 256]), op=ALU.mult),
                         reads=[rpb, R_const], writes=[rpb])
                elif len(p) == 4 and sig == [(384, 128), (0, 128)] * 2:
                    S.op("dve", lambda e: e.tensor_tensor(out=pbt[:, :512].rearrange("p (a c) -> p a c", a=2), in0=pbt[:, :512].rearrange("p (a c) -> p a c", a=2),
                                                          in1=masks[:, 512:768].unsqueeze(1).to_broadcast([128, 2, 256]), op=ALU.mult),
                         reads=[rpb, R_const], writes=[rpb])
                else:
                    for (t, a), (mo, nq) in zip(p, sig):
                        S.op("dve", lambda e, a=a, mo=mo, nq=nq: e.tensor_tensor(out=pbt[:, a:a + nq], in0=pbt[:, a:a + nq], in1=masks[:, mo:mo + nq], op=ALU.mult),
                             reads=[rpb, R_const], writes=[rpb])

            def stage_d(w, slot):
                job, hh, pi, p, last = w
                vt, rv = VT[job % 2]
                pbt, rpb = PB[slot]
                if pi == 0:
                    for bank in range(4):
                        started[bank] = False
                allsegs = []
                for (t, a) in p:
                    d, nq, q0 = t["d"], t["nq"], t["q0"]
                    lw = vt[:, t["idx"], 0:128] if hh == 0 else vt[:, t["idx"], 64:192]
                    if d == 1:
                        assert q0 % 4 == 0 and nq % 4 == 0
                        for j in range(4):
                            allsegs.append((lw, pbt[:, a + j:a + nq:4], j, PS[j][:, q0 // 4:q0 // 4 + nq // 4]))
                    elif d == 4:
                        r4 = q0 % 4
                        c0 = q0 // 4
                        assert c0 + nq <= 512
                        allsegs.append((lw, pbt[:, a:a + nq], r4, PS[r4][:, c0:c0 + nq]))
                    else:
                        assert d == 16
                        bank = q0 % 4
                        c0 = q0 // 4
                        assert c0 + 4 * (nq - 1) < 512
                        allsegs.append((lw, pbt[:, a:a + nq], bank, PS[bank][:, c0:c0 + 4 * (nq - 1) + 1:4]))
                for si, (lw, rhs_ap, bank, out_ap) in enumerate(allsegs):
                    first = not started[bank]
                    started[bank] = True
                    mm(out_ap, lw, rhs_ap, first, True, [rv, rpb], [RPS[bank]], si == len(allsegs) - 1)
                if last:
                    ob, db = (0, 64) if hh == 0 else (64, 0)
                    c = job
                    for bank in range(4):
                        cs = slice(bank * 512, (bank + 1) * 512)
                        if job >= 4:
                            hq = 2 * (job - 4) + hh
                            S.op("act", lambda e, bank=bank, cs=cs, hq=hq: e.activation(
                                out=rec[db:db + 64, cs], in_=PS[bank][db:db + 64, :], func=AF.Ln, bias=esink[db:db + 64, hq:hq + 1]),
                                reads=[RPS[bank], R_const], writes=[R_rec])
                        else:
                            S.op("act", lambda e, bank=bank, cs=cs: e.activation(
                                out=rec[db:db + 64, cs], in_=PS[bank][db:db + 64, :], func=AF.Ln),
                                reads=[RPS[bank]], writes=[R_rec])
                    S.op("act", lambda e: e.activation(out=rec[db:db + 64, :], in_=rec[db:db + 64, :], func=AF.Exp, scale=-1.0),
                         reads=[R_rec], writes=[R_rec])
                    for bank in range(4):
                        cs = slice(bank * 512, (bank + 1) * 512)
                        S.op("dve", lambda e, bank=bank, cs=cs: e.tensor_tensor(
                            out=OT[ob:ob + 64, c, 0:NO].rearrange("p (x f) -> p f x", f=4)[:, bank, :], in0=PS[bank][ob:ob + 64, :],
                            in1=rec[db:db + 64, cs], op=ALU.mult), reads=[RPS[bank], R_rec], writes=[R_OT])

            LOOK = 2
            job_load(0)
            nw = len(work)
            for x in range(0, nw + LOOK, 2):
                for y in (x, x + 1):
                    if y < nw:
                        stage_abc(work[y], y % 4)
                for y in (x - LOOK, x - LOOK + 1):
                    if 0 <= y < nw:
                        w = work[y]
                        stage_d(w, y % 4)
                        if w[1] == 0 and w[2] == 0 and w[0] + 1 < 8:
                            job_load(w[0] + 1)
            S.barrier()
            pa.close()
            pb_ = ExitStack()
            sel = sb(pb_, "sel", [16, 16 * 128], F32)
            selT = sb(pb_, "selT", [128, 16 * 16], F32)
            R_sel = Res()
            S.dma_start("sp", sel[:], sel_d, writes=[R_sel])
            S.dma_start("sp", selT[:], selT_d, writes=[R_sel])
            KS = [(sb(pb_, "s_k%d" % i, [128, 3, 512], F32), Res()) for i in range(2)]
            VS = [(sb(pb_, "s_v%d" % i, [128, 3, 512], F32), Res()) for i in range(2)]
            KBS = [(sb(pb_, "s_kb%d" % i, [128, 128], F32), Res()) for i in range(2)]
            VBS = [(sb(pb_, "s_vb%d" % i, [128, 128], F32), Res()) for i in range(2)]
            prod = [(sb(pb_, "s_pr%d" % i, [128, 512], F32), Res()) for i in range(2)]
            sc = [(sb(pb_, "s_sc%d" % i, [128, 32], F32), Res()) for i in range(2)]
            ee = [(sb(pb_, "s_ee%d" % i, [128, 32], F32), Res()) for i in range(2)]
            pv = [(sb(pb_, "s_pv%d" % i, [128, 512], F32), Res()) for i in range(3)]
            tk = sb(pb_, "s_tk", [16, 1024], F32); R_tk = Res()
            snew = sb(pb_, "s_new", [16, 16], F32); R_snew = Res()
            den = sb(pb_, "s_den", [16, 16], F32); R_den = Res()
            osb = sb(pb_, "s_osb", [16, 1024], F32); R_osb = Res()
            pats = [(1920, 1), (1536, 4), (0, 16)]

            def s_load(b):
                k_, rk = KS[b % 2]; v_, rv = VS[b % 2]
                for pi, (r0, st) in enumerate(pats):
                    lo = r0 + (st if st > 1 else 0)
                    lo = 2048 - 128 * st
                    S.dma_start("sp", k_[:, pi, :], cak[b, lo:lo + st * 127 + 1:st, :], writes=[rk])
                    S.dma_start("sp", v_[:, pi, :], cav[b, lo:lo + st * 127 + 1:st, :], writes=[rv])
                S.dma_start("sp", KBS[b % 2][0][:], cbk[b, :, :], writes=[KBS[b % 2][1]])
                S.dma_start("sp", VBS[b % 2][0][:], cbv[b, :, :], writes=[VBS[b % 2][1]])

            pvc = [0]

            def s_compute(b):
                k_, rk = KS[b % 2]; v_, rv = VS[b % 2]
                kb_, rkb = KBS[b % 2]; vb_, rvb = VBS[b % 2]
                s_, rs_ = sc[b % 2]; e_, re_ = ee[b % 2]
                bqa, r_bqa = PS[4 + 2 * (b % 2)], RPS[4 + 2 * (b % 2)]
                bqb, r_bqb = PS[5 + 2 * (b % 2)], RPS[5 + 2 * (b % 2)]
                for pi in range(3):
                    pr, rp = prod[pi % 2]
                    S.op("dve", lambda e, pi=pi, pr=pr: e.tensor_tensor(out=pr[:], in0=k_[:, pi, :], in1=bqa[:, :], op=ALU.mult),
                         reads=[rk, r_bqa], writes=[rp])
                    S.op("dve", lambda e, pi=pi, pr=pr: e.tensor_reduce(out=s_[:, pi * 8:(pi + 1) * 8], in_=pr[:].rearrange("p (h d) -> p h d", d=64),
                                                                         axis=AX.X, op=ALU.add), reads=[rp], writes=[rs_])
                pr, rp = prod[1]
                kb4 = kb_[:].rearrange("p (g d) -> p g d", d=64).unsqueeze(2).to_broadcast([128, 2, 4, 64])
                S.op("dve", lambda e: e.tensor_tensor(out=pr[:].rearrange("p (g j d) -> p g j d", g=2, j=4), in0=bqb[:, :].rearrange("p (g j d) -> p g j d", g=2, j=4),
                                                      in1=kb4, op=ALU.mult), reads=[rkb, r_bqb], writes=[rp])
                S.op("dve", lambda e: e.tensor_reduce(out=s_[:, 24:32], in_=pr[:].rearrange("p (h d) -> p h d", d=64), axis=AX.X, op=ALU.add),
                     reads=[rp], writes=[rs_])
                S.op("act", lambda e: e.activation(out=e_[:], in_=s_[:], func=AF.Exp, scale=SCALE), reads=[rs_], writes=[re_])
                first = (b == 0)
                last = (b == NS - 1)
                for pi in range(3):
                    p_, rpv = pv[pvc[0] % 3]; pvc[0] += 1
                    eb = e_[:, pi * 8:(pi + 1) * 8].unsqueeze(2).to_broadcast([128, 8, 64])
                    S.op("pool", lambda e, pi=pi, p_=p_, eb=eb: e.tensor_tensor(out=p_[:].rearrange("p (h d) -> p h d", d=64),
                                                                               in0=v_[:, pi, :].rearrange("p (h d) -> p h d", d=64), in1=eb, op=ALU.mult),
                         reads=[rv, re_], writes=[rpv])
                    mm(PS[0][:16, :], selT[:, b * 16:(b + 1) * 16], p_[:], first and pi == 0, last and pi == 2, [R_sel, rpv], [RPS[0]], True)
                p_, rpv = pv[pvc[0] % 3]; pvc[0] += 1
                eb = e_[:, 24:32].unsqueeze(2).to_broadcast([128, 8, 64])
                vb4 = vb_[:].rearrange("p (g d) -> p g d", d=64).unsqueeze(2).to_broadcast([128, 2, 4, 64])
                S.op("pool", lambda e: e.tensor_tensor(out=p_[:].rearrange("p (g j d) -> p g j d", g=2, j=4), in0=e_[:, 24:32].rearrange("p (g j) -> p g j", g=2).unsqueeze(3).to_broadcast([128, 2, 4, 64]),
                                                       in1=vb4, op=ALU.mult), reads=[rvb, re_], writes=[rpv])
                mm(PS[1][:16, :], selT[:, b * 16:(b + 1) * 16], p_[:], first, last, [R_sel, rpv], [RPS[1]], True)
                mm(PS[2][:16, 0:32], selT[:, b * 16:(b + 1) * 16], e_[:], first, last, [R_sel, re_], [RPS[2]], True)

            def s_bcast(b):
                mm(PS[4 + 2 * (b % 2)][:, :], sel[:, b * 128:(b + 1) * 128], zs_qa[:, :], True, True, [R_sel, R_zs], [RPS[4 + 2 * (b % 2)]], True)
                mm(PS[5 + 2 * (b % 2)][:, :], sel[:, b * 128:(b + 1) * 128], zs_qb[:, :], True, True, [R_sel, R_zs], [RPS[5 + 2 * (b % 2)]], True)

            s_load(0)
            s_bcast(0)
            for b in range(NS):
                if b + 1 < NS:
                    s_load(b + 1)
                    s_bcast(b + 1)
                s_compute(b)
            S.op("dve", lambda e: e.tensor_tensor(out=tk[:, 0:512], in0=zs_qa[:], in1=zs_ka[:], op=ALU.mult), reads=[R_zs], writes=[R_tk])
            S.op("dve", lambda e: e.tensor_tensor(out=tk[:, 512:1024].rearrange("p (g j d) -> p g j d", g=2, j=4),
                                                  in0=zs_qb[:].rearrange("p (g j d) -> p g j d", g=2, j=4),
                                                  in1=zs_kb[:].rearrange("p (g d) -> p g d", d=64).unsqueeze(2).to_broadcast([16, 2, 4, 64]), op=ALU.mult),
                 reads=[R_zs], writes=[R_tk])
            S.op("dve", lambda e: e.tensor_reduce(out=snew[:], in_=tk[:].rearrange("p (h d) -> p h d", d=64), axis=AX.X, op=ALU.add),
                 reads=[R_tk], writes=[R_snew])
            S.op("act", lambda e: e.activation(out=snew[:], in_=snew[:], func=AF.Exp, scale=SCALE), reads=[R_snew], writes=[R_snew])
            S.op("dve", lambda e: e.tensor_scalar(out=snew[:, 0:8], in0=snew[:, 0:8], scalar1=3.0, scalar2=None, op0=ALU.mult),
                 reads=[R_snew], writes=[R_snew])
            S.op("dve", lambda e: e.tensor_tensor(out=den[:, 0:8], in0=PS[2][:16, 0:8], in1=snew[:, 0:8], op=ALU.add), reads=[RPS[2], R_snew], writes=[R_den])
            S.op("dve", lambda e: e.tensor_tensor(out=den[:, 0:8], in0=PS[2][:16, 8:16], in1=den[:, 0:8], op=ALU.add), reads=[RPS[2], R_den], writes=[R_den])
            S.op("dve", lambda e: e.tensor_tensor(out=den[:, 0:8], in0=PS[2][:16, 16:24], in1=den[:, 0:8], op=ALU.add), reads=[RPS[2], R_den], writes=[R_den])
            S.op("dve", lambda e: e.tensor_tensor(out=den[:, 8:16], in0=PS[2][:16, 24:32], in1=snew[:, 8:16], op=ALU.add), reads=[RPS[2], R_snew], writes=[R_den])
            S.op("dve", lambda e: e.tensor_tensor(out=den[:, 8:16], in0=den[:, 8:16], in1=esink[:16, :], op=ALU.add), reads=[R_den, R_const], writes=[R_den])
            S.op("dve", lambda e: e.reciprocal(out=den[:], in_=den[:]), reads=[R_den], writes=[R_den])
            S.op("dve", lambda e: e.tensor_tensor(out=tk[:, 0:512].rearrange("p (h d) -> p h d", d=64), in0=zs_va[:].rearrange("p (h d) -> p h d", d=64),
                                                  in1=snew[:, 0:8].unsqueeze(2).to_broadcast([16, 8, 64]), op=ALU.mult), reads=[R_zs, R_snew], writes=[R_tk])
            S.op("dve", lambda e: e.tensor_tensor(out=tk[:, 512:1024].rearrange("p (g j d) -> p g j d", g=2, j=4),
                                                  in0=zs_vb[:].rearrange("p (g d) -> p g d", d=64).unsqueeze(2).to_broadcast([16, 2, 4, 64]),
                                                  in1=snew[:, 8:16].rearrange("p (g j) -> p g j", g=2).unsqueeze(3).to_broadcast([16, 2, 4, 64]), op=ALU.mult),
                 reads=[R_zs, R_snew, R_tk], writes=[R_tk])
            S.op("dve", lambda e: e.tensor_tensor(out=osb[:, 0:512], in0=PS[0][:16, :], in1=tk[:, 0:512], op=ALU.add), reads=[RPS[0], R_tk], writes=[R_osb])
            S.op("dve", lambda e: e.tensor_tensor(out=osb[:, 512:1024], in0=PS[1][:16, :], in1=tk[:, 512:1024], op=ALU.add), reads=[RPS[1], R_tk], writes=[R_osb])
            S.op("dve", lambda e: e.tensor_tensor(out=osb[:].rearrange("p (h d) -> p h d", d=64), in0=osb[:].rearrange("p (h d) -> p h d", d=64),
                                                  in1=den[:].unsqueeze(2).to_broadcast([16, 16, 64]), op=ALU.mult), reads=[R_osb, R_den], writes=[R_osb])
            for c in range(8):
                S.op("pe", lambda e, c=c: e.transpose(PS[4][:, c * 16:(c + 1) * 16], osb[:, c * 128:(c + 1) * 128], identf[:16, :16]),
                     reads=[R_osb, R_const], writes=[RPS[4]], signal=(c == 7))
            S.op("act", lambda e: e.activation(out=OT[:, :, NO:NOS], in_=PS[4][:, 0:128].rearrange("p (c t) -> p c t", t=16), func=AF.Copy),
                 reads=[RPS[4]], writes=[R_OT])

            S.barrier()
            pb_.close()
            TT = 128
            XH = [(sb(ps, "o_x%d" % i, [128, 8, TT], F32), Res()) for i in range(3)]
            sq = sb(ps, "o_sq", [128, 8, TT], BF16); R_sq = Res()
            rs = sb(ps, "o_rs", [128, TT], F32); R_rs = Res()
            TMP = [(sb(ps, "o_tmp%d" % i, [128, TT], F32), Res()) for i in range(2)]
            otiles = [(t0, min(TT, NOS - t0)) for t0 in range(0, NOS, TT)]
            no_ = len(otiles)

            def o_bank(i, c):
                bk = 2 * (i % 3) + c // 4
                return PS[bk], RPS[bk], (c % 4) * 128

            def o_load(i):
                t0, T = otiles[i]
                xt, xr = XH[i % 3]
                S.dma_start("sp", xt[:, :, :T], h1T[:, NH + t0:NH + t0 + T].rearrange("(c p) t -> p c t", p=128), writes=[xr])

            def o_mm(i):
                t0, T = otiles[i]
                for c in range(8):
                    bank, rb, off = o_bank(i, c)
                    for k in range(8):
                        mm(bank[:, off:off + T], w_out[:, k, c * 128:(c + 1) * 128], OT[:, k, t0:t0 + T], k == 0, k == 7,
                           [R_wo, R_OT], [rb], k == 7)

            def o_post(i):
                t0, T = otiles[i]
                xt, xr = XH[i % 3]
                for b in range(2):
                    bk = 2 * (i % 3) + b
                    S.op("act", lambda e, b=b, bk=bk: e.activation(out=sq[:, 4 * b:4 * b + 4, :T], in_=PS[bk][:].rearrange("p (a t) -> p a t", a=4)[:, :, :T],
                                                                    func=AF.Square), reads=[RPS[bk]], writes=[R_sq])
                rstd_from_sq(sq, R_sq, T, PS[6], RPS[6], rs, R_rs)
                for c in range(8):
                    bank, rb, off = o_bank(i, c)
                    tm, rt = TMP[c % 2]
                    S.op("dve", lambda e, c=c, bank=bank, off=off, tm=tm: e.scalar_tensor_tensor(
                        out=tm[:, :T], in0=bank[:, off:off + T], scalar=gcol(G_MIXPOST, c), in1=rs[:, :T],
                        op0=ALU.mult, op1=ALU.mult), reads=[rb, R_rs, R_const], writes=[rt])
                    S.op("pool", lambda e, c=c, tm=tm, xt=xt: e.tensor_tensor(out=xt[:, c, :T], in0=xt[:, c, :T], in1=tm[:, :T], op=ALU.add),
                         reads=[rt, xr], writes=[xr])
                S.dma_start("pool", h2T[:, t0:t0 + T].rearrange("(c p) t -> p c t", p=128), xt[:, :, :T], reads=[xr])

            o_load(0)
            if no_ > 1:
                o_load(1)
            o_mm(0)
            for i in range(no_):
                if i + 2 < no_:
                    o_load(i + 2)
                if i + 1 < no_:
                    o_mm(i + 1)
                o_post(i)
            S.barrier()

        ffn_phase("f2", w_f2g, w_f2u, w_f2d, G_F2PRE, G_F2POST, h2T, 0, h3T, 0, NOS)

        with ExitStack() as ps:
            wpg = sb(ps, "wpg", [128, 8, D], BF16); wpp = sb(ps, "wpp", [128, 2, D], BF16); R_w = Res()
            for k in range(8):
                S.dma_start("pool", wpg[:, k, :], w_pg_d[k * 128:(k + 1) * 128, :], writes=[R_w])
            for k in range(2):
                S.dma_start("pool", wpp[:, k, :], w_pp_d[k * 128:(k + 1) * 128, :], writes=[R_w])
            TT = 256
            X = [(sb(ps, "e_x%d" % i, [128, 8, TT], F32), Res()) for i in range(2)]
            Pt = [(sb(ps, "e_p%d" % i, [128, 2, TT], BF16), Res()) for i in range(2)]
            u = sb(ps, "e_u", [128, 8, TT], BF16); R_u = Res()
            yb = sb(ps, "e_y", [128, 8, TT], F32); R_y = Res()
            sg = [(sb(ps, "e_sg%d" % i, [128, TT], F32), Res()) for i in range(2)]
            sq = sb(ps, "e_sq", [128, 8, TT], BF16); R_sq = Res()
            rs = sb(ps, "e_rs", [128, TT], F32); R_rs = Res()
            TMP = [(sb(ps, "e_tmp%d" % i, [128, TT], F32), Res()) for i in range(2)]
            etiles = [(t0, min(TT, NOS - t0)) for t0 in range(0, NOS, TT)]
            X3 = X + [(sb(ps, "e_x2", [128, 8, TT], F32), Res())]
            U2 = [(u, R_u), (sb(ps, "e_u1", [128, 8, TT], BF16), Res())]
            rs2 = sb(ps, "e_rs2", [128, TT], F32); R_rs2 = Res()

            def e_load(i):
                t0, T = etiles[i]
                xt, xr = X3[i % 3]
                pt, rp = Pt[i % 2]
                S.dma_start("sp", xt[:, :, :T], h3T[:, t0:t0 + T].rearrange("(c p) t -> p c t", p=128), writes=[xr])
                S.dma_start("pool", pt[:, :, :T], pT[:, t0:t0 + T].rearrange("(c p) t -> p c t", p=128), writes=[rp])

            def e_pre(i):
                t0, T = etiles[i]
                xt, xr = X3[i % 3]
                uu, ur = U2[i % 2]
                S.op("act", lambda e: e.activation(out=sq[:, :, :T], in_=xt[:, :, :T], func=AF.Square), reads=[xr], writes=[R_sq])
                rstd_from_sq(sq, R_sq, T, PS[6], RPS[6], rs, R_rs)
                for c in range(8):
                    S.op("dve", lambda e, c=c: e.scalar_tensor_tensor(out=uu[:, c, :T], in0=xt[:, c, :T], scalar=gcol(G_PLEPRE, c), in1=rs[:, :T],
                                                                       op0=ALU.mult, op1=ALU.mult), reads=[xr, R_rs, R_const], writes=[ur])

            def e_gate(i):
                t0, T = etiles[i]
                uu, ur = U2[i % 2]
                pt, rp = Pt[i % 2]
                for c in range(8):
                    pb, rpb = PS[c % 4], RPS[c % 4]
                    s_, rs_ = sg[c % 2]
                    for k in range(8):
                        mm(pb[:, 0:T], wpg[:, k, c * 128:(c + 1) * 128], uu[:, k, :T], k == 0, k == 7, [R_w, ur], [rpb], False)
                    for k in range(2):
                        mm(pb[:, 256:256 + T], wpp[:, k, c * 128:(c + 1) * 128], pt[:, k, :T], k == 0, k == 1, [R_w, rp], [rpb], k == 1)
                    S.op("act", lambda e, pb=pb, s_=s_: e.activation(out=s_[:, :T], in_=pb[:, 0:T], func=AF.Sigmoid), reads=[rpb], writes=[rs_])
                    S.op("dve", lambda e, c=c, pb=pb, s_=s_: e.tensor_tensor(out=yb[:, c, :T], in0=s_[:, :T], in1=pb[:, 256:256 + T], op=ALU.mult),
                         reads=[rs_, rpb], writes=[R_y])

            def e_post(i):
                t0, T = etiles[i]
                xt, xr = X3[i % 3]
                S.op("act", lambda e: e.activation(out=sq[:, :, :T], in_=yb[:, :, :T], func=AF.Square), reads=[R_y], writes=[R_sq])
                rstd_from_sq(sq, R_sq, T, PS[6], RPS[6], rs2, R_rs2)
                for c in range(8):
                    tm, rt = TMP[c % 2]
                    S.op("dve", lambda e, c=c, tm=tm: e.scalar_tensor_tensor(out=tm[:, :T], in0=yb[:, c, :T], scalar=gcol(G_PLEPOST, c), in1=rs2[:, :T],
                                                                            op0=ALU.mult, op1=ALU.mult), reads=[R_y, R_rs2, R_const], writes=[rt])
                    S.op("pool", lambda e, c=c, tm=tm, xt=xt: e.tensor_tensor(out=xt[:, c, :T], in0=xt[:, c, :T], in1=tm[:, :T], op=ALU.add),
                         reads=[rt, xr], writes=[xr])
                S.dma_start("pool", yT[:, t0:t0 + T].rearrange("(c p) t -> p c t", p=128), xt[:, :, :T], reads=[xr])

            ne = len(etiles)
            e_load(0)
            if ne > 1:
                e_load(1)
            e_pre(0)
            for i in range(ne):
                e_gate(i)
                if i + 2 < ne:
                    e_load(i + 2)
                if i + 1 < ne:
                    e_pre(i + 1)
                e_post(i)
            S.barrier(final=True)
    return nc


_NC_CACHE = {}


def _host_inputs(inp):
    f32 = np.float32
    xp = np.asarray(inp["x_prompt"], f32)
    xs = np.asarray(inp["x_sample"], f32)[:, 0, :]
    pp = np.asarray(inp["p_prompt"], f32)[0]
    psm = np.asarray(inp["p_sample"], f32)[0][:, 0, :]
    names = ["norm_f1_pre", "norm_f1_post", "norm_mix_pre", "norm_mix_post", "norm_f2_pre", "norm_f2_post",
             "norm_ple_pre", "norm_ple_post"]
    gains = np.zeros((128, 64), f32)
    for n, nm in enumerate(names):
        g = np.asarray(inp[nm], f32)[0]
        gains[:, n * 8:(n + 1) * 8] = g.reshape(8, 128).T
    ident = np.eye(128, dtype=f32)
    kk = np.arange(128)[:, None]
    qq = np.arange(128)[None, :]
    m_cur = (kk <= qq).astype(f32)
    m_next = (kk >= qq).astype(f32)
    m_own = np.concatenate([m_cur, m_next], axis=1)
    sel = np.zeros((16, 16, 128), f32)
    selT = np.zeros((128, 16, 16), f32)
    for b in range(16):
        sel[b, b, :] = 1.0
        selT[:, b, b] = 1.0
    sinks = np.broadcast_to(np.asarray(inp["sinks_b"], f32)[0][None, :], (128, 8)).copy()
    shared = {
        "gains": gains, "ident": ident, "sel": sel.reshape(16, -1), "selT": selT.reshape(128, -1), "sinks": sinks,
    }
    for nm in ["w_f1_gate", "w_f1_up", "w_f1_down", "w_f2_gate", "w_f2_up", "w_f2_down", "w_in", "w_out",
               "w_ple_gate", "w_ple_proj"]:
        shared[nm] = np.ascontiguousarray(np.asarray(inp[nm], f32)[0])
    cak = np.asarray(inp["cache_a_k"], f32)[0].reshape(128, 2048, 512)
    cav = np.asarray(inp["cache_a_v"], f32)[0].reshape(128, 2048, 512)
    cbk = np.asarray(inp["cache_b_k"], f32)[0].reshape(128, 128, 128)
    cbv = np.asarray(inp["cache_b_v"], f32)[0].reshape(128, 128, 128)
    maps = []
    for c in range(NCORES):
        bb, j = c // 4, c % 4
        s = j * NO
        xT = np.zeros((D, NT), f32)
        if j > 0:
            xT[:, 0:NH] = xp[bb, s - NH:s, :].T
        xT[:, NH:NH + NO] = xp[bb, s:s + NO, :].T
        xT[:, NH + NO:] = xs[c * NS:(c + 1) * NS, :].T
        pT = np.zeros((256, NOS), f32)
        pT[:, :NO] = pp[bb, s:s + NO, :].T
        pT[:, NO:] = psm[c * NS:(c + 1) * NS, :].T
        pos = np.concatenate([np.arange(s - NH, s + NO), np.full(128, PAST)]).astype(np.int64)
        cs, sn = _rope_tables(pos)
        cos_t = cs.reshape(33, 128, 32).transpose(1, 0, 2).reshape(128, -1)
        sin_t = sn.reshape(33, 128, 32).transpose(1, 0, 2).reshape(128, -1)
        hp = 1.0 if j > 0 else 0.0
        masks = np.concatenate([m_own, m_own * hp, m_next * hp, m_cur], axis=1).astype(f32)
        m = dict(shared)
        m.update({
            "xT": xT, "pT": pT, "cos_t": np.ascontiguousarray(cos_t), "sin_t": np.ascontiguousarray(sin_t), "masks": masks,
            "cache_a_k": np.ascontiguousarray(cak[c * NS:(c + 1) * NS]), "cache_a_v": np.ascontiguousarray(cav[c * NS:(c + 1) * NS]),
            "cache_b_k": np.ascontiguousarray(cbk[c * NS:(c + 1) * NS]), "cache_b_v": np.ascontiguousarray(cbv[c * NS:(c + 1) * NS]),
        })
        maps.append(m)
    return maps


def kernel(**inputs):
    if "nc" not in _NC_CACHE:
        _NC_CACHE["nc"] = build_program()
    nc = _NC_CACHE["nc"]
    maps = _host_inputs(inputs)
    res = run_bass_kernel_spmd(nc, maps, core_ids=list(range(NCORES)))
    R = res.results
    f32 = np.float32
    y_prompt = np.zeros((2, 8192, D), f32)
    y_sample = np.zeros((128, 1, D), f32)
    nak_p = np.zeros((1, 2, 2048, 8, 64), f32); nav_p = np.zeros((1, 2, 2048, 8, 64), f32)
    nbk_p = np.zeros((1, 2, 128, 2, 64), f32); nbv_p = np.zeros((1, 2, 128, 2, 64), f32)
    nak_s = np.zeros((1, 128, 2048, 8, 64), f32); nav_s = np.zeros((1, 128, 2048, 8, 64), f32)
    nbk_s = np.zeros((1, 128, 128, 2, 64), f32); nbv_s = np.zeros((1, 128, 128, 2, 64), f32)
    for c in range(NCORES):
        bb, j = c // 4, c % 4
        r = R[c]
        yT = np.asarray(r["yT"], f32)
        y_prompt[bb, j * NO:(j + 1) * NO, :] = yT[:, :NO].T
        y_sample[c * NS:(c + 1) * NS, 0, :] = yT[:, NO:].T
        if j == 3:
            nak_p[0, bb] = np.asarray(r["ka_o"], f32).reshape(2048, 8, 64)
            nav_p[0, bb] = np.asarray(r["va_o"], f32).reshape(2048, 8, 64)
            nbk_p[0, bb] = np.asarray(r["kb_o"], f32)[NO - 128:].reshape(128, 2, 64)
            nbv_p[0, bb] = np.asarray(r["vb_o"], f32)[NO - 128:].reshape(128, 2, 64)
        nak_s[0, c * NS:(c + 1) * NS] = np.asarray(r["nak_s"], f32).reshape(NS, 2048, 8, 64)
        nav_s[0, c * NS:(c + 1) * NS] = np.asarray(r["nav_s"], f32).reshape(NS, 2048, 8, 64)
        nbk_s[0, c * NS:(c + 1) * NS] = np.asarray(r["nbk_s"], f32).reshape(NS, 128, 2, 64)
        nbv_s[0, c * NS:(c + 1) * NS] = np.asarray(r["nbv_s"], f32).reshape(NS, 128, 2, 64)
    return (y_prompt, y_sample, nak_p, nav_p, nbk_p, nbv_p, nak_s, nav_s, nbk_s, nbv_s)
```

```python
import numpy as np
import concourse.bass as bass
import concourse.mybir as mybir
from concourse.bass_utils import run_bass_kernel_spmd
from contextlib import ExitStack

F32, BF16 = mybir.dt.float32, mybir.dt.bfloat16
AF = mybir.ActivationFunctionType
ALU = mybir.AluOpType
AX = mybir.AxisListType

D = 1024
DFF = 2816
FC = 22
NH = 2048
NO = 2048
NS = 16
NOS = NO + NS
NT = NH + NOS
INW = 2304
EPS = 1e-6
SCALE = 0.125
PAST = 16384
NCORES = 8
G_F1PRE, G_F1POST, G_MIXPRE, G_MIXPOST, G_F2PRE, G_F2POST, G_PLEPRE, G_PLEPOST = range(8)

DEBUG = False


class Res:
    __slots__ = ("w", "r")

    def __init__(self):
        self.w = {}
        self.r = {}


class Sched:
    def __init__(self, nc, es, n_dma_sems=40):
        self.nc = nc
        self.sems = []
        self.E = {}
        for name, eng in (("pe", nc.tensor), ("act", nc.scalar), ("dve", nc.vector),
                          ("pool", nc.gpsimd), ("sp", nc.sync)):
            sem = es.enter_context(nc.semaphore("s_" + name))
            self.sems.append(sem)
            self.E[name] = dict(eng=eng, key=len(self.sems) - 1, cnt=0, waited={}, name=name)
        self.dma = []
        for i in range(n_dma_sems):
            sem = es.enter_context(nc.semaphore("s_dma%d" % i))
            self.sems.append(sem)
            self.dma.append(dict(key=len(self.sems) - 1, cnt=0))
        self.dma_rr = 0
        self.big = dict(key=None, cnt=0)
        sem = es.enter_context(nc.semaphore("s_big"))
        self.sems.append(sem)
        self.big["key"] = len(self.sems) - 1

    def _wait(self, e, deps):
        for k, v in deps.items():
            if v <= 0 or e["waited"].get(k, 0) >= v:
                continue
            e["eng"].wait_ge(self.sems[k], v)
            e["waited"][k] = v

    def _deps(self, e, reads, writes):
        deps = {}
        own = e["key"]
        for r in reads:
            for k, v in r.w.items():
                if k == own and e["name"] == "pe":
                    continue
                if deps.get(k, 0) < v:
                    deps[k] = v
        for w in writes:
            for k, v in list(w.w.items()) + list(w.r.items()):
                if k == own:
                    continue
                if deps.get(k, 0) < v:
                    deps[k] = v
        return deps

    def op(self, ename, fn, reads=(), writes=(), signal=True):
        e = self.E[ename]
        self._wait(e, self._deps(e, reads, writes))
        ins = fn(e["eng"])
        if signal:
            e["cnt"] += 1
            ins.then_inc(self.sems[e["key"]], 1)
            val = e["cnt"]
        else:
            val = e["cnt"] + 1
        k = e["key"]
        for r in reads:
            if r.r.get(k, 0) < val:
                r.r[k] = val
        for w in writes:
            w.w[k] = val
            w.r = {}
        return ins

    def dma_start(self, qname, out, in_, reads=(), writes=(), big=False):
        q = self.E[qname]
        deps = self._deps(q, reads, writes)
        if big:
            s = self.big
        else:
            s = self.dma[self.dma_rr]
            self.dma_rr = (self.dma_rr + 1) % len(self.dma)
            if s["cnt"] > 0:
                deps[s["key"]] = max(deps.get(s["key"], 0), s["cnt"])
        self._wait(q, deps)
        ins = q["eng"].dma_start(out=out, in_=in_)
        s["cnt"] += 16
        ins.then_inc(self.sems[s["key"]], 16)
        k = s["key"]
        for r in reads:
            if r.r.get(k, 0) < s["cnt"]:
                r.r[k] = s["cnt"]
        for w in writes:
            w.w[k] = s["cnt"]
            w.r = {}
        return ins

    def barrier(self, final=False):
        tot = {}
        for e in self.E.values():
            tot[e["key"]] = e["cnt"]
        for s in self.dma + ([self.big] if final else []):
            tot[s["key"]] = s["cnt"]
        for e in self.E.values():
            d = dict(tot)
            d.pop(e["key"], None)
            self._wait(e, d)


def _rope_tables(pos):
    inv = np.power(np.float32(10000.0), -np.arange(32, dtype=np.float32) * np.float32(2.0) / np.float32(64.0)).astype(np.float32)
    ang = pos.astype(np.float32)[:, None] * inv[None, :]
    return np.cos(ang).astype(np.float32), np.sin(ang).astype(np.float32)


def build_program():
    nc = bass.Bass("TRN2", target_bir_lowering=False)

    def din(name, shape, dt=F32):
        return nc.dram_tensor(name, list(shape), dt, kind="ExternalInput").ap()

    def dout(name, shape, dt=F32):
        return nc.dram_tensor(name, list(shape), dt, kind="ExternalOutput").ap()

    def dscr(name, shape, dt):
        kind = "ExternalOutput" if DEBUG else "Internal"
        return nc.dram_tensor(name, list(shape), dt, kind=kind).ap()

    xT = din("xT", [D, NT])
    pT = din("pT", [256, NOS])
    gains_d = din("gains", [128, 64])
    cos_d = din("cos_t", [128, 33 * 32])
    sin_d = din("sin_t", [128, 33 * 32])
    mask_d = din("masks", [128, 768])
    ident_d = din("ident", [128, 128])
    sel_d = din("sel", [16, 16 * 128])
    selT_d = din("selT", [128, 16 * 16])
    sinks_d = din("sinks", [128, 8])
    w_f1g = din("w_f1_gate", [D, DFF]); w_f1u = din("w_f1_up", [D, DFF]); w_f1d = din("w_f1_down", [DFF, D])
    w_f2g = din("w_f2_gate", [D, DFF]); w_f2u = din("w_f2_up", [D, DFF]); w_f2d = din("w_f2_down", [DFF, D])
    w_in_d = din("w_in", [D, INW]); w_out_d = din("w_out", [D, D])
    w_pg_d = din("w_ple_gate", [D, D]); w_pp_d = din("w_ple_proj", [256, D])
    cak = din("cache_a_k", [NS, 2048, 512]); cav = din("cache_a_v", [NS, 2048, 512])
    cbk = din("cache_b_k", [NS, 128, 128]); cbv = din("cache_b_v", [NS, 128, 128])

    yT = dout("yT", [D, NOS])
    ka_o = dout("ka_o", [NO, 512]); va_o = dout("va_o", [NO, 512])
    kb_o = dout("kb_o", [NO, 128]); vb_o = dout("vb_o", [NO, 128])
    nak_s = dout("nak_s", [NS, 2048, 512]); nav_s = dout("nav_s", [NS, 2048, 512])
    nbk_s = dout("nbk_s", [NS, 128, 128]); nbv_s = dout("nbv_s", [NS, 128, 128])

    h1T = dscr("h1T", [D, NT], F32)
    h2T = dscr("h2T", [D, NOS], F32)
    h3T = dscr("h3T", [D, NOS], F32)
    QTs = dscr("QTs", [8, 128, NO], BF16)
    KaTs = dscr("KaTs", [4, 128, NH + NO], BF16)
    KbTs = dscr("KbTs", [2, 128, NH + NO], BF16)
    Va_s = dscr("Va_s", [4, NH + NO, 192], BF16)
    Vb_s = dscr("Vb_s", [2, NH + NO, 192], BF16)

    with ExitStack() as es:
        S = Sched(nc, es)

        def sb(stack, name, shape, dt):
            return stack.enter_context(nc.sbuf_tensor("sb_" + name, list(shape), dt))

        ones = sb(es, "ones", [128, 128], BF16)
        ident = sb(es, "ident", [128, 128], BF16)
        identf = sb(es, "identf", [128, 128], F32)
        gains = sb(es, "gains", [128, 64], F32)
        gains_h = sb(es, "gains_h", [128, 64], F32)
        masks = sb(es, "masks", [128, 768], BF16)
        esink = sb(es, "esink", [128, 8], F32)
        zs_qa = sb(es, "zs_qa", [16, 512], F32); zs_ka = sb(es, "zs_ka", [16, 512], F32)
        zs_va = sb(es, "zs_va", [16, 512], F32); zs_qb = sb(es, "zs_qb", [16, 512], F32)
        zs_kb = sb(es, "zs_kb", [16, 128], F32); zs_vb = sb(es, "zs_vb", [16, 128], F32)
        R_const = Res(); R_zs = Res(); R_ostok = Res()
        PS = []
        RPS = []
        for i in range(8):
            PS.append(es.enter_context(nc.psum_tensor("ps%d" % i, [128, 512], F32)))
            RPS.append(Res())

        S.op("dve", lambda e: e.memset(ones[:], 1.0), writes=[R_const])
        S.dma_start("sp", gains[:], gains_d, writes=[R_const])
        S.dma_start("pool", masks[:], mask_d, writes=[R_const])
        S.dma_start("pool", ident[:], ident_d, writes=[R_const])
        S.dma_start("sp", identf[:], ident_d, writes=[R_const])
        S.dma_start("sp", esink[:], sinks_d, writes=[R_const])
        S.op("dve", lambda e: e.tensor_scalar(out=gains_h[:], in0=gains[:], scalar1=0.5, scalar2=None, op0=ALU.mult),
             reads=[R_const], writes=[R_const])
        S.op("act", lambda e: e.activation(out=esink[:], in_=esink[:], func=AF.Exp), reads=[R_const], writes=[R_const])

        R_cache_out = Res()

        def flat16(ap):
            return ap.rearrange("r c -> (r c)").rearrange("(a b x) -> a b x", a=16, b=32)
        bg_dmas = []
        for b in range(NS):
            bg_dmas.append(lambda b=b: S.dma_start("sp", flat16(nak_s[b, 0:2047, :]), flat16(cak[b, 1:2048, :]), writes=[R_cache_out], big=True))
            bg_dmas.append(lambda b=b: S.dma_start("sp", flat16(nav_s[b, 0:2047, :]), flat16(cav[b, 1:2048, :]), writes=[R_cache_out], big=True))
        bg_dmas.append(lambda: S.dma_start("sp", nbk_s[:, 0:127, :], cbk[:, 1:128, :], writes=[R_cache_out], big=True))
        bg_dmas.append(lambda: S.dma_start("sp", nbv_s[:, 0:127, :], cbv[:, 1:128, :], writes=[R_cache_out], big=True))

        def mm(out, lhsT, rhs, start, stop, reads, writes, signal):
            return S.op("pe", lambda e: e.matmul(out, lhsT, rhs, start=start, stop=stop),
                        reads=reads, writes=writes, signal=signal)

        def gcol(n, c, half=False):
            t = gains_h if half else gains
            return t[:, n * 8 + c:n * 8 + c + 1]

        def rstd_from_sq(sq, R_sq, T, psn, R_psn, rstd, R_rstd):
            for c in range(8):
                mm(psn[:, :T], ones[:], sq[:, c, :T], c == 0, c == 7, [R_sq, R_const], [R_psn], c == 7)
            S.op("dve", lambda e: e.tensor_scalar(out=rstd[:, :T], in0=psn[:, :T], scalar1=1.0 / D, scalar2=EPS,
                                                  op0=ALU.mult, op1=ALU.add), reads=[R_psn], writes=[R_rstd])
            S.op("act", lambda e: e.activation(out=rstd[:, :T], in_=rstd[:, :T], func=AF.Sqrt), reads=[R_rstd], writes=[R_rstd])
            S.op("dve", lambda e: e.reciprocal(out=rstd[:, :T], in_=rstd[:, :T]), reads=[R_rstd], writes=[R_rstd])

        def ffn_phase(tag, wg_d, wu_d, wd_d, g_pre, g_post, src, src_off, dst, dst_off, ncols, bg_per_tile=0, wg_pre=None):
            TT = 256
            tiles = [(t0, min(TT, ncols - t0)) for t0 in range(0, ncols, TT)]
            with ExitStack() as ps:
                wg = wg_pre[0] if wg_pre is not None else sb(ps, tag + "wg", [128, 8, DFF], BF16)
                R_wgpre = wg_pre[1] if wg_pre is not None else Res()
                wu = sb(ps, tag + "wu", [128, 8, DFF], BF16)
                wd = sb(ps, tag + "wd", [128, FC, D], BF16)
                FG = [(0, 2), (2, 6), (6, 12), (12, 22)]
                R_wgu = [Res() for _ in FG]
                R_wd = [Res() for _ in FG]
                fgrp = {}
                for gi, (f0, f1) in enumerate(FG):
                    for f in range(f0, f1):
                        fgrp[f] = gi

                def load_weights():
                    for gi, (f0, f1) in enumerate(FG):
                        c0, c1 = f0 * 128, f1 * 128
                        if wg_pre is None:
                            S.dma_start("pool", wg[:, :, c0:c1], wg_d[:, c0:c1].rearrange("(k p) c -> p k c", p=128), writes=[R_wgu[gi]])
                        S.dma_start("pool", wu[:, :, c0:c1], wu_d[:, c0:c1].rearrange("(k p) c -> p k c", p=128), writes=[R_wgu[gi]])
                    for gi, (f0, f1) in enumerate(FG):
                        S.dma_start("pool", wd[:, f0:f1, :],
                                    wd_d[f0 * 128:f1 * 128, :].rearrange("(f p) c -> p f c", p=128), writes=[R_wd[gi]])
                X = [(sb(ps, tag + "x%d" % i, [128, 8, TT], F32), Res()) for i in range(3)]
                U = [(sb(ps, tag + "u%d" % i, [128, 8, TT], BF16), Res()) for i in range(2)]
                hid = sb(ps, tag + "hid", [128, FC, TT], BF16); R_hid = Res()
                sq = sb(ps, tag + "sq", [128, 8, TT], BF16); R_sq = Res()
                RS = [(sb(ps, tag + "rs%d" % i, [128, TT], F32), Res()) for i in range(2)]
                SG = [(sb(ps, tag + "sg%d" % i, [128, TT], F32), Res()) for i in range(3)]
                TMP = [(sb(ps, tag + "tmp%d" % i, [128, TT], F32), Res()) for i in range(2)]
                psn, R_psn = PS[6], RPS[6]

                def load(i):
                    t0, T = tiles[i]
                    xt, xr = X[i % 3]
                    S.dma_start("sp", xt[:, :, :T],
                                src[:, src_off + t0:src_off + t0 + T].rearrange("(c p) t -> p c t", p=128), writes=[xr])

                GUB = [4, 5, 7]

                def prenorm_steps(i):
                    t0, T = tiles[i]
                    xt, xr = X[i % 3]
                    u, ur = U[i % 2]
                    rs, rr = RS[0]

                    def a():
                        S.op("act", lambda e: e.activation(out=sq[:, :, :T], in_=xt[:, :, :T], func=AF.Square),
                             reads=[xr], writes=[R_sq])

                    def b():
                        for c in range(8):
                            mm(psn[:, :T], ones[:], sq[:, c, :T], c == 0, c == 7, [R_sq, R_const], [R_psn], c == 7)
                        S.op("dve", lambda e: e.tensor_scalar(out=rs[:, :T], in0=psn[:, :T], scalar1=1.0 / D, scalar2=EPS,
                                                              op0=ALU.mult, op1=ALU.add), reads=[R_psn], writes=[rr])
                        S.op("act", lambda e: e.activation(out=rs[:, :T], in_=rs[:, :T], func=AF.Sqrt), reads=[rr], writes=[rr])

                    def c_():
                        S.op("dve", lambda e: e.reciprocal(out=rs[:, :T], in_=rs[:, :T]), reads=[rr], writes=[rr])

                    def d(cs):
                        for c in cs:
                            S.op("dve", lambda e, c=c: e.scalar_tensor_tensor(
                                out=u[:, c, :T], in0=xt[:, c, :T], scalar=gcol(g_pre, c), in1=rs[:, :T],
                                op0=ALU.mult, op1=ALU.mult), reads=[xr, rr, R_const], writes=[ur])

                    return [(14, a), (16, b), (18, c_), (19, lambda: d(range(0, 3))), (20, lambda: d(range(3, 6))), (21, lambda: d(range(6, 8)))]

                def gateup(i, hooks):
                    t0, T = tiles[i]
                    u, ur = U[i % 2]
                    for f in range(FC):
                        for fn in hooks.get(f, []):
                            fn()
                        bkn = GUB[f % 3]
                        pb, rpb = PS[bkn], RPS[bkn]
                        sg, rsg = SG[f % 3]
                        for k in range(8):
                            mm(pb[:, 0:T], wg[:, k, f * 128:(f + 1) * 128], u[:, k, :T], k == 0, k == 7,
                               [R_wgu[fgrp[f]], R_wgpre, ur], [rpb], False)
                        for k in range(8):
                            mm(pb[:, 256:256 + T], wu[:, k, f * 128:(f + 1) * 128], u[:, k, :T], k == 0, k == 7,
                               [R_wgu[fgrp[f]], ur], [rpb], k == 7)
                        S.op("act", lambda e: e.activation(out=sg[:, :T], in_=pb[:, 0:T], func=AF.Silu),
                             reads=[rpb], writes=[rsg])
                        S.op("dve", lambda e, f=f: e.tensor_tensor(out=hid[:, f, :T], in0=sg[:, :T], in1=pb[:, 256:256 + T],
                                                                    op=ALU.mult), reads=[rsg, rpb], writes=[R_hid])

                def down_mm(i):
                    t0, T = tiles[i]
                    for c in range(8):
                        bank, rb = PS[c // 2], RPS[c // 2]
                        off = (c % 2) * 256
                        for f in range(FC):
                            mm(bank[:, off:off + T], wd[:, f, c * 128:(c + 1) * 128], hid[:, f, :T], f == 0, f == FC - 1,
                               [R_wd[fgrp[f]], R_hid], [rb], f == FC - 1)
                    for b in range(4):
                        S.op("act", lambda e, b=b: e.activation(
                            out=sq[:, 2 * b:2 * b + 2, :T], in_=PS[b][:].rearrange("p (a t) -> p a t", a=2)[:, :, :T],
                            func=AF.Square), reads=[RPS[b]], writes=[R_sq])

                def post_steps(i):
                    t0, T = tiles[i]
                    xt, xr = X[i % 3]
                    rs, rr = RS[1]

                    def a():
                        for c in range(8):
                            mm(psn[:, :T], ones[:], sq[:, c, :T], c == 0, c == 7, [R_sq, R_const], [R_psn], c == 7)
                        S.op("dve", lambda e: e.tensor_scalar(out=rs[:, :T], in0=psn[:, :T], scalar1=1.0 / D, scalar2=EPS,
                                                              op0=ALU.mult, op1=ALU.add), reads=[R_psn], writes=[rr])
                        S.op("act", lambda e: e.activation(out=rs[:, :T], in_=rs[:, :T], func=AF.Sqrt), reads=[rr], writes=[rr])

                    def b():
                        S.op("dve", lambda e: e.reciprocal(out=rs[:, :T], in_=rs[:, :T]), reads=[rr], writes=[rr])

                    def cstep(c):
                        bank, rb = PS[c // 2], RPS[c // 2]
                        off = (c % 2) * 256
                        tm, rt = TMP[c % 2]
                        S.op("dve", lambda e: e.scalar_tensor_tensor(
                            out=tm[:, :T], in0=bank[:, off:off + T], scalar=gcol(g_post, c, True), in1=rs[:, :T],
                            op0=ALU.mult, op1=ALU.mult), reads=[rb, rr, R_const], writes=[rt])
                        S.op("pool", lambda e: e.tensor_tensor(out=xt[:, c, :T], in0=xt[:, c, :T], in1=tm[:, :T],
                                                               op=ALU.add), reads=[rt, xr], writes=[xr])

                    def st():
                        S.dma_start("pool", dst[:, dst_off + t0:dst_off + t0 + T].rearrange("(c p) t -> p c t", p=128),
                                    xt[:, :, :T], reads=[xr])

                    steps = [(2, a), (4, b)]
                    for c in range(8):
                        steps.append((5 + c, (lambda c=c: cstep(c))))
                    steps.append((13, st))
                    return steps

                n = len(tiles)
                load(0)
                load_weights()
                if n > 1:
                    load(1)
                for (_, fn) in prenorm_steps(0):
                    fn()
                for i in range(n):
                    hooks = {}
                    if i >= 1:
                        for (f, fn) in post_steps(i - 1):
                            hooks.setdefault(f, []).append(fn)
                    if i + 1 < n:
                        for (f, fn) in prenorm_steps(i + 1):
                            hooks.setdefault(f, []).append(fn)
                    gateup(i, hooks)
                    if i + 2 < n:
                        load(i + 2)
                    for _ in range(bg_per_tile if i >= 2 else 0):
                        if bg_dmas:
                            bg_dmas.pop(0)()
                    down_mm(i)
                for (_, fn) in post_steps(n - 1):
                    fn()
                S.barrier()

        ffn_phase("f1", w_f1g, w_f1u, w_f1d, G_F1PRE, G_F1POST, xT, 0, h1T, 0, NT, bg_per_tile=3)
        while bg_dmas:
            bg_dmas.pop(0)()

        with ExitStack() as ps:
            w_in = sb(ps, "w_in", [128, 8, INW], BF16); R_w = Res()
            for k in range(8):
                S.dma_start("pool", w_in[:, k, :], w_in_d[k * 128:(k + 1) * 128, :], writes=[R_w])
            cosT = sb(ps, "cosT", [128, 33, 32], F32); sinT = sb(ps, "sinT", [128, 33, 32], F32); R_tab = Res()
            S.dma_start("sp", cosT[:].rearrange("p a b -> p (a b)"), cos_d, writes=[R_tab])
            S.dma_start("sp", sinT[:].rearrange("p a b -> p (a b)"), sin_d, writes=[R_tab])
            ST = 512
            X = [(sb(ps, "p2x%d" % i, [128, 8, ST], F32), Res()) for i in range(2)]
            U = [(sb(ps, "p2u%d" % i, [128, 8, ST], BF16), Res()) for i in range(2)]
            sq = sb(ps, "p2sq", [128, 8, ST], BF16); R_sq = Res()
            rs = sb(ps, "p2rs", [128, ST], F32); R_rs = Res()
            NB = 2
            NT4 = 4
            T4 = [[(sb(ps, "p2t%d_%d" % (i, j), [128, 8, 32], F32), Res()) for j in range(4)] for i in range(NT4)]
            t4_ctr = [0]
            ka_f = [(sb(ps, "ka_f%d" % i, [128, 512], F32), Res()) for i in range(NB)]
            va_f = [(sb(ps, "va_f%d" % i, [128, 512], F32), Res()) for i in range(NB)]
            kvb_f = [(sb(ps, "kvb_f%d" % i, [128, 256], F32), Res()) for i in range(NB)]
            q_b = [(sb(ps, "q_b%d" % i, [128, 1024], BF16), Res()) for i in range(NB)]
            ka_b = [(sb(ps, "ka_b%d" % i, [128, 512], BF16), Res()) for i in range(NB)]
            kbdup = [(sb(ps, "kbdup%d" % i, [128, 2, 2, 64], BF16), Res()) for i in range(NB)]
            vaug = [(sb(ps, "vaug%d" % i, [128, 4, 192], BF16), Res()) for i in range(NB)]
            vbaug = [(sb(ps, "vbaug%d" % i, [128, 2, 192], BF16), Res()) for i in range(NB)]
            qT_st = [(sb(ps, "qT_st%d" % i, [128, 8, ST], BF16), Res()) for i in range(2)]
            kaT_st = [(sb(ps, "kaT_st%d" % i, [128, 4, ST], BF16), Res()) for i in range(2)]
            kbT_st = [(sb(ps, "kbT_st%d" % i, [128, 2, ST], BF16), Res()) for i in range(2)]
            TA = PS[6][:].bitcast(BF16); R_TA = RPS[6]
            TB = PS[7][:].bitcast(BF16); R_TB = RPS[7]
            psn, R_psn = PS[5], RPS[5]
            for i in range(NB):
                S.op("dve", lambda e, i=i: e.memset(vaug[i][0][:], 1.0), writes=[vaug[i][1]])
                S.op("dve", lambda e, i=i: e.memset(vbaug[i][0][:], 1.0), writes=[vbaug[i][1]])

            stiles = [(t0, min(ST, NT - t0)) for t0 in range(0, NT, ST)]

            def p2_load(i):
                t0, T = stiles[i]
                xt, xr = X[i % 2]
                S.dma_start("sp", xt[:, :, :T], h1T[:, t0:t0 + T].rearrange("(c p) t -> p c t", p=128), writes=[xr])

            def p2_norm_steps(i):
                t0, T = stiles[i]
                xt, xr = X[i % 2]
                u, ur = U[i % 2]

                def a():
                    S.op("act", lambda e: e.activation(out=sq[:, :, :T], in_=xt[:, :, :T], func=AF.Square),
                         reads=[xr], writes=[R_sq])

                def b():
                    rstd_from_sq(sq, R_sq, T, psn, R_psn, rs, R_rs)

                def d(cs):
                    for c in cs:
                        S.op("dve", lambda e, c=c: e.scalar_tensor_tensor(
                            out=u[:, c, :T], in0=xt[:, c, :T], scalar=gcol(G_MIXPRE, c), in1=rs[:, :T],
                            op0=ALU.mult, op1=ALU.mult), reads=[xr, R_rs, R_const], writes=[ur])

                return {0: [a], 1: [b], 2: [lambda: d(range(0, 4))], 3: [lambda: d(range(4, 8))]}

            def rope(src, H, np_, ti, dst1, dst2, reads, wres, bi):
                t4 = T4[t4_ctr[0] % NT4]
                t4_ctr[0] += 1
                cb = cosT[:np_, ti, :].unsqueeze(1).to_broadcast([np_, H, 32])
                sn = sinT[:np_, ti, :].unsqueeze(1).to_broadcast([np_, H, 32])
                x1 = src[:, :, 0:32]
                x2 = src[:, :, 32:64]
                for j, (a, b_) in enumerate(((x1, cb), (x2, sn), (x2, cb), (x1, sn))):
                    S.op("dve", lambda e, j=j, a=a, b_=b_: e.tensor_tensor(out=t4[j][0][:np_, :H, :], in0=a, in1=b_, op=ALU.mult),
                         reads=reads + [R_tab], writes=[t4[j][1]])
                S.op("pool", lambda e: e.tensor_tensor(out=dst1, in0=t4[0][0][:np_, :H, :], in1=t4[1][0][:np_, :H, :],
                                                       op=ALU.subtract), reads=[t4[0][1], t4[1][1]], writes=[wres])
                S.op("pool", lambda e: e.tensor_tensor(out=dst2, in0=t4[2][0][:np_, :H, :], in1=t4[3][0][:np_, :H, :],
                                                       op=ALU.add), reads=[t4[2][1], t4[3][1]], writes=[wres])

            def v3(ap, H):
                return ap.rearrange("p (h d) -> p h d", d=64)

            def p2_info(i, j):
                t0, T = stiles[i]
                c0 = j * 128
                np_ = min(128, T - c0)
                g0 = t0 + c0
                return dict(i=i, j=j, t0=t0, T=T, c0=c0, np_=np_, g0=g0, is_sample=g0 >= NH + NO, is_halo=g0 < NH, ti=g0 // 128)

            def p2_mm(sd):
                i, c0, np_ = sd["i"], sd["c0"], sd["np_"]
                u, ur = U[i % 2]
                slices = [(512, 512, 1), (2048, 256, 4), (1024, 512, 2), (0, 512, 0), (1536, 512, 3)]
                for (s0, w, bk) in slices:
                    if sd["is_halo"] and bk in (0, 3):
                        continue
                    for k in range(8):
                        mm(PS[bk][:np_, :w], u[:, k, c0:c0 + np_], w_in[:, k, s0:s0 + w], k == 0, k == 7,
                           [ur, R_w], [RPS[bk]], k == 7)

            def p2_post_a(sd, bi):
                i, c0, np_, g0, ti = sd["i"], sd["c0"], sd["np_"], sd["g0"], sd["ti"]
                is_halo = sd["is_halo"]
                if sd["is_sample"]:
                    rope(v3(PS[1][:np_, :], 8), 8, np_, ti, v3(zs_ka[:np_, :], 8)[:, :, 0:32], v3(zs_ka[:np_, :], 8)[:, :, 32:64], [RPS[1]], R_zs, bi)
                    rope(v3(PS[4][:np_, 0:128], 2), 2, np_, ti, v3(zs_kb[:np_, :], 2)[:, :, 0:32], v3(zs_kb[:np_, :], 2)[:, :, 32:64], [RPS[4]], R_zs, bi)
                    rope(v3(PS[0][:np_, :], 8), 8, np_, ti, v3(zs_qa[:np_, :], 8)[:, :, 0:32], v3(zs_qa[:np_, :], 8)[:, :, 32:64], [RPS[0]], R_zs, bi)
                    rope(v3(PS[3][:np_, :], 8), 8, np_, ti, v3(zs_qb[:np_, :], 8)[:, :, 0:32], v3(zs_qb[:np_, :], 8)[:, :, 32:64], [RPS[3]], R_zs, bi)
                    S.op("act", lambda e: e.activation(out=zs_va[:np_, :], in_=PS[2][:np_, :], func=AF.Copy), reads=[RPS[2]], writes=[R_zs])
                    S.op("act", lambda e: e.activation(out=zs_vb[:np_, :], in_=PS[4][:np_, 128:256], func=AF.Copy), reads=[RPS[4]], writes=[R_zs])
                    S.dma_start("pool", nak_s[:, 2047, :], zs_ka[:np_, :], reads=[R_zs], writes=[R_cache_out])
                    S.dma_start("pool", nav_s[:, 2047, :], zs_va[:np_, :], reads=[R_zs], writes=[R_cache_out])
                    S.dma_start("pool", nbk_s[:, 127, :], zs_kb[:np_, :], reads=[R_zs], writes=[R_cache_out])
                    S.dma_start("pool", nbv_s[:, 127, :], zs_vb[:np_, :], reads=[R_zs], writes=[R_cache_out])
                    return
                kaf, r_kaf = ka_f[bi]; vaf, r_vaf = va_f[bi]; kvf, r_kvf = kvb_f[bi]
                qb_, r_qb = q_b[bi]; kab, r_kab = ka_b[bi]; kbd, r_kbd = kbdup[bi]
                vg, r_vg = vaug[bi]; vbg, r_vbg = vbaug[bi]
                rope(v3(PS[1][:, :], 8), 8, 128, ti, v3(kaf[:], 8)[:, :, 0:32], v3(kaf[:], 8)[:, :, 32:64], [RPS[1]], r_kaf, bi)
                S.op("act", lambda e: e.activation(out=kab[:], in_=kaf[:], func=AF.Copy), reads=[r_kaf], writes=[r_kab])
                rope(v3(PS[4][:, 0:128], 2), 2, 128, ti, v3(kvf[:, 0:128], 2)[:, :, 0:32], v3(kvf[:, 0:128], 2)[:, :, 32:64], [RPS[4]], r_kvf, bi)
                S.op("act", lambda e: e.activation(out=kvf[:, 128:256], in_=PS[4][:, 128:256], func=AF.Copy), reads=[RPS[4]], writes=[r_kvf])
                for dd in range(2):
                    S.op("act", lambda e, dd=dd: e.activation(out=kbd[:, :, dd, :], in_=v3(kvf[:, 0:128], 2), func=AF.Copy),
                         reads=[r_kvf], writes=[r_kbd])
                vb4 = vbg[:].rearrange("p a (b c) -> p a b c", c=64)
                for dd in (0, 2):
                    S.op("act", lambda e, dd=dd: e.activation(out=vb4[:, :, dd, :], in_=v3(PS[4][:, 128:256], 2), func=AF.Copy),
                         reads=[RPS[4]], writes=[r_vbg])
                va4 = vg[:].rearrange("p a (b c) -> p a b c", c=64)
                S.op("act", lambda e: e.activation(out=va4[:, :, 0:3:2, :], in_=PS[2][:, :].rearrange("p (a b c) -> p a b c", b=2, c=64),
                                                   func=AF.Copy), reads=[RPS[2]], writes=[r_vg])
                if not is_halo:
                    S.op("act", lambda e: e.activation(out=vaf[:], in_=PS[2][:, :], func=AF.Copy), reads=[RPS[2]], writes=[r_vaf])
                    rope(v3(PS[0][:, :], 8), 8, 128, ti, v3(qb_[:, 0:512], 8)[:, :, 0:32], v3(qb_[:, 0:512], 8)[:, :, 32:64], [RPS[0]], r_qb, bi)
                    rope(v3(PS[3][:, :], 8), 8, 128, ti, v3(qb_[:, 512:1024], 8)[:, :, 0:32], v3(qb_[:, 512:1024], 8)[:, :, 32:64], [RPS[3]], r_qb, bi)

            def p2_post_b(sd, bi):
                i, c0, g0 = sd["i"], sd["c0"], sd["g0"]
                is_halo = sd["is_halo"]
                if sd["is_sample"]:
                    return
                kaf, r_kaf = ka_f[bi]; vaf, r_vaf = va_f[bi]; kvf, r_kvf = kvb_f[bi]
                qb_, r_qb = q_b[bi]; kab, r_kab = ka_b[bi]; kbd, r_kbd = kbdup[bi]
                vg, r_vg = vaug[bi]; vbg, r_vbg = vbaug[bi]
                S.dma_start("sp", Va_s[:, g0:g0 + 128, :].rearrange("a t c -> t a c"), vg[:], reads=[r_vg])
                S.dma_start("sp", Vb_s[:, g0:g0 + 128, :].rearrange("a t c -> t a c"), vbg[:], reads=[r_vbg])
                for c in range(4):
                    S.op("pe", lambda e, c=c: e.transpose(TB[:, c * 128:(c + 1) * 128], kab[:, c * 128:(c + 1) * 128], ident[:]),
                         reads=[r_kab, R_const], writes=[R_TB], signal=False)
                for g in range(2):
                    S.op("pe", lambda e, g=g: e.transpose(TB[:, (4 + g) * 128:(5 + g) * 128],
                                                          kbd[:, g, :, :].rearrange("p a b -> p (a b)"), ident[:]),
                         reads=[r_kbd, R_const], writes=[R_TB], signal=(g == 1))
                kst, r_kst = kaT_st[i % 2]
                bst, r_bst = kbT_st[i % 2]
                S.op("act", lambda e: e.activation(out=kst[:, :, c0:c0 + 128], in_=TB[:, 0:512].rearrange("p (c t) -> p c t", t=128),
                                                   func=AF.Copy), reads=[R_TB], writes=[r_kst])
                S.op("act", lambda e: e.activation(out=bst[:, :, c0:c0 + 128], in_=TB[:, 512:768].rearrange("p (c t) -> p c t", t=128),
                                                   func=AF.Copy), reads=[R_TB], writes=[r_bst])
                if not is_halo:
                    o0 = g0 - NH
                    S.dma_start("sp", ka_o[o0:o0 + 128, :], kaf[:], reads=[r_kaf])
                    S.dma_start("sp", va_o[o0:o0 + 128, :], vaf[:], reads=[r_vaf])
                    S.dma_start("sp", kb_o[o0:o0 + 128, :], kvf[:, 0:128], reads=[r_kvf])
                    S.dma_start("sp", vb_o[o0:o0 + 128, :], kvf[:, 128:256], reads=[r_kvf])
                    for c in range(8):
                        S.op("pe", lambda e, c=c: e.transpose(TA[:, c * 128:(c + 1) * 128], qb_[:, c * 128:(c + 1) * 128], ident[:]),
                             reads=[r_qb, R_const], writes=[R_TA], signal=(c == 7))
                    qst, r_qst = qT_st[i % 2]
                    S.op("act", lambda e: e.activation(out=qst[:, :, c0:c0 + 128], in_=TA.rearrange("p (c t) -> p c t", t=128),
                                                       func=AF.Copy), reads=[R_TA], writes=[r_qst])

            def p2_store(i):
                t0, T = stiles[i]
                if t0 >= NH + NO:
                    return
                kst, r_kst = kaT_st[i % 2]
                bst, r_bst = kbT_st[i % 2]
                S.dma_start("sp", KaTs[:, :, t0:t0 + T].rearrange("c p t -> p c t"), kst[:, :, :T], reads=[r_kst])
                S.dma_start("sp", KbTs[:, :, t0:t0 + T].rearrange("c p t -> p c t"), bst[:, :, :T], reads=[r_bst])
                if t0 >= NH:
                    qst, r_qst = qT_st[i % 2]
                    S.dma_start("sp", QTs[:, :, t0 - NH:t0 - NH + T].rearrange("c p t -> p c t"), qst[:, :, :T], reads=[r_qst])

            n = len(stiles)
            subs = [p2_info(i, j) for i in range(n) for j in range((stiles[i][1] + 127) // 128)]
            p2_load(0)
            for j in range(4):
                for fn in p2_norm_steps(0)[j]:
                    fn()
            if n > 1:
                p2_load(1)
            p2_mm(subs[0])
            nsteps = {}
            for si, sd in enumerate(subs):
                i = sd["i"]
                if sd["j"] == 0:
                    nsteps = p2_norm_steps(i + 1) if i + 1 < n else {}
                nsub_i = (stiles[i][1] + 127) // 128
                js = [sd["j"]] if sd["j"] + 1 < nsub_i else list(range(sd["j"], 4))
                for j in js:
                    for fn in nsteps.get(j, []):
                        fn()
                if sd["j"] == 0 and i + 2 < n:
                    p2_load(i + 2)
                p2_post_a(sd, si % NB)
                if si + 1 < len(subs):
                    p2_mm(subs[si + 1])
                p2_post_b(sd, si % NB)
                if si + 1 == len(subs) or subs[si + 1]["i"] != i:
                    p2_store(i)
            S.barrier()

        wg_f2 = sb(es, "f2wg_pre", [128, 8, DFF], BF16); R_wg_f2 = Res()
        with ExitStack() as ps:
            OT = sb(ps, "OT", [128, 8, NOS], BF16); R_OT = Res()
            for (c0, c1) in ((0, 704), (704, 1408), (1408, 2112), (2112, DFF)):
                S.dma_start("pool", wg_f2[:, :, c0:c1], w_f2g[:, c0:c1].rearrange("(k p) c -> p k c", p=128), writes=[R_wg_f2])
            w_out = sb(ps, "w_out", [128, 8, D], BF16); R_wo = Res()
            for k in range(8):
                S.dma_start("pool", w_out[:, k, :], w_out_d[k * 128:(k + 1) * 128, :], writes=[R_wo])
            pa = ExitStack()
            QT = [(sb(pa, "a_qt%d" % i, [128, NO], BF16), Res()) for i in range(2)]
            KT = [(sb(pa, "a_kt%d" % i, [128, NH + NO], BF16), Res()) for i in range(2)]
            NVT = 17 + 20 + 32
            VT = [(sb(pa, "a_vt%d" % i, [128, NVT, 192], BF16), {1: Res(), 4: Res(), 16: Res()}) for i in range(2)]
            PB = [(sb(pa, "a_pb%d" % i, [128, 512], BF16), Res()) for i in range(4)]
            rec = sb(pa, "a_rec", [128, NO], F32); R_rec = Res()

            def tiles_for(dils):
                out = []
                idx = 0
                for d in dils:
                    for r in range(d):
                        for kb in range(16 // d - 1, 32 // d):
                            ci0 = max(128 * kb, NH // d)
                            ci1 = min(128 * (kb + 2), (NH + NO) // d)
                            nq = ci1 - ci0
                            out.append(dict(d=d, r=r, kb=kb, idx=idx, halo=(kb < 16 // d), nq=nq,
                                            q0=r + d * ci0 - NH, moff=ci0 - 128 * kb, k0=r + d * 128 * kb))
                            idx += 1
                return out

            tilesA = tiles_for((1, 4, 16))
            tilesB = tiles_for((1,))

            def job_load(job):
                qt, rq = QT[job % 2]; kt, rk = KT[job % 2]; vt, rvs = VT[job % 2]
                S.dma_start("sp", qt[:], QTs[job, :, :], writes=[rq])
                if job < 4:
                    S.dma_start("sp", kt[:], KaTs[job, :, :], writes=[rk])
                    src, tl = Va_s[job], tilesA
                else:
                    g = (job - 4) // 2
                    S.dma_start("sp", kt[:], KbTs[g, :, :], writes=[rk])
                    src, tl = Vb_s[g], tilesB
                seen = {}
                for t in tl:
                    seen.setdefault((t["d"], t["r"]), []).append(t)
                for (d, r), ts in seen.items():
                    n = len(ts)
                    k0 = ts[0]["k0"]
                    S.dma_start("sp", vt[:, ts[0]["idx"]:ts[0]["idx"] + n, :],
                                src[k0:k0 + d * 128 * (n - 1) + d * 127 + 1:d, :].rearrange("(j i) c -> i j c", i=128),
                                writes=[rvs[d]])

            def build_packs(tl):
                packs, cur, cols = [], [], 0
                for t in tl:
                    if cur and (cols + t["nq"] > 512 or cur[0][0]["d"] != t["d"]):
                        packs.append(cur)
                        cur, cols = [], 0
                    cur.append((t, cols))
                    cols += t["nq"]
                if cur:
                    packs.append(cur)
                return packs

            packsA = build_packs(tilesA)
            packsB = build_packs(tilesB)
            work = []
            for job in range(8):
                pk = packsA if job < 4 else packsB
                for hh in range(2):
                    for pi, p in enumerate(pk):
                        work.append((job, hh, pi, p, pi == len(pk) - 1))
            started = {}

            def stage_abc(w, slot):
                job, hh, pi, p, last = w
                qt, rq = QT[job % 2]; kt, rk = KT[job % 2]
                hb = 64 * hh
                st_ps, r_st = PS[4 + slot], RPS[4 + slot]
                pbt, rpb = PB[slot]
                ncols = p[-1][1] + p[-1][0]["nq"]
                for ti, (t, a) in enumerate(p):
                    d, nq, q0, k0 = t["d"], t["nq"], t["q0"], t["k0"]
                    mm(st_ps[:, a:a + nq], kt[hb:hb + 64, k0:k0 + d * 127 + 1:d], qt[hb:hb + 64, q0:q0 + d * (nq - 1) + 1:d],
                       True, True, [rq, rk], [r_st], ti == len(p) - 1)
                S.op("act", lambda e: e.activation(out=pbt[:, :ncols], in_=st_ps[:, :ncols], func=AF.Exp, scale=SCALE),
                     reads=[r_st], writes=[rpb])
                sig = [(t["moff"] + (256 if t["halo"] else 0), t["nq"]) for (t, a) in p]
                if len(p) == 2 and sig == [(0, 256), (0, 256)]:
                    S.op("dve", lambda e: e.tensor_tensor(out=pbt[:, :512].rearrange("p (a c) -> p a c", a=2), in0=pbt[:, :512].rearrange("p (a c) -> p a c", a=2),
                                                          in1=masks[:, 0:256].unsqueeze(1).to_broadcast([128, 2, 256]), op=ALU.mult),
                         reads=[rpb, R_const], writes=[rpb])
                elif len(p) == 4 and sig == [(384, 128), (0, 128)] * 2:
                    S.op("dve", lambda e: e.tensor_tensor(out=pbt[:, :512].rearrange("p (a c) -> p a c", a=2), in0=pbt[:, :512].rearrange("p (a c) -> p a c", a=2),
                                                          in1=masks[:, 512:768].unsqueeze(1).to_broadcast([128, 2, 256]), op=ALU.mult),
                         reads=[rpb, R_const], writes=[rpb])
                else:
                    for (t, a), (mo, nq) in zip(p, sig):
                        S.op("dve", lambda e, a=a, mo=mo, nq=nq: e.tensor_tensor(out=pbt[:, a:a + nq], in0=pbt[:, a:a + nq], in1=masks[:, mo:mo + nq], op=ALU.mult),
                             reads=[rpb, R_const], writes=[rpb])

            def stage_d(w, slot):
                job, hh, pi, p, last = w
                vt, rvs = VT[job % 2]
                pbt, rpb = PB[slot]
                if pi == 0:
                    for bank in range(4):
                        started[bank] = False
                allsegs = []
                for (t, a) in p:
                    d, nq, q0 = t["d"], t["nq"], t["q0"]
                    lw = vt[:, t["idx"], 0:128] if hh == 0 else vt[:, t["idx"], 64:192]
                    if d == 1:
                        assert q0 % 4 == 0 and nq % 4 == 0
                        for j in range(4):
                            allsegs.append((lw, pbt[:, a + j:a + nq:4], j, PS[j][:, q0 // 4:q0 // 4 + nq // 4], d))
                    elif d == 4:
                        r4 = q0 % 4
                        c0 = q0 // 4
                        assert c0 + nq <= 512
                        allsegs.append((lw, pbt[:, a:a + nq], r4, PS[r4][:, c0:c0 + nq], d))
                    else:
                        assert d == 16
                        bank = q0 % 4
                        c0 = q0 // 4
                        assert c0 + 4 * (nq - 1) < 512
                        allsegs.append((lw, pbt[:, a:a + nq], bank, PS[bank][:, c0:c0 + 4 * (nq - 1) + 1:4], d))
                for si, (lw, rhs_ap, bank, out_ap, dd) in enumerate(allsegs):
                    first = not started[bank]
                    started[bank] = True
                    mm(out_ap, lw, rhs_ap, first, True, [rvs[dd], rpb], [RPS[bank]], si == len(allsegs) - 1)
                if last:
                    ob, db = (0, 64) if hh == 0 else (64, 0)
                    c = job
                    for bank in range(4):
                        cs = slice(bank * 512, (bank + 1) * 512)
                        if job >= 4:
                            hq = 2 * (job - 4) + hh
                            S.op("act", lambda e, bank=bank, cs=cs, hq=hq: e.activation(
                                out=rec[db:db + 64, cs], in_=PS[bank][db:db + 64, :], func=AF.Ln, bias=esink[db:db + 64, hq:hq + 1]),
                                reads=[RPS[bank], R_const], writes=[R_rec])
                        else:
                            S.op("act", lambda e, bank=bank, cs=cs: e.activation(
                                out=rec[db:db + 64, cs], in_=PS[bank][db:db + 64, :], func=AF.Ln),
                                reads=[RPS[bank]], writes=[R_rec])
                    S.op("act", lambda e: e.activation(out=rec[db:db + 64, :], in_=rec[db:db + 64, :], func=AF.Exp, scale=-1.0),
                         reads=[R_rec], writes=[R_rec])
                    for bank in range(4):
                        cs = slice(bank * 512, (bank + 1) * 512)
                        S.op("dve", lambda e, bank=bank, cs=cs: e.tensor_tensor(
                            out=OT[ob:ob + 64, c, 0:NO].rearrange("p (x f) -> p f x", f=4)[:, bank, :], in0=PS[bank][ob:ob + 64, :],
                            in1=rec[db:db + 64, cs], op=ALU.mult), reads=[RPS[bank], R_rec], writes=[R_OT])

            LOOK = 2
            job_load(0)
            nw = len(work)
            for x in range(0, nw + LOOK, 2):
                for y in (x, x + 1):
                    if y < nw:
                        stage_abc(work[y], y % 4)
                for y in (x - LOOK, x - LOOK + 1):
                    if 0 <= y < nw:
                        w = work[y]
                        stage_d(w, y % 4)
                        if w[1] == 0 and w[2] == 0 and w[0] + 1 < 8:
                            job_load(w[0] + 1)
            S.barrier()
            pa.close()
            pb_ = ExitStack()
            sel = sb(pb_, "sel", [16, 16 * 128], F32)
            selT = sb(pb_, "selT", [128, 16 * 16], F32)
            R_sel = Res()
            S.dma_start("sp", sel[:], sel_d, writes=[R_sel])
            S.dma_start("sp", selT[:], selT_d, writes=[R_sel])
            KS = [(sb(pb_, "s_k%d" % i, [128, 3, 512], F32), Res()) for i in range(2)]
            VS = [(sb(pb_, "s_v%d" % i, [128, 3, 512], F32), Res()) for i in range(2)]
            KBS = [(sb(pb_, "s_kb%d" % i, [128, 128], F32), Res()) for i in range(2)]
            VBS = [(sb(pb_, "s_vb%d" % i, [128, 128], F32), Res()) for i in range(2)]
            prod = [(sb(pb_, "s_pr%d" % i, [128, 512], F32), Res()) for i in range(2)]
            sc = [(sb(pb_, "s_sc%d" % i, [128, 32], F32), Res()) for i in range(2)]
            ee = [(sb(pb_, "s_ee%d" % i, [128, 32], F32), Res()) for i in range(2)]
            pv = [(sb(pb_, "s_pv%d" % i, [128, 512], F32), Res()) for i in range(3)]
            tk = sb(pb_, "s_tk", [16, 1024], F32); R_tk = Res()
            snew = sb(pb_, "s_new", [16, 16], F32); R_snew = Res()
            den = sb(pb_, "s_den", [16, 16], F32); R_den = Res()
            osb = sb(pb_, "s_osb", [16, 1024], F32); R_osb = Res()
            pats = [(1920, 1), (1536, 4), (0, 16)]

            def s_load(b):
                k_, rk = KS[b % 2]; v_, rv = VS[b % 2]
                for pi, (r0, st) in enumerate(pats):
                    lo = r0 + (st if st > 1 else 0)
                    lo = 2048 - 128 * st
                    S.dma_start("sp", k_[:, pi, :], cak[b, lo:lo + st * 127 + 1:st, :], writes=[rk])
                    S.dma_start("sp", v_[:, pi, :], cav[b, lo:lo + st * 127 + 1:st, :], writes=[rv])
                S.dma_start("sp", KBS[b % 2][0][:], cbk[b, :, :], writes=[KBS[b % 2][1]])
                S.dma_start("sp", VBS[b % 2][0][:], cbv[b, :, :], writes=[VBS[b % 2][1]])

            pvc = [0]

            def s_compute(b):
                k_, rk = KS[b % 2]; v_, rv = VS[b % 2]
                kb_, rkb = KBS[b % 2]; vb_, rvb = VBS[b % 2]
                s_, rs_ = sc[b % 2]; e_, re_ = ee[b % 2]
                bqa, r_bqa = PS[4 + 2 * (b % 2)], RPS[4 + 2 * (b % 2)]
                bqb, r_bqb = PS[5 + 2 * (b % 2)], RPS[5 + 2 * (b % 2)]
                for pi in range(3):
                    pr, rp = prod[pi % 2]
                    S.op("dve", lambda e, pi=pi, pr=pr: e.tensor_tensor(out=pr[:], in0=k_[:, pi, :], in1=bqa[:, :], op=ALU.mult),
                         reads=[rk, r_bqa], writes=[rp])
                    S.op("dve", lambda e, pi=pi, pr=pr: e.tensor_reduce(out=s_[:, pi * 8:(pi + 1) * 8], in_=pr[:].rearrange("p (h d) -> p h d", d=64),
                                                                         axis=AX.X, op=ALU.add), reads=[rp], writes=[rs_])
                pr, rp = prod[1]
                kb4 = kb_[:].rearrange("p (g d) -> p g d", d=64).unsqueeze(2).to_broadcast([128, 2, 4, 64])
                S.op("dve", lambda e: e.tensor_tensor(out=pr[:].rearrange("p (g j d) -> p g j d", g=2, j=4), in0=bqb[:, :].rearrange("p (g j d) -> p g j d", g=2, j=4),
                                                      in1=kb4, op=ALU.mult), reads=[rkb, r_bqb], writes=[rp])
                S.op("dve", lambda e: e.tensor_reduce(out=s_[:, 24:32], in_=pr[:].rearrange("p (h d) -> p h d", d=64), axis=AX.X, op=ALU.add),
                     reads=[rp], writes=[rs_])
                S.op("act", lambda e: e.activation(out=e_[:], in_=s_[:], func=AF.Exp, scale=SCALE), reads=[rs_], writes=[re_])
                first = (b == 0)
                last = (b == NS - 1)
                for pi in range(3):
                    p_, rpv = pv[pvc[0] % 3]; pvc[0] += 1
                    eb = e_[:, pi * 8:(pi + 1) * 8].unsqueeze(2).to_broadcast([128, 8, 64])
                    S.op("pool", lambda e, pi=pi, p_=p_, eb=eb: e.tensor_tensor(out=p_[:].rearrange("p (h d) -> p h d", d=64),
                                                                               in0=v_[:, pi, :].rearrange("p (h d) -> p h d", d=64), in1=eb, op=ALU.mult),
                         reads=[rv, re_], writes=[rpv])
                    mm(PS[0][:16, :], selT[:, b * 16:(b + 1) * 16], p_[:], first and pi == 0, last and pi == 2, [R_sel, rpv], [RPS[0]], True)
                p_, rpv = pv[pvc[0] % 3]; pvc[0] += 1
                eb = e_[:, 24:32].unsqueeze(2).to_broadcast([128, 8, 64])
                vb4 = vb_[:].rearrange("p (g d) -> p g d", d=64).unsqueeze(2).to_broadcast([128, 2, 4, 64])
                S.op("pool", lambda e: e.tensor_tensor(out=p_[:].rearrange("p (g j d) -> p g j d", g=2, j=4), in0=e_[:, 24:32].rearrange("p (g j) -> p g j", g=2).unsqueeze(3).to_broadcast([128, 2, 4, 64]),
                                                       in1=vb4, op=ALU.mult), reads=[rvb, re_], writes=[rpv])
                mm(PS[1][:16, :], selT[:, b * 16:(b + 1) * 16], p_[:], first, last, [R_sel, rpv], [RPS[1]], True)
                mm(PS[2][:16, 0:32], selT[:, b * 16:(b + 1) * 16], e_[:], first, last, [R_sel, re_], [RPS[2]], True)

            def s_bcast(b):
                mm(PS[4 + 2 * (b % 2)][:, :], sel[:, b * 128:(b + 1) * 128], zs_qa[:, :], True, True, [R_sel, R_zs], [RPS[4 + 2 * (b % 2)]], True)
                mm(PS[5 + 2 * (b % 2)][:, :], sel[:, b * 128:(b + 1) * 128], zs_qb[:, :], True, True, [R_sel, R_zs], [RPS[5 + 2 * (b % 2)]], True)

            s_load(0)
            s_bcast(0)
            for b in range(NS):
                if b + 1 < NS:
                    s_load(b + 1)
                    s_bcast(b + 1)
                s_compute(b)
            S.op("dve", lambda e: e.tensor_tensor(out=tk[:, 0:512], in0=zs_qa[:], in1=zs_ka[:], op=ALU.mult), reads=[R_zs], writes=[R_tk])
            S.op("dve", lambda e: e.tensor_tensor(out=tk[:, 512:1024].rearrange("p (g j d) -> p g j d", g=2, j=4),
                                                  in0=zs_qb[:].rearrange("p (g j d) -> p g j d", g=2, j=4),
                                                  in1=zs_kb[:].rearrange("p (g d) -> p g d", d=64).unsqueeze(2).to_broadcast([16, 2, 4, 64]), op=ALU.mult),
                 reads=[R_zs], writes=[R_tk])
            S.op("dve", lambda e: e.tensor_reduce(out=snew[:], in_=tk[:].rearrange("p (h d) -> p h d", d=64), axis=AX.X, op=ALU.add),
                 reads=[R_tk], writes=[R_snew])
            S.op("act", lambda e: e.activation(out=snew[:], in_=snew[:], func=AF.Exp, scale=SCALE), reads=[R_snew], writes=[R_snew])
            S.op("dve", lambda e: e.tensor_scalar(out=snew[:, 0:8], in0=snew[:, 0:8], scalar1=3.0, scalar2=None, op0=ALU.mult),
                 reads=[R_snew], writes=[R_snew])
            S.op("dve", lambda e: e.tensor_tensor(out=den[:, 0:8], in0=PS[2][:16, 0:8], in1=snew[:, 0:8], op=ALU.add), reads=[RPS[2], R_snew], writes=[R_den])
            S.op("dve", lambda e: e.tensor_tensor(out=den[:, 0:8], in0=PS[2][:16, 8:16], in1=den[:, 0:8], op=ALU.add), reads=[RPS[2], R_den], writes=[R_den])
            S.op("dve", lambda e: e.tensor_tensor(out=den[:, 0:8], in0=PS[2][:16, 16:24], in1=den[:, 0:8], op=ALU.add), reads=[RPS[2], R_den], writes=[R_den])
            S.op("dve", lambda e: e.tensor_tensor(out=den[:, 8:16], in0=PS[2][:16, 24:32], in1=snew[:, 8:16], op=ALU.add), reads=[RPS[2], R_snew], writes=[R_den])
            S.op("dve", lambda e: e.tensor_tensor(out=den[:, 8:16], in0=den[:, 8:16], in1=esink[:16, :], op=ALU.add), reads=[R_den, R_const], writes=[R_den])
            S.op("dve", lambda e: e.reciprocal(out=den[:], in_=den[:]), reads=[R_den], writes=[R_den])
            S.op("dve", lambda e: e.tensor_tensor(out=tk[:, 0:512].rearrange("p (h d) -> p h d", d=64), in0=zs_va[:].rearrange("p (h d) -> p h d", d=64),
                                                  in1=snew[:, 0:8].unsqueeze(2).to_broadcast([16, 8, 64]), op=ALU.mult), reads=[R_zs, R_snew], writes=[R_tk])
            S.op("dve", lambda e: e.tensor_tensor(out=tk[:, 512:1024].rearrange("p (g j d) -> p g j d", g=2, j=4),
                                                  in0=zs_vb[:].rearrange("p (g d) -> p g d", d=64).unsqueeze(2).to_broadcast([16, 2, 4, 64]),
                                                  in1=snew[:, 8:16].rearrange("p (g j) -> p g j", g=2).unsqueeze(3).to_broadcast([16, 2, 4, 64]), op=ALU.mult),
                 reads=[R_zs, R_snew, R_tk], writes=[R_tk])
            S.op("dve", lambda e: e.tensor_tensor(out=osb[:, 0:512], in0=PS[0][:16, :], in1=tk[:, 0:512], op=ALU.add), reads=[RPS[0], R_tk], writes=[R_osb])
            S.op("dve", lambda e: e.tensor_tensor(out=osb[:, 512:1024], in0=PS[1][:16, :], in1=tk[:, 512:1024], op=ALU.add), reads=[RPS[1], R_tk], writes=[R_osb])
            S.op("dve", lambda e: e.tensor_tensor(out=osb[:].rearrange("p (h d) -> p h d", d=64), in0=osb[:].rearrange("p (h d) -> p h d", d=64),
                                                  in1=den[:].unsqueeze(2).to_broadcast([16, 16, 64]), op=ALU.mult), reads=[R_osb, R_den], writes=[R_osb])
            for c in range(8):
                S.op("pe", lambda e, c=c: e.transpose(PS[4][:, c * 16:(c + 1) * 16], osb[:, c * 128:(c + 1) * 128], identf[:16, :16]),
                     reads=[R_osb, R_const], writes=[RPS[4]], signal=(c == 7))
            S.op("act", lambda e: e.activation(out=OT[:, :, NO:NOS], in_=PS[4][:, 0:128].rearrange("p (c t) -> p c t", t=16), func=AF.Copy),
                 reads=[RPS[4]], writes=[R_OT])

            S.barrier()
            pb_.close()
            TT = 128
            XH = [(sb(ps, "o_x%d" % i, [128, 8, TT], F32), Res()) for i in range(3)]
            sq = sb(ps, "o_sq", [128, 8, TT], BF16); R_sq = Res()
            rs = sb(ps, "o_rs", [128, TT], F32); R_rs = Res()
            TMP = [(sb(ps, "o_tmp%d" % i, [128, TT], F32), Res()) for i in range(2)]
            otiles = [(t0, min(TT, NOS - t0)) for t0 in range(0, NOS, TT)]
            no_ = len(otiles)

            def o_bank(i, c):
                bk = 2 * (i % 3) + c // 4
                return PS[bk], RPS[bk], (c % 4) * 128

            def o_load(i):
                t0, T = otiles[i]
                xt, xr = XH[i % 3]
                S.dma_start("sp", xt[:, :, :T], h1T[:, NH + t0:NH + t0 + T].rearrange("(c p) t -> p c t", p=128), writes=[xr])

            def o_mm(i):
                t0, T = otiles[i]
                for c in range(8):
                    bank, rb, off = o_bank(i, c)
                    for k in range(8):
                        mm(bank[:, off:off + T], w_out[:, k, c * 128:(c + 1) * 128], OT[:, k, t0:t0 + T], k == 0, k == 7,
                           [R_wo, R_OT], [rb], k == 7)

            def o_post(i):
                t0, T = otiles[i]
                xt, xr = XH[i % 3]
                for b in range(2):
                    bk = 2 * (i % 3) + b
                    S.op("act", lambda e, b=b, bk=bk: e.activation(out=sq[:, 4 * b:4 * b + 4, :T], in_=PS[bk][:].rearrange("p (a t) -> p a t", a=4)[:, :, :T],
                                                                    func=AF.Square), reads=[RPS[bk]], writes=[R_sq])
                rstd_from_sq(sq, R_sq, T, PS[6], RPS[6], rs, R_rs)
                for c in range(8):
                    bank, rb, off = o_bank(i, c)
                    tm, rt = TMP[c % 2]
                    S.op("dve", lambda e, c=c, bank=bank, off=off, tm=tm: e.scalar_tensor_tensor(
                        out=tm[:, :T], in0=bank[:, off:off + T], scalar=gcol(G_MIXPOST, c), in1=rs[:, :T],
                        op0=ALU.mult, op1=ALU.mult), reads=[rb, R_rs, R_const], writes=[rt])
                    S.op("pool", lambda e, c=c, tm=tm, xt=xt: e.tensor_tensor(out=xt[:, c, :T], in0=xt[:, c, :T], in1=tm[:, :T], op=ALU.add),
                         reads=[rt, xr], writes=[xr])
                S.dma_start("pool", h2T[:, t0:t0 + T].rearrange("(c p) t -> p c t", p=128), xt[:, :, :T], reads=[xr])

            o_load(0)
            if no_ > 1:
                o_load(1)
            o_mm(0)
            for i in range(no_):
                if i + 2 < no_:
                    o_load(i + 2)
                if i + 1 < no_:
                    o_mm(i + 1)
                o_post(i)
            S.barrier()

        ffn_phase("f2", w_f2g, w_f2u, w_f2d, G_F2PRE, G_F2POST, h2T, 0, h3T, 0, NOS, wg_pre=(wg_f2, R_wg_f2))

        with ExitStack() as ps:
            wpg = sb(ps, "wpg", [128, 8, D], BF16); wpp = sb(ps, "wpp", [128, 2, D], BF16); R_w = Res()
            for k in range(8):
                S.dma_start("pool", wpg[:, k, :], w_pg_d[k * 128:(k + 1) * 128, :], writes=[R_w])
            for k in range(2):
                S.dma_start("pool", wpp[:, k, :], w_pp_d[k * 128:(k + 1) * 128, :], writes=[R_w])
            TT = 256
            X = [(sb(ps, "e_x%d" % i, [128, 8, TT], F32), Res()) for i in range(2)]
            Pt = [(sb(ps, "e_p%d" % i, [128, 2, TT], BF16), Res()) for i in range(2)]
            u = sb(ps, "e_u", [128, 8, TT], BF16); R_u = Res()
            yb = sb(ps, "e_y", [128, 8, TT], F32); R_y = Res()
            sg = [(sb(ps, "e_sg%d" % i, [128, TT], F32), Res()) for i in range(2)]
            sq = sb(ps, "e_sq", [128, 8, TT], BF16); R_sq = Res()
            rs = sb(ps, "e_rs", [128, TT], F32); R_rs = Res()
            TMP = [(sb(ps, "e_tmp%d" % i, [128, TT], F32), Res()) for i in range(2)]
            etiles = [(t0, min(TT, NOS - t0)) for t0 in range(0, NOS, TT)]
            X3 = X + [(sb(ps, "e_x2", [128, 8, TT], F32), Res())]
            U2 = [(u, R_u), (sb(ps, "e_u1", [128, 8, TT], BF16), Res())]
            rs2 = sb(ps, "e_rs2", [128, TT], F32); R_rs2 = Res()

            def e_load(i):
                t0, T = etiles[i]
                xt, xr = X3[i % 3]
                pt, rp = Pt[i % 2]
                S.dma_start("sp", xt[:, :, :T], h3T[:, t0:t0 + T].rearrange("(c p) t -> p c t", p=128), writes=[xr])
                S.dma_start("pool", pt[:, :, :T], pT[:, t0:t0 + T].rearrange("(c p) t -> p c t", p=128), writes=[rp])

            def e_pre(i):
                t0, T = etiles[i]
                xt, xr = X3[i % 3]
                uu, ur = U2[i % 2]
                S.op("act", lambda e: e.activation(out=sq[:, :, :T], in_=xt[:, :, :T], func=AF.Square), reads=[xr], writes=[R_sq])
                rstd_from_sq(sq, R_sq, T, PS[6], RPS[6], rs, R_rs)
                for c in range(8):
                    S.op("dve", lambda e, c=c: e.scalar_tensor_tensor(out=uu[:, c, :T], in0=xt[:, c, :T], scalar=gcol(G_PLEPRE, c), in1=rs[:, :T],
                                                                       op0=ALU.mult, op1=ALU.mult), reads=[xr, R_rs, R_const], writes=[ur])

            def e_gate(i):
                t0, T = etiles[i]
                uu, ur = U2[i % 2]
                pt, rp = Pt[i % 2]
                for c in range(8):
                    pb, rpb = PS[c % 4], RPS[c % 4]
                    s_, rs_ = sg[c % 2]
                    for k in range(8):
                        mm(pb[:, 0:T], wpg[:, k, c * 128:(c + 1) * 128], uu[:, k, :T], k == 0, k == 7, [R_w, ur], [rpb], False)
                    for k in range(2):
                        mm(pb[:, 256:256 + T], wpp[:, k, c * 128:(c + 1) * 128], pt[:, k, :T], k == 0, k == 1, [R_w, rp], [rpb], k == 1)
                    S.op("act", lambda e, pb=pb, s_=s_: e.activation(out=s_[:, :T], in_=pb[:, 0:T], func=AF.Sigmoid), reads=[rpb], writes=[rs_])
                    S.op("dve", lambda e, c=c, pb=pb, s_=s_: e.tensor_tensor(out=yb[:, c, :T], in0=s_[:, :T], in1=pb[:, 256:256 + T], op=ALU.mult),
                         reads=[rs_, rpb], writes=[R_y])

            def e_post(i):
                t0, T = etiles[i]
                xt, xr = X3[i % 3]
                S.op("act", lambda e: e.activation(out=sq[:, :, :T], in_=yb[:, :, :T], func=AF.Square), reads=[R_y], writes=[R_sq])
                rstd_from_sq(sq, R_sq, T, PS[6], RPS[6], rs2, R_rs2)
                for c in range(8):
                    tm, rt = TMP[c % 2]
                    S.op("dve", lambda e, c=c, tm=tm: e.scalar_tensor_tensor(out=tm[:, :T], in0=yb[:, c, :T], scalar=gcol(G_PLEPOST, c), in1=rs2[:, :T],
                                                                            op0=ALU.mult, op1=ALU.mult), reads=[R_y, R_rs2, R_const], writes=[rt])
                    S.op("pool", lambda e, c=c, tm=tm, xt=xt: e.tensor_tensor(out=xt[:, c, :T], in0=xt[:, c, :T], in1=tm[:, :T], op=ALU.add),
                         reads=[rt, xr], writes=[xr])
                S.dma_start("pool", yT[:, t0:t0 + T].rearrange("(c p) t -> p c t", p=128), xt[:, :, :T], reads=[xr])

            ne = len(etiles)
            e_load(0)
            if ne > 1:
                e_load(1)
            e_pre(0)
            for i in range(ne):
                e_gate(i)
                if i + 2 < ne:
                    e_load(i + 2)
                if i + 1 < ne:
                    e_pre(i + 1)
                e_post(i)
            S.barrier(final=True)
    return nc


_NC_CACHE = {}


def _host_inputs(inp):
    f32 = np.float32
    xp = np.asarray(inp["x_prompt"], f32)
    xs = np.asarray(inp["x_sample"], f32)[:, 0, :]
    pp = np.asarray(inp["p_prompt"], f32)[0]
    psm = np.asarray(inp["p_sample"], f32)[0][:, 0, :]
    names = ["norm_f1_pre", "norm_f1_post", "norm_mix_pre", "norm_mix_post", "norm_f2_pre", "norm_f2_post",
             "norm_ple_pre", "norm_ple_post"]
    gains = np.zeros((128, 64), f32)
    for n, nm in enumerate(names):
        g = np.asarray(inp[nm], f32)[0]
        gains[:, n * 8:(n + 1) * 8] = g.reshape(8, 128).T
    ident = np.eye(128, dtype=f32)
    kk = np.arange(128)[:, None]
    qq = np.arange(128)[None, :]
    m_cur = (kk <= qq).astype(f32)
    m_next = (kk >= qq).astype(f32)
    m_own = np.concatenate([m_cur, m_next], axis=1)
    sel = np.zeros((16, 16, 128), f32)
    selT = np.zeros((128, 16, 16), f32)
    for b in range(16):
        sel[b, b, :] = 1.0
        selT[:, b, b] = 1.0
    sinks = np.broadcast_to(np.asarray(inp["sinks_b"], f32)[0][None, :], (128, 8)).copy()
    shared = {
        "gains": gains, "ident": ident, "sel": sel.reshape(16, -1), "selT": selT.reshape(128, -1), "sinks": sinks,
    }
    for nm in ["w_f1_gate", "w_f1_up", "w_f1_down", "w_f2_gate", "w_f2_up", "w_f2_down", "w_in", "w_out",
               "w_ple_gate", "w_ple_proj"]:
        shared[nm] = np.ascontiguousarray(np.asarray(inp[nm], f32)[0])
    cak = np.asarray(inp["cache_a_k"], f32)[0].reshape(128, 2048, 512)
    cav = np.asarray(inp["cache_a_v"], f32)[0].reshape(128, 2048, 512)
    cbk = np.asarray(inp["cache_b_k"], f32)[0].reshape(128, 128, 128)
    cbv = np.asarray(inp["cache_b_v"], f32)[0].reshape(128, 128, 128)
    maps = []
    for c in range(NCORES):
        bb, j = c // 4, c % 4
        s = j * NO
        xT = np.zeros((D, NT), f32)
        if j > 0:
            xT[:, 0:NH] = xp[bb, s - NH:s, :].T
        xT[:, NH:NH + NO] = xp[bb, s:s + NO, :].T
        xT[:, NH + NO:] = xs[c * NS:(c + 1) * NS, :].T
        pT = np.zeros((256, NOS), f32)
        pT[:, :NO] = pp[bb, s:s + NO, :].T
        pT[:, NO:] = psm[c * NS:(c + 1) * NS, :].T
        pos = np.concatenate([np.arange(s - NH, s + NO), np.full(128, PAST)]).astype(np.int64)
        cs, sn = _rope_tables(pos)
        cos_t = cs.reshape(33, 128, 32).transpose(1, 0, 2).reshape(128, -1)
        sin_t = sn.reshape(33, 128, 32).transpose(1, 0, 2).reshape(128, -1)
        hp = 1.0 if j > 0 else 0.0
        masks = np.concatenate([m_own, m_own * hp, m_next * hp, m_cur], axis=1).astype(f32)
        m = dict(shared)
        m.update({
            "xT": xT, "pT": pT, "cos_t": np.ascontiguousarray(cos_t), "sin_t": np.ascontiguousarray(sin_t), "masks": masks,
            "cache_a_k": np.ascontiguousarray(cak[c * NS:(c + 1) * NS]), "cache_a_v": np.ascontiguousarray(cav[c * NS:(c + 1) * NS]),
            "cache_b_k": np.ascontiguousarray(cbk[c * NS:(c + 1) * NS]), "cache_b_v": np.ascontiguousarray(cbv[c * NS:(c + 1) * NS]),
        })
        maps.append(m)
    return maps


def kernel(**inputs):
    if "nc" not in _NC_CACHE:
        _NC_CACHE["nc"] = build_program()
    nc = _NC_CACHE["nc"]
    maps = _host_inputs(inputs)
    res = run_bass_kernel_spmd(nc, maps, core_ids=list(range(NCORES)))
    R = res.results
    f32 = np.float32
    y_prompt = np.zeros((2, 8192, D), f32)
    y_sample = np.zeros((128, 1, D), f32)
    nak_p = np.zeros((1, 2, 2048, 8, 64), f32); nav_p = np.zeros((1, 2, 2048, 8, 64), f32)
    nbk_p = np.zeros((1, 2, 128, 2, 64), f32); nbv_p = np.zeros((1, 2, 128, 2, 64), f32)
    nak_s = np.zeros((1, 128, 2048, 8, 64), f32); nav_s = np.zeros((1, 128, 2048, 8, 64), f32)
    nbk_s = np.zeros((1, 128, 128, 2, 64), f32); nbv_s = np.zeros((1, 128, 128, 2, 64), f32)
    for c in range(NCORES):
        bb, j = c // 4, c % 4
        r = R[c]
        yT = np.asarray(r["yT"], f32)
        y_prompt[bb, j * NO:(j + 1) * NO, :] = yT[:, :NO].T
        y_sample[c * NS:(c + 1) * NS, 0, :] = yT[:, NO:].T
        if j == 3:
            nak_p[0, bb] = np.asarray(r["ka_o"], f32).reshape(2048, 8, 64)
            nav_p[0, bb] = np.asarray(r["va_o"], f32).reshape(2048, 8, 64)
            nbk_p[0, bb] = np.asarray(r["kb_o"], f32)[NO - 128:].reshape(128, 2, 64)
            nbv_p[0, bb] = np.asarray(r["vb_o"], f32)[NO - 128:].reshape(128, 2, 64)
        nak_s[0, c * NS:(c + 1) * NS] = np.asarray(r["nak_s"], f32).reshape(NS, 2048, 8, 64)
        nav_s[0, c * NS:(c + 1) * NS] = np.asarray(r["nav_s"], f32).reshape(NS, 2048, 8, 64)
        nbk_s[0, c * NS:(c + 1) * NS] = np.asarray(r["nbk_s"], f32).reshape(NS, 128, 2, 64)
        nbv_s[0, c * NS:(c + 1) * NS] = np.asarray(r["nbv_s"], f32).reshape(NS, 128, 2, 64)
    return (y_prompt, y_sample, nak_p, nav_p, nbk_p, nbv_p, nak_s, nav_s, nbk_s, nbv_s)
```

```python
import numpy as np
import concourse.bass as bass
import concourse.mybir as mybir
from concourse.bass_utils import run_bass_kernel_spmd
from contextlib import ExitStack

F32, BF16 = mybir.dt.float32, mybir.dt.bfloat16
AF = mybir.ActivationFunctionType
ALU = mybir.AluOpType
AX = mybir.AxisListType

D = 1024
DFF = 2816
FC = 22
NH = 2048
NO = 2048
NS = 16
NOS = NO + NS
NT = NH + NOS
INW = 2304
EPS = 1e-6
SCALE = 0.125
PAST = 16384
NCORES = 8
G_F1PRE, G_F1POST, G_MIXPRE, G_MIXPOST, G_F2PRE, G_F2POST, G_PLEPRE, G_PLEPOST = range(8)

DEBUG = False


class Res:
    __slots__ = ("w", "r")

    def __init__(self):
        self.w = {}
        self.r = {}


class Sched:
    def __init__(self, nc, es, n_dma_sems=40):
        self.nc = nc
        self.sems = []
        self.E = {}
        for name, eng in (("pe", nc.tensor), ("act", nc.scalar), ("dve", nc.vector),
                          ("pool", nc.gpsimd), ("sp", nc.sync)):
            sem = es.enter_context(nc.semaphore("s_" + name))
            self.sems.append(sem)
            self.E[name] = dict(eng=eng, key=len(self.sems) - 1, cnt=0, waited={}, name=name)
        self.dma = []
        for i in range(n_dma_sems):
            sem = es.enter_context(nc.semaphore("s_dma%d" % i))
            self.sems.append(sem)
            self.dma.append(dict(key=len(self.sems) - 1, cnt=0))
        self.dma_rr = 0
        self.big = dict(key=None, cnt=0)
        sem = es.enter_context(nc.semaphore("s_big"))
        self.sems.append(sem)
        self.big["key"] = len(self.sems) - 1

    def _wait(self, e, deps):
        for k, v in deps.items():
            if v <= 0 or e["waited"].get(k, 0) >= v:
                continue
            e["eng"].wait_ge(self.sems[k], v)
            e["waited"][k] = v

    def _deps(self, e, reads, writes):
        deps = {}
        own = e["key"]
        for r in reads:
            for k, v in r.w.items():
                if k == own and e["name"] == "pe":
                    continue
                if deps.get(k, 0) < v:
                    deps[k] = v
        for w in writes:
            for k, v in list(w.w.items()) + list(w.r.items()):
                if k == own:
                    continue
                if deps.get(k, 0) < v:
                    deps[k] = v
        return deps

    def op(self, ename, fn, reads=(), writes=(), signal=True):
        e = self.E[ename]
        self._wait(e, self._deps(e, reads, writes))
        ins = fn(e["eng"])
        if signal:
            e["cnt"] += 1
            ins.then_inc(self.sems[e["key"]], 1)
            val = e["cnt"]
        else:
            val = e["cnt"] + 1
        k = e["key"]
        for r in reads:
            if r.r.get(k, 0) < val:
                r.r[k] = val
        for w in writes:
            w.w[k] = val
            w.r = {}
        return ins

    def dma_start(self, qname, out, in_, reads=(), writes=(), big=False):
        q = self.E[qname]
        deps = self._deps(q, reads, writes)
        if big:
            s = self.big
        else:
            s = self.dma[self.dma_rr]
            self.dma_rr = (self.dma_rr + 1) % len(self.dma)
            if s["cnt"] > 0:
                deps[s["key"]] = max(deps.get(s["key"], 0), s["cnt"])
        self._wait(q, deps)
        ins = q["eng"].dma_start(out=out, in_=in_)
        s["cnt"] += 16
        ins.then_inc(self.sems[s["key"]], 16)
        k = s["key"]
        for r in reads:
            if r.r.get(k, 0) < s["cnt"]:
                r.r[k] = s["cnt"]
        for w in writes:
            w.w[k] = s["cnt"]
            w.r = {}
        return ins

    def barrier(self, final=False):
        tot = {}
        for e in self.E.values():
            tot[e["key"]] = e["cnt"]
        for s in self.dma + ([self.big] if final else []):
            tot[s["key"]] = s["cnt"]
        for e in self.E.values():
            d = dict(tot)
            d.pop(e["key"], None)
            self._wait(e, d)


def _rope_tables(pos):
    inv = np.power(np.float32(10000.0), -np.arange(32, dtype=np.float32) * np.float32(2.0) / np.float32(64.0)).astype(np.float32)
    ang = pos.astype(np.float32)[:, None] * inv[None, :]
    return np.cos(ang).astype(np.float32), np.sin(ang).astype(np.float32)


def build_program():
    nc = bass.Bass("TRN2", target_bir_lowering=False)

    def din(name, shape, dt=F32):
        return nc.dram_tensor(name, list(shape), dt, kind="ExternalInput").ap()

    def dout(name, shape, dt=F32):
        return nc.dram_tensor(name, list(shape), dt, kind="ExternalOutput").ap()

    def dscr(name, shape, dt):
        kind = "ExternalOutput" if DEBUG else "Internal"
        return nc.dram_tensor(name, list(shape), dt, kind=kind).ap()

    xT = din("xT", [D, NT])
    pT = din("pT", [256, NOS])
    gains_d = din("gains", [128, 64])
    cos_d = din("cos_t", [128, 33 * 32])
    sin_d = din("sin_t", [128, 33 * 32])
    mask_d = din("masks", [128, 768])
    ident_d = din("ident", [128, 128])
    sel_d = din("sel", [16, 16 * 128])
    selT_d = din("selT", [128, 16 * 16])
    sinks_d = din("sinks", [128, 8])
    w_f1g = din("w_f1_gate", [D, DFF]); w_f1u = din("w_f1_up", [D, DFF]); w_f1d = din("w_f1_down", [DFF, D])
    w_f2g = din("w_f2_gate", [D, DFF]); w_f2u = din("w_f2_up", [D, DFF]); w_f2d = din("w_f2_down", [DFF, D])
    w_in_d = din("w_in", [D, INW]); w_out_d = din("w_out", [D, D])
    w_pg_d = din("w_ple_gate", [D, D]); w_pp_d = din("w_ple_proj", [256, D])
    cak = din("cache_a_k", [NS, 2048, 512]); cav = din("cache_a_v", [NS, 2048, 512])
    cbk = din("cache_b_k", [NS, 128, 128]); cbv = din("cache_b_v", [NS, 128, 128])

    yT = dout("yT", [D, NOS])
    ka_o = dout("ka_o", [NO, 512]); va_o = dout("va_o", [NO, 512])
    kb_o = dout("kb_o", [NO, 128]); vb_o = dout("vb_o", [NO, 128])
    nak_s = dout("nak_s", [NS, 2048, 512]); nav_s = dout("nav_s", [NS, 2048, 512])
    nbk_s = dout("nbk_s", [NS, 128, 128]); nbv_s = dout("nbv_s", [NS, 128, 128])

    h1T = dscr("h1T", [D, NT], F32)
    h2T = dscr("h2T", [D, NOS], F32)
    h3T = dscr("h3T", [D, NOS], F32)
    QTs = dscr("QTs", [8, 128, NO], BF16)
    KaTs = dscr("KaTs", [4, 128, NH + NO], BF16)
    KbTs = dscr("KbTs", [2, 128, NH + NO], BF16)
    Va_s = dscr("Va_s", [4, NH + NO, 192], BF16)
    Vb_s = dscr("Vb_s", [2, NH + NO, 192], BF16)

    with ExitStack() as es:
        S = Sched(nc, es)

        def sb(stack, name, shape, dt):
            return stack.enter_context(nc.sbuf_tensor("sb_" + name, list(shape), dt))

        ones = sb(es, "ones", [128, 128], BF16)
        ident = sb(es, "ident", [128, 128], BF16)
        identf = sb(es, "identf", [128, 128], F32)
        gains = sb(es, "gains", [128, 64], F32)
        gains_h = sb(es, "gains_h", [128, 64], F32)
        masks = sb(es, "masks", [128, 768], BF16)
        esink = sb(es, "esink", [128, 8], F32)
        zs_qa = sb(es, "zs_qa", [16, 512], F32); zs_ka = sb(es, "zs_ka", [16, 512], F32)
        zs_va = sb(es, "zs_va", [16, 512], F32); zs_qb = sb(es, "zs_qb", [16, 512], F32)
        zs_kb = sb(es, "zs_kb", [16, 128], F32); zs_vb = sb(es, "zs_vb", [16, 128], F32)
        R_const = Res(); R_zs = Res(); R_ostok = Res()
        PS = []
        RPS = []
        for i in range(8):
            PS.append(es.enter_context(nc.psum_tensor("ps%d" % i, [128, 512], F32)))
            RPS.append(Res())

        S.op("dve", lambda e: e.memset(ones[:], 1.0), writes=[R_const])
        S.dma_start("sp", gains[:], gains_d, writes=[R_const])
        S.dma_start("pool", masks[:], mask_d, writes=[R_const])
        S.dma_start("pool", ident[:], ident_d, writes=[R_const])
        S.dma_start("sp", identf[:], ident_d, writes=[R_const])
        S.dma_start("sp", esink[:], sinks_d, writes=[R_const])
        S.op("dve", lambda e: e.tensor_scalar(out=gains_h[:], in0=gains[:], scalar1=0.5, scalar2=None, op0=ALU.mult),
             reads=[R_const], writes=[R_const])
        S.op("act", lambda e: e.activation(out=esink[:], in_=esink[:], func=AF.Exp), reads=[R_const], writes=[R_const])

        R_cache_out = Res()

        def flat16(ap):
            return ap.rearrange("r c -> (r c)").rearrange("(a b x) -> a b x", a=16, b=32)
        bg_dmas = []
        for b in range(NS):
            bg_dmas.append(lambda b=b: S.dma_start("sp", flat16(nak_s[b, 0:2047, :]), flat16(cak[b, 1:2048, :]), writes=[R_cache_out], big=True))
            bg_dmas.append(lambda b=b: S.dma_start("sp", flat16(nav_s[b, 0:2047, :]), flat16(cav[b, 1:2048, :]), writes=[R_cache_out], big=True))
        bg_dmas.append(lambda: S.dma_start("sp", nbk_s[:, 0:127, :], cbk[:, 1:128, :], writes=[R_cache_out], big=True))
        bg_dmas.append(lambda: S.dma_start("sp", nbv_s[:, 0:127, :], cbv[:, 1:128, :], writes=[R_cache_out], big=True))

        def mm(out, lhsT, rhs, start, stop, reads, writes, signal):
            return S.op("pe", lambda e: e.matmul(out, lhsT, rhs, start=start, stop=stop),
                        reads=reads, writes=writes, signal=signal)

        def gcol(n, c, half=False):
            t = gains_h if half else gains
            return t[:, n * 8 + c:n * 8 + c + 1]

        def rstd_from_sq(sq, R_sq, T, psn, R_psn, rstd, R_rstd):
            for c in range(8):
                mm(psn[:, :T], ones[:], sq[:, c, :T], c == 0, c == 7, [R_sq, R_const], [R_psn], c == 7)
            S.op("dve", lambda e: e.tensor_scalar(out=rstd[:, :T], in0=psn[:, :T], scalar1=1.0 / D, scalar2=EPS,
                                                  op0=ALU.mult, op1=ALU.add), reads=[R_psn], writes=[R_rstd])
            S.op("act", lambda e: e.activation(out=rstd[:, :T], in_=rstd[:, :T], func=AF.Sqrt), reads=[R_rstd], writes=[R_rstd])
            S.op("dve", lambda e: e.reciprocal(out=rstd[:, :T], in_=rstd[:, :T]), reads=[R_rstd], writes=[R_rstd])

        def ffn_phase(tag, wg_d, wu_d, wd_d, g_pre, g_post, src, src_off, dst, dst_off, ncols, bg_per_tile=0, wg_pre=None):
            TT = 256
            tiles = [(t0, min(TT, ncols - t0)) for t0 in range(0, ncols, TT)]
            with ExitStack() as ps:
                wg = wg_pre[0] if wg_pre is not None else sb(ps, tag + "wg", [128, 8, DFF], BF16)
                R_wgpre = wg_pre[1] if wg_pre is not None else Res()
                wu = sb(ps, tag + "wu", [128, 8, DFF], BF16)
                wd = sb(ps, tag + "wd", [128, FC, D], BF16)
                FG = [(0, 2), (2, 6), (6, 12), (12, 22)]
                R_wgu = [Res() for _ in FG]
                R_wd = [Res() for _ in FG]
                fgrp = {}
                for gi, (f0, f1) in enumerate(FG):
                    for f in range(f0, f1):
                        fgrp[f] = gi

                def load_weights():
                    for gi, (f0, f1) in enumerate(FG):
                        c0, c1 = f0 * 128, f1 * 128
                        if wg_pre is None:
                            S.dma_start("pool", wg[:, :, c0:c1], wg_d[:, c0:c1].rearrange("(k p) c -> p k c", p=128), writes=[R_wgu[gi]])
                        S.dma_start("pool", wu[:, :, c0:c1], wu_d[:, c0:c1].rearrange("(k p) c -> p k c", p=128), writes=[R_wgu[gi]])
                    for gi, (f0, f1) in enumerate(FG):
                        S.dma_start("pool", wd[:, f0:f1, :],
                                    wd_d[f0 * 128:f1 * 128, :].rearrange("(f p) c -> p f c", p=128), writes=[R_wd[gi]])
                X = [(sb(ps, tag + "x%d" % i, [128, 8, TT], F32), Res()) for i in range(3)]
                U = [(sb(ps, tag + "u%d" % i, [128, 8, TT], BF16), Res()) for i in range(2)]
                hid = sb(ps, tag + "hid", [128, FC, TT], BF16); R_hid = Res()
                sq = sb(ps, tag + "sq", [128, 8, TT], BF16); R_sq = Res()
                RS = [(sb(ps, tag + "rs%d" % i, [128, TT], F32), Res()) for i in range(2)]
                SG = [(sb(ps, tag + "sg%d" % i, [128, TT], F32), Res()) for i in range(3)]
                TMP = [(sb(ps, tag + "tmp%d" % i, [128, TT], F32), Res()) for i in range(2)]
                psn, R_psn = PS[6], RPS[6]

                def load(i):
                    t0, T = tiles[i]
                    xt, xr = X[i % 3]
                    S.dma_start("sp", xt[:, :, :T],
                                src[:, src_off + t0:src_off + t0 + T].rearrange("(c p) t -> p c t", p=128), writes=[xr])

                GUB = [4, 5, 7]

                def prenorm_steps(i):
                    t0, T = tiles[i]
                    xt, xr = X[i % 3]
                    u, ur = U[i % 2]
                    rs, rr = RS[0]

                    def a():
                        S.op("act", lambda e: e.activation(out=sq[:, :, :T], in_=xt[:, :, :T], func=AF.Square),
                             reads=[xr], writes=[R_sq])

                    def b():
                        for c in range(8):
                            mm(psn[:, :T], ones[:], sq[:, c, :T], c == 0, c == 7, [R_sq, R_const], [R_psn], c == 7)
                        S.op("dve", lambda e: e.tensor_scalar(out=rs[:, :T], in0=psn[:, :T], scalar1=1.0 / D, scalar2=EPS,
                                                              op0=ALU.mult, op1=ALU.add), reads=[R_psn], writes=[rr])
                        S.op("act", lambda e: e.activation(out=rs[:, :T], in_=rs[:, :T], func=AF.Sqrt), reads=[rr], writes=[rr])

                    def c_():
                        S.op("dve", lambda e: e.reciprocal(out=rs[:, :T], in_=rs[:, :T]), reads=[rr], writes=[rr])

                    def d(cs):
                        for c in cs:
                            S.op("dve", lambda e, c=c: e.scalar_tensor_tensor(
                                out=u[:, c, :T], in0=xt[:, c, :T], scalar=gcol(g_pre, c), in1=rs[:, :T],
                                op0=ALU.mult, op1=ALU.mult), reads=[xr, rr, R_const], writes=[ur])

                    return [(14, a), (16, b), (18, c_), (19, lambda: d(range(0, 3))), (20, lambda: d(range(3, 6))), (21, lambda: d(range(6, 8)))]

                def gateup(i, hooks):
                    t0, T = tiles[i]
                    u, ur = U[i % 2]
                    for f in range(FC):
                        for fn in hooks.get(f, []):
                            fn()
                        bkn = GUB[f % 3]
                        pb, rpb = PS[bkn], RPS[bkn]
                        sg, rsg = SG[f % 3]
                        for k in range(8):
                            mm(pb[:, 0:T], wg[:, k, f * 128:(f + 1) * 128], u[:, k, :T], k == 0, k == 7,
                               [R_wgu[fgrp[f]], R_wgpre, ur], [rpb], False)
                        for k in range(8):
                            mm(pb[:, 256:256 + T], wu[:, k, f * 128:(f + 1) * 128], u[:, k, :T], k == 0, k == 7,
                               [R_wgu[fgrp[f]], ur], [rpb], k == 7)
                        S.op("act", lambda e: e.activation(out=sg[:, :T], in_=pb[:, 0:T], func=AF.Silu),
                             reads=[rpb], writes=[rsg])
                        S.op("dve", lambda e, f=f: e.tensor_tensor(out=hid[:, f, :T], in0=sg[:, :T], in1=pb[:, 256:256 + T],
                                                                    op=ALU.mult), reads=[rsg, rpb], writes=[R_hid])

                def down_mm(i):
                    t0, T = tiles[i]
                    for c in range(8):
                        bank, rb = PS[c // 2], RPS[c // 2]
                        off = (c % 2) * 256
                        for f in range(FC):
                            mm(bank[:, off:off + T], wd[:, f, c * 128:(c + 1) * 128], hid[:, f, :T], f == 0, f == FC - 1,
                               [R_wd[fgrp[f]], R_hid], [rb], f == FC - 1)
                    for b in range(4):
                        S.op("act", lambda e, b=b: e.activation(
                            out=sq[:, 2 * b:2 * b + 2, :T], in_=PS[b][:].rearrange("p (a t) -> p a t", a=2)[:, :, :T],
                            func=AF.Square), reads=[RPS[b]], writes=[R_sq])

                def post_steps(i):
                    t0, T = tiles[i]
                    xt, xr = X[i % 3]
                    rs, rr = RS[1]

                    def a():
                        for c in range(8):
                            mm(psn[:, :T], ones[:], sq[:, c, :T], c == 0, c == 7, [R_sq, R_const], [R_psn], c == 7)
                        S.op("dve", lambda e: e.tensor_scalar(out=rs[:, :T], in0=psn[:, :T], scalar1=1.0 / D, scalar2=EPS,
                                                              op0=ALU.mult, op1=ALU.add), reads=[R_psn], writes=[rr])
                        S.op("act", lambda e: e.activation(out=rs[:, :T], in_=rs[:, :T], func=AF.Sqrt), reads=[rr], writes=[rr])

                    def b():
                        S.op("dve", lambda e: e.reciprocal(out=rs[:, :T], in_=rs[:, :T]), reads=[rr], writes=[rr])

                    def cstep(c):
                        bank, rb = PS[c // 2], RPS[c // 2]
                        off = (c % 2) * 256
                        tm, rt = TMP[c % 2]
                        S.op("dve", lambda e: e.scalar_tensor_tensor(
                            out=tm[:, :T], in0=bank[:, off:off + T], scalar=gcol(g_post, c, True), in1=rs[:, :T],
                            op0=ALU.mult, op1=ALU.mult), reads=[rb, rr, R_const], writes=[rt])
                        S.op("pool", lambda e: e.tensor_tensor(out=xt[:, c, :T], in0=xt[:, c, :T], in1=tm[:, :T],
                                                               op=ALU.add), reads=[rt, xr], writes=[xr])

                    def st():
                        S.dma_start("pool", dst[:, dst_off + t0:dst_off + t0 + T].rearrange("(c p) t -> p c t", p=128),
                                    xt[:, :, :T], reads=[xr])

                    steps = [(2, a), (4, b)]
                    for c in range(8):
                        steps.append((5 + c, (lambda c=c: cstep(c))))
                    steps.append((13, st))
                    return steps

                n = len(tiles)
                load(0)
                load_weights()
                if n > 1:
                    load(1)
                for (_, fn) in prenorm_steps(0):
                    fn()
                for i in range(n):
                    hooks = {}
                    if i >= 1:
                        for (f, fn) in post_steps(i - 1):
                            hooks.setdefault(f, []).append(fn)
                    if i + 1 < n:
                        for (f, fn) in prenorm_steps(i + 1):
                            hooks.setdefault(f, []).append(fn)
                    gateup(i, hooks)
                    if i + 2 < n:
                        load(i + 2)
                    for _ in range(bg_per_tile):
                        if bg_dmas:
                            bg_dmas.pop(0)()
                    down_mm(i)
                for (_, fn) in post_steps(n - 1):
                    fn()
                S.barrier()

        ffn_phase("f1", w_f1g, w_f1u, w_f1d, G_F1PRE, G_F1POST, xT, 0, h1T, 0, NT, bg_per_tile=2)
        while bg_dmas:
            bg_dmas.pop(0)()

        with ExitStack() as ps:
            w_in = sb(ps, "w_in", [128, 8, INW], BF16); R_w = Res()
            for k in range(8):
                S.dma_start("pool", w_in[:, k, :], w_in_d[k * 128:(k + 1) * 128, :], writes=[R_w])
            cosT = sb(ps, "cosT", [128, 33, 32], F32); sinT = sb(ps, "sinT", [128, 33, 32], F32); R_tab = Res()
            S.dma_start("sp", cosT[:].rearrange("p a b -> p (a b)"), cos_d, writes=[R_tab])
            S.dma_start("sp", sinT[:].rearrange("p a b -> p (a b)"), sin_d, writes=[R_tab])
            ST = 512
            X = [(sb(ps, "p2x%d" % i, [128, 8, ST], F32), Res()) for i in range(2)]
            U = [(sb(ps, "p2u%d" % i, [128, 8, ST], BF16), Res()) for i in range(2)]
            sq = sb(ps, "p2sq", [128, 8, ST], BF16); R_sq = Res()
            rs = sb(ps, "p2rs", [128, ST], F32); R_rs = Res()
            NB = 2
            NT4 = 4
            T4 = [[(sb(ps, "p2t%d_%d" % (i, j), [128, 8, 32], F32), Res()) for j in range(4)] for i in range(NT4)]
            t4_ctr = [0]
            ka_f = [(sb(ps, "ka_f%d" % i, [128, 512], F32), Res()) for i in range(NB)]
            va_f = [(sb(ps, "va_f%d" % i, [128, 512], F32), Res()) for i in range(NB)]
            kvb_f = [(sb(ps, "kvb_f%d" % i, [128, 256], F32), Res()) for i in range(NB)]
            q_b = [(sb(ps, "q_b%d" % i, [128, 1024], BF16), Res()) for i in range(NB)]
            ka_b = [(sb(ps, "ka_b%d" % i, [128, 512], BF16), Res()) for i in range(NB)]
            kbdup = [(sb(ps, "kbdup%d" % i, [128, 2, 2, 64], BF16), Res()) for i in range(NB)]
            vaug = [(sb(ps, "vaug%d" % i, [128, 4, 192], BF16), Res()) for i in range(NB)]
            vbaug = [(sb(ps, "vbaug%d" % i, [128, 2, 192], BF16), Res()) for i in range(NB)]
            qT_st = [(sb(ps, "qT_st%d" % i, [128, 8, ST], BF16), Res()) for i in range(2)]
            kaT_st = [(sb(ps, "kaT_st%d" % i, [128, 4, ST], BF16), Res()) for i in range(2)]
            kbT_st = [(sb(ps, "kbT_st%d" % i, [128, 2, ST], BF16), Res()) for i in range(2)]
            TA = PS[6][:].bitcast(BF16); R_TA = RPS[6]
            TB = PS[7][:].bitcast(BF16); R_TB = RPS[7]
            psn, R_psn = PS[5], RPS[5]
            for i in range(NB):
                S.op("dve", lambda e, i=i: e.memset(vaug[i][0][:], 1.0), writes=[vaug[i][1]])
                S.op("dve", lambda e, i=i: e.memset(vbaug[i][0][:], 1.0), writes=[vbaug[i][1]])

            stiles = [(t0, min(ST, NT - t0)) for t0 in range(0, NT, ST)]

            def p2_load(i):
                t0, T = stiles[i]
                xt, xr = X[i % 2]
                S.dma_start("sp", xt[:, :, :T], h1T[:, t0:t0 + T].rearrange("(c p) t -> p c t", p=128), writes=[xr])

            def p2_norm_steps(i):
                t0, T = stiles[i]
                xt, xr = X[i % 2]
                u, ur = U[i % 2]

                def a():
                    S.op("act", lambda e: e.activation(out=sq[:, :, :T], in_=xt[:, :, :T], func=AF.Square),
                         reads=[xr], writes=[R_sq])

                def b():
                    rstd_from_sq(sq, R_sq, T, psn, R_psn, rs, R_rs)

                def d(cs):
                    for c in cs:
                        S.op("dve", lambda e, c=c: e.scalar_tensor_tensor(
                            out=u[:, c, :T], in0=xt[:, c, :T], scalar=gcol(G_MIXPRE, c), in1=rs[:, :T],
                            op0=ALU.mult, op1=ALU.mult), reads=[xr, R_rs, R_const], writes=[ur])

                return {0: [a], 1: [b], 2: [lambda: d(range(0, 4))], 3: [lambda: d(range(4, 8))]}

            def rope(src, H, np_, ti, dst1, dst2, reads, wres, bi):
                t4 = T4[t4_ctr[0] % NT4]
                t4_ctr[0] += 1
                cb = cosT[:np_, ti, :].unsqueeze(1).to_broadcast([np_, H, 32])
                sn = sinT[:np_, ti, :].unsqueeze(1).to_broadcast([np_, H, 32])
                x1 = src[:, :, 0:32]
                x2 = src[:, :, 32:64]
                for j, (a, b_) in enumerate(((x1, cb), (x2, sn), (x2, cb), (x1, sn))):
                    S.op("dve", lambda e, j=j, a=a, b_=b_: e.tensor_tensor(out=t4[j][0][:np_, :H, :], in0=a, in1=b_, op=ALU.mult),
                         reads=reads + [R_tab], writes=[t4[j][1]])
                S.op("pool", lambda e: e.tensor_tensor(out=dst1, in0=t4[0][0][:np_, :H, :], in1=t4[1][0][:np_, :H, :],
                                                       op=ALU.subtract), reads=[t4[0][1], t4[1][1]], writes=[wres])
                S.op("pool", lambda e: e.tensor_tensor(out=dst2, in0=t4[2][0][:np_, :H, :], in1=t4[3][0][:np_, :H, :],
                                                       op=ALU.add), reads=[t4[2][1], t4[3][1]], writes=[wres])

            def v3(ap, H):
                return ap.rearrange("p (h d) -> p h d", d=64)

            def p2_info(i, j):
                t0, T = stiles[i]
                c0 = j * 128
                np_ = min(128, T - c0)
                g0 = t0 + c0
                return dict(i=i, j=j, t0=t0, T=T, c0=c0, np_=np_, g0=g0, is_sample=g0 >= NH + NO, is_halo=g0 < NH, ti=g0 // 128)

            def p2_mm(sd):
                i, c0, np_ = sd["i"], sd["c0"], sd["np_"]
                u, ur = U[i % 2]
                slices = [(512, 512, 1), (2048, 256, 4), (1024, 512, 2), (0, 512, 0), (1536, 512, 3)]
                for (s0, w, bk) in slices:
                    if sd["is_halo"] and bk in (0, 3):
                        continue
                    for k in range(8):
                        mm(PS[bk][:np_, :w], u[:, k, c0:c0 + np_], w_in[:, k, s0:s0 + w], k == 0, k == 7,
                           [ur, R_w], [RPS[bk]], k == 7)

            def p2_post_a(sd, bi):
                i, c0, np_, g0, ti = sd["i"], sd["c0"], sd["np_"], sd["g0"], sd["ti"]
                is_halo = sd["is_halo"]
                if sd["is_sample"]:
                    rope(v3(PS[1][:np_, :], 8), 8, np_, ti, v3(zs_ka[:np_, :], 8)[:, :, 0:32], v3(zs_ka[:np_, :], 8)[:, :, 32:64], [RPS[1]], R_zs, bi)
                    rope(v3(PS[4][:np_, 0:128], 2), 2, np_, ti, v3(zs_kb[:np_, :], 2)[:, :, 0:32], v3(zs_kb[:np_, :], 2)[:, :, 32:64], [RPS[4]], R_zs, bi)
                    rope(v3(PS[0][:np_, :], 8), 8, np_, ti, v3(zs_qa[:np_, :], 8)[:, :, 0:32], v3(zs_qa[:np_, :], 8)[:, :, 32:64], [RPS[0]], R_zs, bi)
                    rope(v3(PS[3][:np_, :], 8), 8, np_, ti, v3(zs_qb[:np_, :], 8)[:, :, 0:32], v3(zs_qb[:np_, :], 8)[:, :, 32:64], [RPS[3]], R_zs, bi)
                    S.op("act", lambda e: e.activation(out=zs_va[:np_, :], in_=PS[2][:np_, :], func=AF.Copy), reads=[RPS[2]], writes=[R_zs])
                    S.op("act", lambda e: e.activation(out=zs_vb[:np_, :], in_=PS[4][:np_, 128:256], func=AF.Copy), reads=[RPS[4]], writes=[R_zs])
                    S.dma_start("pool", nak_s[:, 2047, :], zs_ka[:np_, :], reads=[R_zs], writes=[R_cache_out])
                    S.dma_start("pool", nav_s[:, 2047, :], zs_va[:np_, :], reads=[R_zs], writes=[R_cache_out])
                    S.dma_start("pool", nbk_s[:, 127, :], zs_kb[:np_, :], reads=[R_zs], writes=[R_cache_out])
                    S.dma_start("pool", nbv_s[:, 127, :], zs_vb[:np_, :], reads=[R_zs], writes=[R_cache_out])
                    return
                kaf, r_kaf = ka_f[bi]; vaf, r_vaf = va_f[bi]; kvf, r_kvf = kvb_f[bi]
                qb_, r_qb = q_b[bi]; kab, r_kab = ka_b[bi]; kbd, r_kbd = kbdup[bi]
                vg, r_vg = vaug[bi]; vbg, r_vbg = vbaug[bi]
                rope(v3(PS[1][:, :], 8), 8, 128, ti, v3(kaf[:], 8)[:, :, 0:32], v3(kaf[:], 8)[:, :, 32:64], [RPS[1]], r_kaf, bi)
                S.op("act", lambda e: e.activation(out=kab[:], in_=kaf[:], func=AF.Copy), reads=[r_kaf], writes=[r_kab])
                rope(v3(PS[4][:, 0:128], 2), 2, 128, ti, v3(kvf[:, 0:128], 2)[:, :, 0:32], v3(kvf[:, 0:128], 2)[:, :, 32:64], [RPS[4]], r_kvf, bi)
                S.op("act", lambda e: e.activation(out=kvf[:, 128:256], in_=PS[4][:, 128:256], func=AF.Copy), reads=[RPS[4]], writes=[r_kvf])
                for dd in range(2):
                    S.op("act", lambda e, dd=dd: e.activation(out=kbd[:, :, dd, :], in_=v3(kvf[:, 0:128], 2), func=AF.Copy),
                         reads=[r_kvf], writes=[r_kbd])
                vb4 = vbg[:].rearrange("p a (b c) -> p a b c", c=64)
                for dd in (0, 2):
                    S.op("act", lambda e, dd=dd: e.activation(out=vb4[:, :, dd, :], in_=v3(PS[4][:, 128:256], 2), func=AF.Copy),
                         reads=[RPS[4]], writes=[r_vbg])
                va4 = vg[:].rearrange("p a (b c) -> p a b c", c=64)
                S.op("act", lambda e: e.activation(out=va4[:, :, 0:3:2, :], in_=PS[2][:, :].rearrange("p (a b c) -> p a b c", b=2, c=64),
                                                   func=AF.Copy), reads=[RPS[2]], writes=[r_vg])
                if not is_halo:
                    S.op("act", lambda e: e.activation(out=vaf[:], in_=PS[2][:, :], func=AF.Copy), reads=[RPS[2]], writes=[r_vaf])
                    rope(v3(PS[0][:, :], 8), 8, 128, ti, v3(qb_[:, 0:512], 8)[:, :, 0:32], v3(qb_[:, 0:512], 8)[:, :, 32:64], [RPS[0]], r_qb, bi)
                    rope(v3(PS[3][:, :], 8), 8, 128, ti, v3(qb_[:, 512:1024], 8)[:, :, 0:32], v3(qb_[:, 512:1024], 8)[:, :, 32:64], [RPS[3]], r_qb, bi)

            def p2_post_b(sd, bi):
                i, c0, g0 = sd["i"], sd["c0"], sd["g0"]
                is_halo = sd["is_halo"]
                if sd["is_sample"]:
                    return
                kaf, r_kaf = ka_f[bi]; vaf, r_vaf = va_f[bi]; kvf, r_kvf = kvb_f[bi]
                qb_, r_qb = q_b[bi]; kab, r_kab = ka_b[bi]; kbd, r_kbd = kbdup[bi]
                vg, r_vg = vaug[bi]; vbg, r_vbg = vbaug[bi]
                S.dma_start("sp", Va_s[:, g0:g0 + 128, :].rearrange("a t c -> t a c"), vg[:], reads=[r_vg])
                S.dma_start("sp", Vb_s[:, g0:g0 + 128, :].rearrange("a t c -> t a c"), vbg[:], reads=[r_vbg])
                for c in range(4):
                    S.op("pe", lambda e, c=c: e.transpose(TB[:, c * 128:(c + 1) * 128], kab[:, c * 128:(c + 1) * 128], ident[:]),
                         reads=[r_kab, R_const], writes=[R_TB], signal=False)
                for g in range(2):
                    S.op("pe", lambda e, g=g: e.transpose(TB[:, (4 + g) * 128:(5 + g) * 128],
                                                          kbd[:, g, :, :].rearrange("p a b -> p (a b)"), ident[:]),
                         reads=[r_kbd, R_const], writes=[R_TB], signal=(g == 1))
                kst, r_kst = kaT_st[i % 2]
                bst, r_bst = kbT_st[i % 2]
                S.op("act", lambda e: e.activation(out=kst[:, :, c0:c0 + 128], in_=TB[:, 0:512].rearrange("p (c t) -> p c t", t=128),
                                                   func=AF.Copy), reads=[R_TB], writes=[r_kst])
                S.op("act", lambda e: e.activation(out=bst[:, :, c0:c0 + 128], in_=TB[:, 512:768].rearrange("p (c t) -> p c t", t=128),
                                                   func=AF.Copy), reads=[R_TB], writes=[r_bst])
                if not is_halo:
                    o0 = g0 - NH
                    S.dma_start("sp", ka_o[o0:o0 + 128, :], kaf[:], reads=[r_kaf])
                    S.dma_start("sp", va_o[o0:o0 + 128, :], vaf[:], reads=[r_vaf])
                    S.dma_start("sp", kb_o[o0:o0 + 128, :], kvf[:, 0:128], reads=[r_kvf])
                    S.dma_start("sp", vb_o[o0:o0 + 128, :], kvf[:, 128:256], reads=[r_kvf])
                    for c in range(8):
                        S.op("pe", lambda e, c=c: e.transpose(TA[:, c * 128:(c + 1) * 128], qb_[:, c * 128:(c + 1) * 128], ident[:]),
                             reads=[r_qb, R_const], writes=[R_TA], signal=(c == 7))
                    qst, r_qst = qT_st[i % 2]
                    S.op("act", lambda e: e.activation(out=qst[:, :, c0:c0 + 128], in_=TA.rearrange("p (c t) -> p c t", t=128),
                                                       func=AF.Copy), reads=[R_TA], writes=[r_qst])

            def p2_store(i):
                t0, T = stiles[i]
                if t0 >= NH + NO:
                    return
                kst, r_kst = kaT_st[i % 2]
                bst, r_bst = kbT_st[i % 2]
                S.dma_start("sp", KaTs[:, :, t0:t0 + T].rearrange("c p t -> p c t"), kst[:, :, :T], reads=[r_kst])
                S.dma_start("sp", KbTs[:, :, t0:t0 + T].rearrange("c p t -> p c t"), bst[:, :, :T], reads=[r_bst])
                if t0 >= NH:
                    qst, r_qst = qT_st[i % 2]
                    S.dma_start("sp", QTs[:, :, t0 - NH:t0 - NH + T].rearrange("c p t -> p c t"), qst[:, :, :T], reads=[r_qst])

            n = len(stiles)
            subs = [p2_info(i, j) for i in range(n) for j in range((stiles[i][1] + 127) // 128)]
            p2_load(0)
            for j in range(4):
                for fn in p2_norm_steps(0)[j]:
                    fn()
            if n > 1:
                p2_load(1)
            p2_mm(subs[0])
            nsteps = {}
            for si, sd in enumerate(subs):
                i = sd["i"]
                if sd["j"] == 0:
                    nsteps = p2_norm_steps(i + 1) if i + 1 < n else {}
                nsub_i = (stiles[i][1] + 127) // 128
                js = [sd["j"]] if sd["j"] + 1 < nsub_i else list(range(sd["j"], 4))
                for j in js:
                    for fn in nsteps.get(j, []):
                        fn()
                if sd["j"] == 0 and i + 2 < n:
                    p2_load(i + 2)
                p2_post_a(sd, si % NB)
                if si + 1 < len(subs):
                    p2_mm(subs[si + 1])
                p2_post_b(sd, si % NB)
                if si + 1 == len(subs) or subs[si + 1]["i"] != i:
                    p2_store(i)
            S.barrier()

        wg_f2 = sb(es, "f2wg_pre", [128, 8, DFF], BF16); R_wg_f2 = Res()
        with ExitStack() as ps:
            OT = sb(ps, "OT", [128, 8, NOS], BF16); R_OT = Res()
            for (c0, c1) in ((0, 704), (704, 1408), (1408, 2112), (2112, DFF)):
                S.dma_start("pool", wg_f2[:, :, c0:c1], w_f2g[:, c0:c1].rearrange("(k p) c -> p k c", p=128), writes=[R_wg_f2])
            w_out = sb(ps, "w_out", [128, 8, D], BF16); R_wo = Res()
            for k in range(8):
                S.dma_start("pool", w_out[:, k, :], w_out_d[k * 128:(k + 1) * 128, :], writes=[R_wo])
            pa = ExitStack()
            QT = [(sb(pa, "a_qt%d" % i, [128, NO], BF16), Res()) for i in range(2)]
            KT = [(sb(pa, "a_kt%d" % i, [128, NH + NO], BF16), Res()) for i in range(2)]
            NVT = 17 + 20 + 32
            VT = [(sb(pa, "a_vt%d" % i, [128, NVT, 192], BF16), {1: Res(), 4: Res(), 16: Res()}) for i in range(2)]
            PB = [(sb(pa, "a_pb%d" % i, [128, 512], BF16), Res()) for i in range(4)]
            rec = sb(pa, "a_rec", [128, NO], F32); R_rec = Res()

            def tiles_for(dils):
                out = []
                idx = 0
                for d in dils:
                    for r in range(d):
                        for kb in range(16 // d - 1, 32 // d):
                            ci0 = max(128 * kb, NH // d)
                            ci1 = min(128 * (kb + 2), (NH + NO) // d)
                            nq = ci1 - ci0
                            out.append(dict(d=d, r=r, kb=kb, idx=idx, halo=(kb < 16 // d), nq=nq,
                                            q0=r + d * ci0 - NH, moff=ci0 - 128 * kb, k0=r + d * 128 * kb))
                            idx += 1
                return out

            tilesA = tiles_for((1, 4, 16))
            tilesB = tiles_for((1,))

            def job_load(job):
                qt, rq = QT[job % 2]; kt, rk = KT[job % 2]; vt, rvs = VT[job % 2]
                S.dma_start("sp", qt[:], QTs[job, :, :], writes=[rq])
                if job < 4:
                    S.dma_start("sp", kt[:], KaTs[job, :, :], writes=[rk])
                    src, tl = Va_s[job], tilesA
                else:
                    g = (job - 4) // 2
                    S.dma_start("sp", kt[:], KbTs[g, :, :], writes=[rk])
                    src, tl = Vb_s[g], tilesB
                seen = {}
                for t in tl:
                    seen.setdefault((t["d"], t["r"]), []).append(t)
                for (d, r), ts in seen.items():
                    n = len(ts)
                    k0 = ts[0]["k0"]
                    S.dma_start("sp", vt[:, ts[0]["idx"]:ts[0]["idx"] + n, :],
                                src[k0:k0 + d * 128 * (n - 1) + d * 127 + 1:d, :].rearrange("(j i) c -> i j c", i=128),
                                writes=[rvs[d]])

            def build_packs(tl):
                packs, cur, cols = [], [], 0
                for t in tl:
                    if cur and (cols + t["nq"] > 512 or cur[0][0]["d"] != t["d"]):
                        packs.append(cur)
                        cur, cols = [], 0
                    cur.append((t, cols))
                    cols += t["nq"]
                if cur:
                    packs.append(cur)
                return packs

            packsA = build_packs(tilesA)
            packsB = build_packs(tilesB)
            work = []
            for job in range(8):
                pk = packsA if job < 4 else packsB
                for hh in range(2):
                    for pi, p in enumerate(pk):
                        work.append((job, hh, pi, p, pi == len(pk) - 1))
            started = {}

            def stage_abc(w, slot):
                job, hh, pi, p, last = w
                qt, rq = QT[job % 2]; kt, rk = KT[job % 2]
                hb = 64 * hh
                st_ps, r_st = PS[4 + slot], RPS[4 + slot]
                pbt, rpb = PB[slot]
                ncols = p[-1][1] + p[-1][0]["nq"]
                for ti, (t, a) in enumerate(p):
                    d, nq, q0, k0 = t["d"], t["nq"], t["q0"], t["k0"]
                    mm(st_ps[:, a:a + nq], kt[hb:hb + 64, k0:k0 + d * 127 + 1:d], qt[hb:hb + 64, q0:q0 + d * (nq - 1) + 1:d],
                       True, True, [rq, rk], [r_st], ti == len(p) - 1)
                S.op("act", lambda e: e.activation(out=pbt[:, :ncols], in_=st_ps[:, :ncols], func=AF.Exp, scale=SCALE),
                     reads=[r_st], writes=[rpb])
                sig = [(t["moff"] + (256 if t["halo"] else 0), t["nq"]) for (t, a) in p]
                if len(p) == 2 and sig == [(0, 256), (0, 256)]:
                    S.op("dve", lambda e: e.tensor_tensor(out=pbt[:, :512].rearrange("p (a c) -> p a c", a=2), in0=pbt[:, :512].rearrange("p (a c) -> p a c", a=2),
                                                          in1=masks[:, 0:256].unsqueeze(1).to_broadcast([128, 2, 256]), op=ALU.mult),
                         reads=[rpb, R_const], writes=[rpb])
                elif len(p) == 4 and sig == [(384, 128), (0, 128)] * 2:
                    S.op("dve", lambda e: e.tensor_tensor(out=pbt[:, :512].rearrange("p (a c) -> p a c", a=2), in0=pbt[:, :512].rearrange("p (a c) -> p a c", a=2),
                                                          in1=masks[:, 512:768].unsqueeze(1).to_broadcast([128, 2, 256]), op=ALU.mult),
                         reads=[rpb, R_const], writes=[rpb])
                else:
                    for (t, a), (mo, nq) in zip(p, sig):
                        S.op("dve", lambda e, a=a, mo=mo, nq=nq: e.tensor_tensor(out=pbt[:, a:a + nq], in0=pbt[:, a:a + nq], in1=masks[:, mo:mo + nq], op=ALU.mult),
                             reads=[rpb, R_const], writes=[rpb])

            def stage_d(w, slot):
                job, hh, pi, p, last = w
                vt, rvs = VT[job % 2]
                pbt, rpb = PB[slot]
                if pi == 0:
                    for bank in range(4):
                        started[bank] = False
                allsegs = []
                for (t, a) in p:
                    d, nq, q0 = t["d"], t["nq"], t["q0"]
                    lw = vt[:, t["idx"], 0:128] if hh == 0 else vt[:, t["idx"], 64:192]
                    if d == 1:
                        assert q0 % 4 == 0 and nq % 4 == 0
                        for j in range(4):
                            allsegs.append((lw, pbt[:, a + j:a + nq:4], j, PS[j][:, q0 // 4:q0 // 4 + nq // 4], d))
                    elif d == 4:
                        r4 = q0 % 4
                        c0 = q0 // 4
                        assert c0 + nq <= 512
                        allsegs.append((lw, pbt[:, a:a + nq], r4, PS[r4][:, c0:c0 + nq], d))
                    else:
                        assert d == 16
                        bank = q0 % 4
                        c0 = q0 // 4
                        assert c0 + 4 * (nq - 1) < 512
                        allsegs.append((lw, pbt[:, a:a + nq], bank, PS[bank][:, c0:c0 + 4 * (nq - 1) + 1:4], d))
                for si, (lw, rhs_ap, bank, out_ap, dd) in enumerate(allsegs):
                    first = not started[bank]
                    started[bank] = True
                    mm(out_ap, lw, rhs_ap, first, True, [rvs[dd], rpb], [RPS[bank]], si == len(allsegs) - 1)
                if last:
                    ob, db = (0, 64) if hh == 0 else (64, 0)
                    c = job
                    for bank in range(4):
                        cs = slice(bank * 512, (bank + 1) * 512)
                        if job >= 4:
                            hq = 2 * (job - 4) + hh
                            S.op("act", lambda e, bank=bank, cs=cs, hq=hq: e.activation(
                                out=rec[db:db + 64, cs], in_=PS[bank][db:db + 64, :], func=AF.Ln, bias=esink[db:db + 64, hq:hq + 1]),
                                reads=[RPS[bank], R_const], writes=[R_rec])
                        else:
                            S.op("act", lambda e, bank=bank, cs=cs: e.activation(
                                out=rec[db:db + 64, cs], in_=PS[bank][db:db + 64, :], func=AF.Ln),
                                reads=[RPS[bank]], writes=[R_rec])
                    S.op("act", lambda e: e.activation(out=rec[db:db + 64, :], in_=rec[db:db + 64, :], func=AF.Exp, scale=-1.0),
                         reads=[R_rec], writes=[R_rec])
                    for bank in range(4):
                        cs = slice(bank * 512, (bank + 1) * 512)
                        S.op("dve", lambda e, bank=bank, cs=cs: e.tensor_tensor(
                            out=OT[ob:ob + 64, c, 0:NO].rearrange("p (x f) -> p f x", f=4)[:, bank, :], in0=PS[bank][ob:ob + 64, :],
                            in1=rec[db:db + 64, cs], op=ALU.mult), reads=[RPS[bank], R_rec], writes=[R_OT])

            LOOK = 2
            job_load(0)
            nw = len(work)
            for x in range(0, nw + LOOK, 2):
                for y in (x, x + 1):
                    if y < nw:
                        stage_abc(work[y], y % 4)
                for y in (x - LOOK, x - LOOK + 1):
                    if 0 <= y < nw:
                        w = work[y]
                        stage_d(w, y % 4)
                        if w[1] == 0 and w[2] == 0 and w[0] + 1 < 8:
                            job_load(w[0] + 1)
            S.barrier()
            pa.close()
            pb_ = ExitStack()
            sel = sb(pb_, "sel", [16, 16 * 128], F32)
            selT = sb(pb_, "selT", [128, 16 * 16], F32)
            R_sel = Res()
            S.dma_start("sp", sel[:], sel_d, writes=[R_sel])
            S.dma_start("sp", selT[:], selT_d, writes=[R_sel])
            KS = [(sb(pb_, "s_k%d" % i, [128, 3, 512], F32), Res()) for i in range(2)]
            VS = [(sb(pb_, "s_v%d" % i, [128, 3, 512], F32), Res()) for i in range(2)]
            KBS = [(sb(pb_, "s_kb%d" % i, [128, 128], F32), Res()) for i in range(2)]
            VBS = [(sb(pb_, "s_vb%d" % i, [128, 128], F32), Res()) for i in range(2)]
            prod = [(sb(pb_, "s_pr%d" % i, [128, 512], F32), Res()) for i in range(2)]
            sc = [(sb(pb_, "s_sc%d" % i, [128, 32], F32), Res()) for i in range(2)]
            ee = [(sb(pb_, "s_ee%d" % i, [128, 32], F32), Res()) for i in range(2)]
            pv = [(sb(pb_, "s_pv%d" % i, [128, 512], F32), Res()) for i in range(3)]
            tk = sb(pb_, "s_tk", [16, 1024], F32); R_tk = Res()
            snew = sb(pb_, "s_new", [16, 16], F32); R_snew = Res()
            den = sb(pb_, "s_den", [16, 16], F32); R_den = Res()
            osb = sb(pb_, "s_osb", [16, 1024], F32); R_osb = Res()
            pats = [(1920, 1), (1536, 4), (0, 16)]

            def s_load(b):
                k_, rk = KS[b % 2]; v_, rv = VS[b % 2]
                for pi, (r0, st) in enumerate(pats):
                    lo = r0 + (st if st > 1 else 0)
                    lo = 2048 - 128 * st
                    S.dma_start("sp", k_[:, pi, :], cak[b, lo:lo + st * 127 + 1:st, :], writes=[rk])
                    S.dma_start("sp", v_[:, pi, :], cav[b, lo:lo + st * 127 + 1:st, :], writes=[rv])
                S.dma_start("sp", KBS[b % 2][0][:], cbk[b, :, :], writes=[KBS[b % 2][1]])
                S.dma_start("sp", VBS[b % 2][0][:], cbv[b, :, :], writes=[VBS[b % 2][1]])

            pvc = [0]

            def s_compute(b):
                k_, rk = KS[b % 2]; v_, rv = VS[b % 2]
                kb_, rkb = KBS[b % 2]; vb_, rvb = VBS[b % 2]
                s_, rs_ = sc[b % 2]; e_, re_ = ee[b % 2]
                bqa, r_bqa = PS[4 + 2 * (b % 2)], RPS[4 + 2 * (b % 2)]
                bqb, r_bqb = PS[5 + 2 * (b % 2)], RPS[5 + 2 * (b % 2)]
                for pi in range(3):
                    pr, rp = prod[pi % 2]
                    S.op("dve", lambda e, pi=pi, pr=pr: e.tensor_tensor(out=pr[:], in0=k_[:, pi, :], in1=bqa[:, :], op=ALU.mult),
                         reads=[rk, r_bqa], writes=[rp])
                    S.op("dve", lambda e, pi=pi, pr=pr: e.tensor_reduce(out=s_[:, pi * 8:(pi + 1) * 8], in_=pr[:].rearrange("p (h d) -> p h d", d=64),
                                                                         axis=AX.X, op=ALU.add), reads=[rp], writes=[rs_])
                pr, rp = prod[1]
                kb4 = kb_[:].rearrange("p (g d) -> p g d", d=64).unsqueeze(2).to_broadcast([128, 2, 4, 64])
                S.op("dve", lambda e: e.tensor_tensor(out=pr[:].rearrange("p (g j d) -> p g j d", g=2, j=4), in0=bqb[:, :].rearrange("p (g j d) -> p g j d", g=2, j=4),
                                                      in1=kb4, op=ALU.mult), reads=[rkb, r_bqb], writes=[rp])
                S.op("dve", lambda e: e.tensor_reduce(out=s_[:, 24:32], in_=pr[:].rearrange("p (h d) -> p h d", d=64), axis=AX.X, op=ALU.add),
                     reads=[rp], writes=[rs_])
                S.op("act", lambda e: e.activation(out=e_[:], in_=s_[:], func=AF.Exp, scale=SCALE), reads=[rs_], writes=[re_])
                first = (b == 0)
                last = (b == NS - 1)
                for pi in range(3):
                    p_, rpv = pv[pvc[0] % 3]; pvc[0] += 1
                    eb = e_[:, pi * 8:(pi + 1) * 8].unsqueeze(2).to_broadcast([128, 8, 64])
                    S.op("pool", lambda e, pi=pi, p_=p_, eb=eb: e.tensor_tensor(out=p_[:].rearrange("p (h d) -> p h d", d=64),
                                                                               in0=v_[:, pi, :].rearrange("p (h d) -> p h d", d=64), in1=eb, op=ALU.mult),
                         reads=[rv, re_], writes=[rpv])
                    mm(PS[0][:16, :], selT[:, b * 16:(b + 1) * 16], p_[:], first and pi == 0, last and pi == 2, [R_sel, rpv], [RPS[0]], True)
                p_, rpv = pv[pvc[0] % 3]; pvc[0] += 1
                eb = e_[:, 24:32].unsqueeze(2).to_broadcast([128, 8, 64])
                vb4 = vb_[:].rearrange("p (g d) -> p g d", d=64).unsqueeze(2).to_broadcast([128, 2, 4, 64])
                S.op("pool", lambda e: e.tensor_tensor(out=p_[:].rearrange("p (g j d) -> p g j d", g=2, j=4), in0=e_[:, 24:32].rearrange("p (g j) -> p g j", g=2).unsqueeze(3).to_broadcast([128, 2, 4, 64]),
                                                       in1=vb4, op=ALU.mult), reads=[rvb, re_], writes=[rpv])
                mm(PS[1][:16, :], selT[:, b * 16:(b + 1) * 16], p_[:], first, last, [R_sel, rpv], [RPS[1]], True)
                mm(PS[2][:16, 0:32], selT[:, b * 16:(b + 1) * 16], e_[:], first, last, [R_sel, re_], [RPS[2]], True)

            def s_bcast(b):
                mm(PS[4 + 2 * (b % 2)][:, :], sel[:, b * 128:(b + 1) * 128], zs_qa[:, :], True, True, [R_sel, R_zs], [RPS[4 + 2 * (b % 2)]], True)
                mm(PS[5 + 2 * (b % 2)][:, :], sel[:, b * 128:(b + 1) * 128], zs_qb[:, :], True, True, [R_sel, R_zs], [RPS[5 + 2 * (b % 2)]], True)

            s_load(0)
            s_bcast(0)
            for b in range(NS):
                if b + 1 < NS:
                    s_load(b + 1)
                    s_bcast(b + 1)
                s_compute(b)
            S.op("dve", lambda e: e.tensor_tensor(out=tk[:, 0:512], in0=zs_qa[:], in1=zs_ka[:], op=ALU.mult), reads=[R_zs], writes=[R_tk])
            S.op("dve", lambda e: e.tensor_tensor(out=tk[:, 512:1024].rearrange("p (g j d) -> p g j d", g=2, j=4),
                                                  in0=zs_qb[:].rearrange("p (g j d) -> p g j d", g=2, j=4),
                                                  in1=zs_kb[:].rearrange("p (g d) -> p g d", d=64).unsqueeze(2).to_broadcast([16, 2, 4, 64]), op=ALU.mult),
                 reads=[R_zs], writes=[R_tk])
            S.op("dve", lambda e: e.tensor_reduce(out=snew[:], in_=tk[:].rearrange("p (h d) -> p h d", d=64), axis=AX.X, op=ALU.add),
                 reads=[R_tk], writes=[R_snew])
            S.op("act", lambda e: e.activation(out=snew[:], in_=snew[:], func=AF.Exp, scale=SCALE), reads=[R_snew], writes=[R_snew])
            S.op("dve", lambda e: e.tensor_scalar(out=snew[:, 0:8], in0=snew[:, 0:8], scalar1=3.0, scalar2=None, op0=ALU.mult),
                 reads=[R_snew], writes=[R_snew])
            S.op("dve", lambda e: e.tensor_tensor(out=den[:, 0:8], in0=PS[2][:16, 0:8], in1=snew[:, 0:8], op=ALU.add), reads=[RPS[2], R_snew], writes=[R_den])
            S.op("dve", lambda e: e.tensor_tensor(out=den[:, 0:8], in0=PS[2][:16, 8:16], in1=den[:, 0:8], op=ALU.add), reads=[RPS[2], R_den], writes=[R_den])
            S.op("dve", lambda e: e.tensor_tensor(out=den[:, 0:8], in0=PS[2][:16, 16:24], in1=den[:, 0:8], op=ALU.add), reads=[RPS[2], R_den], writes=[R_den])
            S.op("dve", lambda e: e.tensor_tensor(out=den[:, 8:16], in0=PS[2][:16, 24:32], in1=snew[:, 8:16], op=ALU.add), reads=[RPS[2], R_snew], writes=[R_den])
            S.op("dve", lambda e: e.tensor_tensor(out=den[:, 8:16], in0=den[:, 8:16], in1=esink[:16, :], op=ALU.add), reads=[R_den, R_const], writes=[R_den])
            S.op("dve", lambda e: e.reciprocal(out=den[:], in_=den[:]), reads=[R_den], writes=[R_den])
            S.op("dve", lambda e: e.tensor_tensor(out=tk[:, 0:512].rearrange("p (h d) -> p h d", d=64), in0=zs_va[:].rearrange("p (h d) -> p h d", d=64),
                                                  in1=snew[:, 0:8].unsqueeze(2).to_broadcast([16, 8, 64]), op=ALU.mult), reads=[R_zs, R_snew], writes=[R_tk])
            S.op("dve", lambda e: e.tensor_tensor(out=tk[:, 512:1024].rearrange("p (g j d) -> p g j d", g=2, j=4),
                                                  in0=zs_vb[:].rearrange("p (g d) -> p g d", d=64).unsqueeze(2).to_broadcast([16, 2, 4, 64]),
                                                  in1=snew[:, 8:16].rearrange("p (g j) -> p g j", g=2).unsqueeze(3).to_broadcast([16, 2, 4, 64]), op=ALU.mult),
                 reads=[R_zs, R_snew, R_tk], writes=[R_tk])
            S.op("dve", lambda e: e.tensor_tensor(out=osb[:, 0:512], in0=PS[0][:16, :], in1=tk[:, 0:512], op=ALU.add), reads=[RPS[0], R_tk], writes=[R_osb])
            S.op("dve", lambda e: e.tensor_tensor(out=osb[:, 512:1024], in0=PS[1][:16, :], in1=tk[:, 512:1024], op=ALU.add), reads=[RPS[1], R_tk], writes=[R_osb])
            S.op("dve", lambda e: e.tensor_tensor(out=osb[:].rearrange("p (h d) -> p h d", d=64), in0=osb[:].rearrange("p (h d) -> p h d", d=64),
                                                  in1=den[:].unsqueeze(2).to_broadcast([16, 16, 64]), op=ALU.mult), reads=[R_osb, R_den], writes=[R_osb])
            for c in range(8):
                S.op("pe", lambda e, c=c: e.transpose(PS[4][:, c * 16:(c + 1) * 16], osb[:, c * 128:(c + 1) * 128], identf[:16, :16]),
                     reads=[R_osb, R_const], writes=[RPS[4]], signal=(c == 7))
            S.op("act", lambda e: e.activation(out=OT[:, :, NO:NOS], in_=PS[4][:, 0:128].rearrange("p (c t) -> p c t", t=16), func=AF.Copy),
                 reads=[RPS[4]], writes=[R_OT])

            S.barrier()
            pb_.close()
            TT = 128
            XH = [(sb(ps, "o_x%d" % i, [128, 8, TT], F32), Res()) for i in range(3)]
            sq = sb(ps, "o_sq", [128, 8, TT], BF16); R_sq = Res()
            rs = sb(ps, "o_rs", [128, TT], F32); R_rs = Res()
            TMP = [(sb(ps, "o_tmp%d" % i, [128, TT], F32), Res()) for i in range(2)]
            otiles = [(t0, min(TT, NOS - t0)) for t0 in range(0, NOS, TT)]
            no_ = len(otiles)

            def o_bank(i, c):
                bk = 2 * (i % 3) + c // 4
                return PS[bk], RPS[bk], (c % 4) * 128

            def o_load(i):
                t0, T = otiles[i]
                xt, xr = XH[i % 3]
                S.dma_start("sp", xt[:, :, :T], h1T[:, NH + t0:NH + t0 + T].rearrange("(c p) t -> p c t", p=128), writes=[xr])

            def o_mm(i):
                t0, T = otiles[i]
                for c in range(8):
                    bank, rb, off = o_bank(i, c)
                    for k in range(8):
                        mm(bank[:, off:off + T], w_out[:, k, c * 128:(c + 1) * 128], OT[:, k, t0:t0 + T], k == 0, k == 7,
                           [R_wo, R_OT], [rb], k == 7)

            def o_post(i):
                t0, T = otiles[i]
                xt, xr = XH[i % 3]
                for b in range(2):
                    bk = 2 * (i % 3) + b
                    S.op("act", lambda e, b=b, bk=bk: e.activation(out=sq[:, 4 * b:4 * b + 4, :T], in_=PS[bk][:].rearrange("p (a t) -> p a t", a=4)[:, :, :T],
                                                                    func=AF.Square), reads=[RPS[bk]], writes=[R_sq])
                rstd_from_sq(sq, R_sq, T, PS[6], RPS[6], rs, R_rs)
                for c in range(8):
                    bank, rb, off = o_bank(i, c)
                    tm, rt = TMP[c % 2]
                    S.op("dve", lambda e, c=c, bank=bank, off=off, tm=tm: e.scalar_tensor_tensor(
                        out=tm[:, :T], in0=bank[:, off:off + T], scalar=gcol(G_MIXPOST, c), in1=rs[:, :T],
                        op0=ALU.mult, op1=ALU.mult), reads=[rb, R_rs, R_const], writes=[rt])
                    S.op("pool", lambda e, c=c, tm=tm, xt=xt: e.tensor_tensor(out=xt[:, c, :T], in0=xt[:, c, :T], in1=tm[:, :T], op=ALU.add),
                         reads=[rt, xr], writes=[xr])
                S.dma_start("pool", h2T[:, t0:t0 + T].rearrange("(c p) t -> p c t", p=128), xt[:, :, :T], reads=[xr])

            o_load(0)
            if no_ > 1:
                o_load(1)
            o_mm(0)
            for i in range(no_):
                if i + 2 < no_:
                    o_load(i + 2)
                if i + 1 < no_:
                    o_mm(i + 1)
                o_post(i)
            S.barrier()

        ffn_phase("f2", w_f2g, w_f2u, w_f2d, G_F2PRE, G_F2POST, h2T, 0, h3T, 0, NOS, wg_pre=(wg_f2, R_wg_f2))

        with ExitStack() as ps:
            wpg = sb(ps, "wpg", [128, 8, D], BF16); wpp = sb(ps, "wpp", [128, 2, D], BF16); R_w = Res()
            for k in range(8):
                S.dma_start("pool", wpg[:, k, :], w_pg_d[k * 128:(k + 1) * 128, :], writes=[R_w])
            for k in range(2):
                S.dma_start("pool", wpp[:, k, :], w_pp_d[k * 128:(k + 1) * 128, :], writes=[R_w])
            TT = 256
            X = [(sb(ps, "e_x%d" % i, [128, 8, TT], F32), Res()) for i in range(2)]
            Pt = [(sb(ps, "e_p%d" % i, [128, 2, TT], BF16), Res()) for i in range(2)]
            u = sb(ps, "e_u", [128, 8, TT], BF16); R_u = Res()
            yb = sb(ps, "e_y", [128, 8, TT], F32); R_y = Res()
            sg = [(sb(ps, "e_sg%d" % i, [128, TT], F32), Res()) for i in range(2)]
            sq = sb(ps, "e_sq", [128, 8, TT], BF16); R_sq = Res()
            rs = sb(ps, "e_rs", [128, TT], F32); R_rs = Res()
            TMP = [(sb(ps, "e_tmp%d" % i, [128, TT], F32), Res()) for i in range(2)]
            etiles = [(t0, min(TT, NOS - t0)) for t0 in range(0, NOS, TT)]
            X3 = X + [(sb(ps, "e_x2", [128, 8, TT], F32), Res())]
            U2 = [(u, R_u), (sb(ps, "e_u1", [128, 8, TT], BF16), Res())]
            rs2 = sb(ps, "e_rs2", [128, TT], F32); R_rs2 = Res()

            def e_load(i):
                t0, T = etiles[i]
                xt, xr = X3[i % 3]
                pt, rp = Pt[i % 2]
                S.dma_start("sp", xt[:, :, :T], h3T[:, t0:t0 + T].rearrange("(c p) t -> p c t", p=128), writes=[xr])
                S.dma_start("pool", pt[:, :, :T], pT[:, t0:t0 + T].rearrange("(c p) t -> p c t", p=128), writes=[rp])

            def e_pre(i):
                t0, T = etiles[i]
                xt, xr = X3[i % 3]
                uu, ur = U2[i % 2]
                S.op("act", lambda e: e.activation(out=sq[:, :, :T], in_=xt[:, :, :T], func=AF.Square), reads=[xr], writes=[R_sq])
                rstd_from_sq(sq, R_sq, T, PS[6], RPS[6], rs, R_rs)
                for c in range(8):
                    S.op("dve", lambda e, c=c: e.scalar_tensor_tensor(out=uu[:, c, :T], in0=xt[:, c, :T], scalar=gcol(G_PLEPRE, c), in1=rs[:, :T],
                                                                       op0=ALU.mult, op1=ALU.mult), reads=[xr, R_rs, R_const], writes=[ur])

            def e_gate(i):
                t0, T = etiles[i]
                uu, ur = U2[i % 2]
                pt, rp = Pt[i % 2]
                for c in range(8):
                    pb, rpb = PS[c % 4], RPS[c % 4]
                    s_, rs_ = sg[c % 2]
                    for k in range(8):
                        mm(pb[:, 0:T], wpg[:, k, c * 128:(c + 1) * 128], uu[:, k, :T], k == 0, k == 7, [R_w, ur], [rpb], False)
                    for k in range(2):
                        mm(pb[:, 256:256 + T], wpp[:, k, c * 128:(c + 1) * 128], pt[:, k, :T], k == 0, k == 1, [R_w, rp], [rpb], k == 1)
                    S.op("act", lambda e, pb=pb, s_=s_: e.activation(out=s_[:, :T], in_=pb[:, 0:T], func=AF.Sigmoid), reads=[rpb], writes=[rs_])
                    S.op("dve", lambda e, c=c, pb=pb, s_=s_: e.tensor_tensor(out=yb[:, c, :T], in0=s_[:, :T], in1=pb[:, 256:256 + T], op=ALU.mult),
                         reads=[rs_, rpb], writes=[R_y])

            def e_post(i):
                t0, T = etiles[i]
                xt, xr = X3[i % 3]
                S.op("act", lambda e: e.activation(out=sq[:, :, :T], in_=yb[:, :, :T], func=AF.Square), reads=[R_y], writes=[R_sq])
                rstd_from_sq(sq, R_sq, T, PS[6], RPS[6], rs2, R_rs2)
                for c in range(8):
                    tm, rt = TMP[c % 2]
                    S.op("dve", lambda e, c=c, tm=tm: e.scalar_tensor_tensor(out=tm[:, :T], in0=yb[:, c, :T], scalar=gcol(G_PLEPOST, c), in1=rs2[:, :T],
                                                                            op0=ALU.mult, op1=ALU.mult), reads=[R_y, R_rs2, R_const], writes=[rt])
                    S.op("pool", lambda e, c=c, tm=tm, xt=xt: e.tensor_tensor(out=xt[:, c, :T], in0=xt[:, c, :T], in1=tm[:, :T], op=ALU.add),
                         reads=[rt, xr], writes=[xr])
                S.dma_start("pool", yT[:, t0:t0 + T].rearrange("(c p) t -> p c t", p=128), xt[:, :, :T], reads=[xr])

            ne = len(etiles)
            e_load(0)
            if ne > 1:
                e_load(1)
            e_pre(0)
            for i in range(ne):
                e_gate(i)
                if i + 2 < ne:
                    e_load(i + 2)
                if i + 1 < ne:
                    e_pre(i + 1)
                e_post(i)
            S.barrier(final=True)
    return nc


_NC_CACHE = {}


def _host_inputs(inp):
    f32 = np.float32
    xp = np.asarray(inp["x_prompt"], f32)
    xs = np.asarray(inp["x_sample"], f32)[:, 0, :]
    pp = np.asarray(inp["p_prompt"], f32)[0]
    psm = np.asarray(inp["p_sample"], f32)[0][:, 0, :]
    names = ["norm_f1_pre", "norm_f1_post", "norm_mix_pre", "norm_mix_post", "norm_f2_pre", "norm_f2_post",
             "norm_ple_pre", "norm_ple_post"]
    gains = np.zeros((128, 64), f32)
    for n, nm in enumerate(names):
        g = np.asarray(inp[nm], f32)[0]
        gains[:, n * 8:(n + 1) * 8] = g.reshape(8, 128).T
    ident = np.eye(128, dtype=f32)
    kk = np.arange(128)[:, None]
    qq = np.arange(128)[None, :]
    m_cur = (kk <= qq).astype(f32)
    m_next = (kk >= qq).astype(f32)
    m_own = np.concatenate([m_cur, m_next], axis=1)
    sel = np.zeros((16, 16, 128), f32)
    selT = np.zeros((128, 16, 16), f32)
    for b in range(16):
        sel[b, b, :] = 1.0
        selT[:, b, b] = 1.0
    sinks = np.broadcast_to(np.asarray(inp["sinks_b"], f32)[0][None, :], (128, 8)).copy()
    shared = {
        "gains": gains, "ident": ident, "sel": sel.reshape(16, -1), "selT": selT.reshape(128, -1), "sinks": sinks,
    }
    for nm in ["w_f1_gate", "w_f1_up", "w_f1_down", "w_f2_gate", "w_f2_up", "w_f2_down", "w_in", "w_out",
               "w_ple_gate", "w_ple_proj"]:
        shared[nm] = np.ascontiguousarray(np.asarray(inp[nm], f32)[0])
    cak = np.asarray(inp["cache_a_k"], f32)[0].reshape(128, 2048, 512)
    cav = np.asarray(inp["cache_a_v"], f32)[0].reshape(128, 2048, 512)
    cbk = np.asarray(inp["cache_b_k"], f32)[0].reshape(128, 128, 128)
    cbv = np.asarray(inp["cache_b_v"], f32)[0].reshape(128, 128, 128)
    maps = []
    for c in range(NCORES):
        bb, j = c // 4, c % 4
        s = j * NO
        xT = np.zeros((D, NT), f32)
        if j > 0:
            xT[:, 0:NH] = xp[bb, s - NH:s, :].T
        xT[:, NH:NH + NO] = xp[bb, s:s + NO, :].T
        xT[:, NH + NO:] = xs[c * NS:(c + 1) * NS, :].T
        pT = np.zeros((256, NOS), f32)
        pT[:, :NO] = pp[bb, s:s + NO, :].T
        pT[:, NO:] = psm[c * NS:(c + 1) * NS, :].T
        pos = np.concatenate([np.arange(s - NH, s + NO), np.full(128, PAST)]).astype(np.int64)
        cs, sn = _rope_tables(pos)
        cos_t = cs.reshape(33, 128, 32).transpose(1, 0, 2).reshape(128, -1)
        sin_t = sn.reshape(33, 128, 32).transpose(1, 0, 2).reshape(128, -1)
        hp = 1.0 if j > 0 else 0.0
        masks = np.concatenate([m_own, m_own * hp, m_next * hp, m_cur], axis=1).astype(f32)
        m = dict(shared)
        m.update({
            "xT": xT, "pT": pT, "cos_t": np.ascontiguousarray(cos_t), "sin_t": np.ascontiguousarray(sin_t), "masks": masks,
            "cache_a_k": np.ascontiguousarray(cak[c * NS:(c + 1) * NS]), "cache_a_v": np.ascontiguousarray(cav[c * NS:(c + 1) * NS]),
            "cache_b_k": np.ascontiguousarray(cbk[c * NS:(c + 1) * NS]), "cache_b_v": np.ascontiguousarray(cbv[c * NS:(c + 1) * NS]),
        })
        maps.append(m)
    return maps


def kernel(**inputs):
    if "nc" not in _NC_CACHE:
        _NC_CACHE["nc"] = build_program()
    nc = _NC_CACHE["nc"]
    maps = _host_inputs(inputs)
    res = run_bass_kernel_spmd(nc, maps, core_ids=list(range(NCORES)))
    R = res.results
    f32 = np.float32
    y_prompt = np.zeros((2, 8192, D), f32)
    y_sample = np.zeros((128, 1, D), f32)
    nak_p = np.zeros((1, 2, 2048, 8, 64), f32); nav_p = np.zeros((1, 2, 2048, 8, 64), f32)
    nbk_p = np.zeros((1, 2, 128, 2, 64), f32); nbv_p = np.zeros((1, 2, 128, 2, 64), f32)
    nak_s = np.zeros((1, 128, 2048, 8, 64), f32); nav_s = np.zeros((1, 128, 2048, 8, 64), f32)
    nbk_s = np.zeros((1, 128, 128, 2, 64), f32); nbv_s = np.zeros((1, 128, 128, 2, 64), f32)
    for c in range(NCORES):
        bb, j = c // 4, c % 4
        r = R[c]
        yT = np.asarray(r["yT"], f32)
        y_prompt[bb, j * NO:(j + 1) * NO, :] = yT[:, :NO].T
        y_sample[c * NS:(c + 1) * NS, 0, :] = yT[:, NO:].T
        if j == 3:
            nak_p[0, bb] = np.asarray(r["ka_o"], f32).reshape(2048, 8, 64)
            nav_p[0, bb] = np.asarray(r["va_o"], f32).reshape(2048, 8, 64)
            nbk_p[0, bb] = np.asarray(r["kb_o"], f32)[NO - 128:].reshape(128, 2, 64)
            nbv_p[0, bb] = np.asarray(r["vb_o"], f32)[NO - 128:].reshape(128, 2, 64)
        nak_s[0, c * NS:(c + 1) * NS] = np.asarray(r["nak_s"], f32).reshape(NS, 2048, 8, 64)
        nav_s[0, c * NS:(c + 1) * NS] = np.asarray(r["nav_s"], f32).reshape(NS, 2048, 8, 64)
        nbk_s[0, c * NS:(c + 1) * NS] = np.asarray(r["nbk_s"], f32).reshape(NS, 128, 2, 64)
        nbv_s[0, c * NS:(c + 1) * NS] = np.asarray(r["nbv_s"], f32).reshape(NS, 128, 2, 64)
    return (y_prompt, y_sample, nak_p, nav_p, nbk_p, nbv_p, nak_s, nav_s, nbk_s, nbv_s)
```
